# Optimizing a Trainium2 kernel written in Bass

```python
import jax, jax.numpy as jnp
from jax import lax
import numpy as np

D_MODEL = 1024
BATCH = 16
SEQ = 256
DEPTH = 2
DEC_BATCH = 2
DEC_SEQ = 4096
PAST_LEN = 512

GRID_W = 64
N_MIXERS = 2
N_ATTN_LAYERS = (DEPTH + 1) // 2
N_DELTA_LAYERS = DEPTH // 2
HEAD_DIM = 128
N_HEADS = 8
N_KV_HEADS = 2
KV_GROUPS = N_HEADS // N_KV_HEADS
ROPE_HALF = HEAD_DIM // 2
ROPE_THETA = 10000.0
Q_BLOCK = 128
DN_HEADS = 8
DN_DK = 128
DN_DV = 128
DN_KEY_W = DN_HEADS * DN_DK
DN_VAL_W = DN_HEADS * DN_DV
DN_QKV_W = 2 * DN_KEY_W + DN_VAL_W
DN_CONV = 3
DN_CHUNK = 64
D_FF = 2816
N_MOD = 9
EPS = 1e-6

kernel_name = "hybrid_diffusion_prefix_step"

F32 = jnp.float32


def rmsnorm(x, g):
    xf = x.astype(F32)
    y = xf * lax.rsqrt(jnp.mean(xf * xf, axis=-1, keepdims=True) + EPS)
    return (y * g.astype(F32)).astype(x.dtype)


def l2norm(x):
    xf = x.astype(F32)
    return xf * lax.rsqrt(jnp.sum(xf * xf, axis=-1, keepdims=True) + EPS)


def modulation(cond, w, b):
    m = jax.nn.silu(cond) @ w + b
    return jnp.split(m, N_MOD, axis=-1)


def modulate(h, gain, shift, scale):
    return rmsnorm(h, gain) * (1 + scale) + shift


def swiglu(x, w_in, w_out):
    a, b = jnp.split(x @ w_in, 2, axis=-1)
    return (jax.nn.silu(a) * b) @ w_out


def axial_rope_tables(n_tokens):
    rows = n_tokens // GRID_W
    row = jnp.repeat(jnp.arange(rows), GRID_W).astype(F32)
    col = jnp.tile(jnp.arange(GRID_W), rows).astype(F32)
    n_freq = ROPE_HALF // 2
    inv = ROPE_THETA ** (-jnp.arange(n_freq, dtype=F32) / n_freq)
    ang = jnp.concatenate([row[:, None] * inv, col[:, None] * inv], axis=-1)
    return jnp.cos(ang), jnp.sin(ang)


def apply_axial_rope(x, cos, sin):
    B, T, H, D = x.shape
    n_freq = ROPE_HALF // 2
    xr = x.astype(F32).reshape(B, T, H, 2, 2, n_freq)
    x1, x2 = xr[..., 0, :], xr[..., 1, :]
    c = cos.reshape(1, T, 1, 2, n_freq)
    s = sin.reshape(1, T, 1, 2, n_freq)
    out = jnp.stack([x1 * c - x2 * s, x2 * c + x1 * s], axis=-2)
    return out.reshape(B, T, H, D).astype(x.dtype)


def attn_project(x, w_qkv, q_gain, k_gain):
    B, T, _ = x.shape
    qkv = x @ w_qkv
    nq, nk = N_HEADS * HEAD_DIM, N_KV_HEADS * HEAD_DIM
    q = qkv[..., :nq].reshape(B, T, N_HEADS, HEAD_DIM)
    k = qkv[..., nq:nq + nk].reshape(B, T, N_KV_HEADS, HEAD_DIM)
    v = qkv[..., nq + nk:].reshape(B, T, N_KV_HEADS, HEAD_DIM)
    return rmsnorm(q, q_gain), rmsnorm(k, k_gain), v


def block_attention(q, k, v):
    B, T = q.shape[:2]
    nb = T // Q_BLOCK
    qb = q.reshape(B, nb, Q_BLOCK, N_KV_HEADS, KV_GROUPS, HEAD_DIM).swapaxes(0, 1)
    scale = HEAD_DIM ** -0.5

    def one_block(qi):
        s = jnp.einsum('bqkgd,bskd->bkgqs', qi, k).astype(F32) * scale
        p = jax.nn.softmax(s, axis=-1).astype(v.dtype)
        return jnp.einsum('bkgqs,bskd->bqkgd', p, v)

    o = lax.map(one_block, qb)
    return o.swapaxes(0, 1).reshape(B, T, N_HEADS * HEAD_DIM)


def attn_context(x, w_qkv, q_gain, k_gain, w_o):
    q, k, v = attn_project(x, w_qkv, q_gain, k_gain)
    return block_attention(q, k, v) @ w_o, k, v


def attn_latent(x, ck, cv, w_qkv, q_gain, k_gain, w_o):
    q, k, v = attn_project(x, w_qkv, q_gain, k_gain)
    cos, sin = axial_rope_tables(x.shape[1])
    q = apply_axial_rope(q, cos, sin)
    k = apply_axial_rope(k, cos, sin)
    keys = jnp.concatenate([ck.astype(k.dtype), k], axis=1)
    vals = jnp.concatenate([cv.astype(v.dtype), v], axis=1)
    return block_attention(q, keys, vals) @ w_o


def centred_conv(x, w):
    T = x.shape[1]
    pad = DN_CONV // 2
    xp = jnp.pad(x, ((0, 0), (pad, pad), (0, 0)))
    out = xp[:, 0:T] * w[0]
    for j in range(1, DN_CONV):
        out = out + xp[:, j:j + T] * w[j]
    return out


def gated_delta_chunked(q, k, v, g, beta, s0):
    B, T, H, DK = q.shape
    DV = v.shape[-1]
    C = DN_CHUNK
    n = T // C

    def to_chunks(t):
        t = t.reshape((B, n, C, H) + t.shape[3:])
        return jnp.moveaxis(t, 3, 1)

    q = to_chunks(q) * (DK ** -0.5)
    k = to_chunks(k)
    v = to_chunks(v)
    gc = jnp.cumsum(to_chunks(g), axis=-1)
    beta = to_chunks(beta)
    idx = jnp.arange(C)
    causal = idx[:, None] >= idx[None, :]
    strict = idx[:, None] > idx[None, :]
    diff = gc[..., :, None] - gc[..., None, :]
    decay = jnp.where(causal, jnp.exp(jnp.where(causal, diff, 0.0)), 0.0)
    kb = k * beta[..., None]
    a = jnp.where(strict, jnp.einsum('bhncd,bhnsd->bhncs', kb, k) * decay, 0.0)
    eye = jnp.eye(C, dtype=F32)
    t_inv = lax.linalg.triangular_solve(a + eye, jnp.broadcast_to(eye, a.shape),
                                        left_side=True, lower=True, unit_diagonal=True)
    u = jnp.einsum('bhncs,bhnsv->bhncv', t_inv, v * beta[..., None])
    w = jnp.einsum('bhncs,bhnsd->bhncd', t_inv, kb * jnp.exp(gc)[..., None])
    qk = jnp.einsum('bhncd,bhnsd->bhncs', q, k) * decay
    q_dec = q * jnp.exp(gc)[..., None]
    k_dec = k * jnp.exp(gc[..., -1:] - gc)[..., None]
    g_last = jnp.exp(gc[..., -1])

    def step(S, xs):
        qk_i, qd_i, w_i, u_i, kd_i, gl_i = xs
        v_new = u_i - jnp.einsum('bhcd,bhdv->bhcv', w_i, S)
        o_i = jnp.einsum('bhcd,bhdv->bhcv', qd_i, S) + jnp.einsum('bhcs,bhsv->bhcv', qk_i, v_new)
        S = S * gl_i[..., None, None] + jnp.einsum('bhcd,bhcv->bhdv', kd_i, v_new)
        return S, o_i

    xs = tuple(jnp.moveaxis(t, 2, 0) for t in (qk, q_dec, w, u, k_dec, g_last))
    s_final, o = lax.scan(step, s0.astype(F32), xs)
    o = jnp.transpose(o, (1, 0, 3, 2, 4)).reshape(B, T, H, DV)
    return o, s_final


def delta_core(x, s_f0, s_b0, w_in, conv_w, w_a, dt_bias, a_log, w_b, out_gain, w_o):
    B, T, _ = x.shape
    proj = x @ w_in
    qkv = jax.nn.silu(centred_conv(proj[..., :DN_QKV_W], conv_w))
    z = proj[..., DN_QKV_W:]
    q = l2norm(qkv[..., :DN_KEY_W].reshape(B, T, DN_HEADS, DN_DK))
    k = l2norm(qkv[..., DN_KEY_W:2 * DN_KEY_W].reshape(B, T, DN_HEADS, DN_DK))
    v = qkv[..., 2 * DN_KEY_W:].reshape(B, T, DN_HEADS, DN_DV).astype(F32)
    a = jnp.einsum('btd,edh->ebth', x, w_a).astype(F32) + dt_bias.astype(F32)[:, None, None, :]
    g = -jnp.exp(a_log.astype(F32))[:, None, None, :] * jax.nn.softplus(a)
    beta = jax.nn.sigmoid(jnp.einsum('btd,edh->ebth', x, w_b).astype(F32))
    o_f, s_f = gated_delta_chunked(q, k, v, g[0], beta[0], s_f0)
    flip = lambda t: jnp.flip(t, axis=1)
    o_b, s_b = gated_delta_chunked(flip(q), flip(k), flip(v), flip(g[1]), flip(beta[1]), s_b0)
    o = rmsnorm(o_f + flip(o_b), out_gain) * jax.nn.silu(z.astype(F32).reshape(B, T, DN_HEADS, DN_DV))
    y = o.reshape(B, T, DN_VAL_W).astype(x.dtype) @ w_o
    return y, s_f, s_b


def setup_inputs(seed: int = 0) -> dict:
    key = jax.random.key(seed)
    ks = iter(jax.random.split(key, 40))

    def nrm(shape, scale):
        return jax.random.normal(next(ks), shape, F32) * scale

    def gain(shape):
        return 1.0 + nrm(shape, 0.05)

    d = D_MODEL
    dt = jnp.exp(jax.random.uniform(next(ks), (N_DELTA_LAYERS, 2, DN_HEADS), F32,
                                    minval=float(np.log(1e-3)), maxval=float(np.log(1e-1))))
    dt_bias = dt + jnp.log(-jnp.expm1(-dt))
    a_log = jnp.log(jax.random.uniform(next(ks), (N_DELTA_LAYERS, 2, DN_HEADS), F32, minval=1.0, maxval=16.0))
    return {
        "x_prompt": nrm((BATCH, SEQ, d), 1.0),
        "x_sample": nrm((DEC_BATCH, DEC_SEQ, d), 1.0),
        "cache_k": nrm((DEC_BATCH, N_ATTN_LAYERS, PAST_LEN, N_KV_HEADS, HEAD_DIM), 1.0),
        "cache_v": nrm((DEC_BATCH, N_ATTN_LAYERS, PAST_LEN, N_KV_HEADS, HEAD_DIM), 1.0),
        "state_fwd": nrm((DEC_BATCH, N_DELTA_LAYERS, DN_HEADS, DN_DK, DN_DV), 0.1),
        "state_bwd": nrm((DEC_BATCH, N_DELTA_LAYERS, DN_HEADS, DN_DK, DN_DV), 0.1),
        "c": nrm((DEC_BATCH, d), 1.0),
        "c_ctx": nrm((d,), 1.0),
        "ada_w": nrm((DEPTH, d, N_MOD * d), 0.5 * d ** -0.5),
        "ada_b": nrm((DEPTH, N_MOD * d), 0.02),
        "norm_ffn1": gain((DEPTH, d)),
        "ffn1_w_in": nrm((DEPTH, d, 2 * D_FF), d ** -0.5),
        "ffn1_w_out": nrm((DEPTH, D_FF, d), D_FF ** -0.5),
        "norm_mix": gain((DEPTH, d)),
        "attn_w_qkv": nrm((N_ATTN_LAYERS, d, (N_HEADS + 2 * N_KV_HEADS) * HEAD_DIM), d ** -0.5),
        "attn_q_norm": gain((N_ATTN_LAYERS, HEAD_DIM)),
        "attn_k_norm": gain((N_ATTN_LAYERS, HEAD_DIM)),
        "attn_w_o": nrm((N_ATTN_LAYERS, N_HEADS * HEAD_DIM, d), (N_HEADS * HEAD_DIM) ** -0.5),
        "dn_w_in": nrm((N_DELTA_LAYERS, d, DN_QKV_W + DN_VAL_W), d ** -0.5),
        "dn_conv": nrm((N_DELTA_LAYERS, DN_CONV, DN_QKV_W), DN_CONV ** -0.5),
        "dn_w_a": nrm((N_DELTA_LAYERS, 2, d, DN_HEADS), 0.1 * d ** -0.5),
        "dn_dt_bias": dt_bias,
        "dn_a_log": a_log,
        "dn_w_b": nrm((N_DELTA_LAYERS, 2, d, DN_HEADS), d ** -0.5),
        "dn_out_norm": gain((N_DELTA_LAYERS, DN_DV)),
        "dn_w_o": nrm((N_DELTA_LAYERS, DN_VAL_W, d), DN_VAL_W ** -0.5),
        "norm_ffn2": gain((DEPTH, d)),
        "ffn2_w_in": nrm((DEPTH, d, 2 * D_FF), d ** -0.5),
        "ffn2_w_out": nrm((DEPTH, D_FF, d), D_FF ** -0.5),
        "final_norm": gain((d,)),
    }


def reference(x_prompt, x_sample, cache_k, cache_v, state_fwd, state_bwd, c, c_ctx,
              ada_w, ada_b, norm_ffn1, ffn1_w_in, ffn1_w_out, norm_mix,
              attn_w_qkv, attn_q_norm, attn_k_norm, attn_w_o,
              dn_w_in, dn_conv, dn_w_a, dn_dt_bias, dn_a_log, dn_w_b, dn_out_norm, dn_w_o,
              norm_ffn2, ffn2_w_in, ffn2_w_out, final_norm):
    ctx_cond = c_ctx[None, None, :]
    lat_cond = c[:, None, :]
    hp, hs = x_prompt, x_sample
    new_k, new_v, new_sf, new_sb = [], [], [], []
    for i in range(DEPTH):
        mp = modulation(ctx_cond, ada_w[i], ada_b[i])
        ms = modulation(lat_cond, ada_w[i], ada_b[i])
        hp = hp + 0.5 * mp[2] * swiglu(modulate(hp, norm_ffn1[i], mp[0], mp[1]), ffn1_w_in[i], ffn1_w_out[i])
        hs = hs + 0.5 * ms[2] * swiglu(modulate(hs, norm_ffn1[i], ms[0], ms[1]), ffn1_w_in[i], ffn1_w_out[i])
        up = modulate(hp, norm_mix[i], mp[3], mp[4])
        us = modulate(hs, norm_mix[i], ms[3], ms[4])
        j = i // N_MIXERS
        if i % N_MIXERS == 0:
            yp, kc, vc = attn_context(up, attn_w_qkv[j], attn_q_norm[j], attn_k_norm[j], attn_w_o[j])
            ys = attn_latent(us, cache_k[:, j], cache_v[:, j], attn_w_qkv[j], attn_q_norm[j],
                             attn_k_norm[j], attn_w_o[j])
            new_k.append(kc)
            new_v.append(vc)
        else:
            zero_state = jnp.zeros((hp.shape[0], DN_HEADS, DN_DK, DN_DV), F32)
            yp, sf, sb = delta_core(up, zero_state, zero_state, dn_w_in[j], dn_conv[j], dn_w_a[j],
                                    dn_dt_bias[j], dn_a_log[j], dn_w_b[j], dn_out_norm[j], dn_w_o[j])
            ys, _, _ = delta_core(us, state_fwd[:, j], state_bwd[:, j], dn_w_in[j], dn_conv[j], dn_w_a[j],
                                  dn_dt_bias[j], dn_a_log[j], dn_w_b[j], dn_out_norm[j], dn_w_o[j])
            new_sf.append(sf.astype(x_prompt.dtype))
            new_sb.append(sb.astype(x_prompt.dtype))
        hp = hp + mp[5] * yp
        hs = hs + ms[5] * ys
        hp = hp + 0.5 * mp[8] * swiglu(modulate(hp, norm_ffn2[i], mp[6], mp[7]), ffn2_w_in[i], ffn2_w_out[i])
        hs = hs + 0.5 * ms[8] * swiglu(modulate(hs, norm_ffn2[i], ms[6], ms[7]), ffn2_w_in[i], ffn2_w_out[i])
    y_prompt = rmsnorm(hp, final_norm)
    y_sample = rmsnorm(hs, final_norm)
    return (y_prompt, y_sample, jnp.stack(new_k, axis=1), jnp.stack(new_v, axis=1),
            jnp.stack(new_sf, axis=1), jnp.stack(new_sb, axis=1))
```

```python
import numpy as np
from contextlib import ExitStack
import concourse.bass as bass
import concourse.mybir as mybir
from concourse.bass_utils import run_bass_kernel_spmd

F32 = mybir.dt.float32
BF16 = mybir.dt.bfloat16
AF = mybir.ActivationFunctionType
ALU = mybir.AluOpType

D = 1024
KC = 8
NT = 1536
TILES = [(0, 512), (512, 1024), (1024, 1536)]
DFF = 2816
FC = 22
EPS = 1e-6
GROUPS = [[0, 1, 2, 3], [4, 5, 6, 7]]
SLOT = 4096
NSLOT = 3


class Reg:
    __slots__ = ("w", "r", "excl")

    def __init__(self, excl=False):
        self.w = None
        self.r = {}
        self.excl = excl


def regs(*shape):
    if len(shape) == 1:
        return [Reg() for _ in range(shape[0])]
    return [regs(*shape[1:]) for _ in range(shape[0])]


class Eng:
    def __init__(self, h, sem):
        self.h = h
        self.sem = sem
        self.cnt = 0
        self.waited = {}

    def wait(self, toks):
        best = {}
        for t in toks:
            if t is None:
                continue
            if id(t[0]) not in best or best[id(t[0])][1] < t[1]:
                best[id(t[0])] = t
        for t in best.values():
            sem, val = t
            k = id(sem)
            if sem is self.sem and val > self.cnt:
                continue
            if self.waited.get(k, 0) < val:
                self.h.wait_ge(sem, val)
                self.waited[k] = val

    def _deps(self, reads, writes, deps):
        toks = list(deps)
        for R in reads:
            toks.append(R.w)
            if R.excl:
                toks.extend(R.r.values())
        for R in writes:
            toks.append(R.w)
            toks.extend(R.r.values())
        self.wait(toks)

    @staticmethod
    def _upd(tok, reads, writes):
        for R in reads:
            if R.excl:
                R.w = tok
                R.r = {}
                continue
            k = id(tok[0])
            if k not in R.r or R.r[k][1] < tok[1]:
                R.r[k] = tok
        for R in writes:
            R.w = tok
            R.r = {}

    def op(self, name, *a, reads=(), writes=(), deps=(), inc=True, **kw):
        self._deps(reads, writes, deps)
        ins = getattr(self.h, name)(*a, **kw)
        if inc:
            self.cnt += 1
            ins.then_inc(self.sem, 1)
            tok = (self.sem, self.cnt)
        else:
            tok = (self.sem, self.cnt + 1)
        self._upd(tok, reads, writes)
        return tok


class DSem:
    def __init__(self, s):
        self.s = s
        self.n = 0


BAR = {"toks": []}


def dma(q, out, in_, ds, reads=(), writes=(), deps=(), phase=True, **kw):
    if phase:
        q.wait(BAR["toks"])
    q._deps(reads, writes, deps)
    q.h.dma_start(out=out, in_=in_, **kw).then_inc(ds.s, 16)
    ds.n += 16
    tok = (ds.s, ds.n)
    Eng._upd(tok, reads, writes)
    return tok


class K:
    pass


def build_program(stage=99, dbg=False, sub=9, dbgsrc='h', dsub=9):
    nc = bass.Bass("TRN2", target_bir_lowering=False)
    BAR["toks"] = []
    k = K()
    k.nc = nc
    k.stage = stage
    k.uid = 0
    k.sub = sub
    k.dsub = dsub
    k.dbgsrc = dbgsrc

    def din(name, shape, dt=F32):
        return nc.dram_tensor(name, list(shape), dt, kind="ExternalInput").ap()

    def dout(name, shape, dt=F32):
        return nc.dram_tensor(name, list(shape), dt, kind="ExternalOutput").ap()

    I = {}
    I["xT"] = din("xT", [D, NT])
    I["condT"] = din("condT", [128, KC, 2])
    I["ada_w"] = din("ada_w", [2, D, 2304])
    I["adabT"] = din("adabT", [2, 128, 18])
    I["gainsT"] = din("gainsT", [128, 7, KC])
    I["ffn_w_in"] = din("ffn_w_in", [4, D, 2 * DFF])
    I["ffn_w_out"] = din("ffn_w_out", [4, DFF, D])
    I["ident"] = din("ident", [128, 128])
    I["attn_w_qkv"] = din("attn_w_qkv", [D, 1536])
    I["attn_w_o"] = din("attn_w_o", [D, D])
    I["qkg"] = din("qkg", [128, 2])
    I["ckT"] = din("ckT", [128, 2, 512])
    I["cv"] = din("cv", [128, 4, 256])
    I["ropeC"] = din("ropeC", [128, 1024])
    I["ropeS"] = din("ropeS", [128, 1024])
    I["rmat"] = din("rmat", [128, 128])
    I["dn_w_in"] = din("dn_w_in", [D, 4096])
    I["dn_w_in_own"] = din("dn_w_in_own", [D, 1024])
    I["convT"] = din("convT", [128, 24, 3])
    I["convT_own"] = din("convT_own", [128, 6, 3])
    I["wab_p"] = din("wab_p", [D, 32])
    I["wab_o"] = din("wab_o", [D, 8])
    I["dtb_p"] = din("dtb_p", [16])
    I["alog_p"] = din("alog_p", [16])
    I["dtb_o"] = din("dtb_o", [4])
    I["alog_o"] = din("alog_o", [4])
    I["ong"] = din("ong", [128, 1])
    I["dn_w_o"] = din("dn_w_o", [D, D])
    I["s0f"] = din("s0f", [128, 2, 128])
    I["s0b"] = din("s0b", [128, 2, 128])
    I["sel"] = din("sel", [4])
    I["masks"] = din("masks", [128, 4, 128])
    I["lmask"] = din("lmask", [128, 7, 128])
    O = {}
    O["yT"] = dout("yT", [D, NT])
    O["kout"] = dout("kout", [128, 2, 512])
    O["vout"] = dout("vout", [128, 4, 256])
    O["sfo"] = dout("sfo", [128, 16, 128])
    O["sbo"] = dout("sbo", [128, 16, 128])
    agin2 = [nc.dram_tensor(f"agin2_{x}", [512, 1024], BF16) for x in range(2)]
    agout2 = [nc.dram_tensor(f"agout2_{x}", [2048, 1024], BF16) for x in range(2)]
    agin3 = [nc.dram_tensor(f"agin3_{x}", [128, 4096], BF16) for x in range(2)]
    agout3 = [nc.dram_tensor(f"agout3_{x}", [512, 4096], BF16) for x in range(2)]
    agin1 = nc.dram_tensor("agin1", [256, 2048], BF16)
    agin0 = nc.dram_tensor("agin0", [128, 72], F32)
    agout0 = nc.dram_tensor("agout0", [512, 72], F32)
    agout1 = nc.dram_tensor("agout1", [1024, 2048], BF16)
    if dbg:
        O["dbg"] = dout("dbg", [D, NT])
    k.I, k.O = I, O

    es = ExitStack()
    with es:
        def sb(name, shape, dt):
            return es.enter_context(nc.sbuf_tensor("t_" + name, list(shape), dt))

        def ps(name, shape, dt):
            return es.enter_context(nc.psum_tensor(name, list(shape), dt))

        def sem(name):
            return es.enter_context(nc.semaphore(name))

        PE = Eng(nc.tensor, sem("s_pe"))
        ACT = Eng(nc.scalar, sem("s_act"))
        DVE = Eng(nc.vector, sem("s_dve"))
        POOL = Eng(nc.gpsimd, sem("s_pool"))
        SP = Eng(nc.sync, sem("s_sp"))
        k.PE, k.ACT, k.DVE, k.POOL, k.SP = PE, ACT, DVE, POOL, SP
        csem = DSem(sem("csem"))
        osem = DSem(sem("osem"))
        wsems = [DSem(sem(f"wsem{i}")) for i in range(NSLOT)]
        asem = DSem(sem("asem"))
        asem2 = DSem(sem("asem2"))
        ccsem = DSem(sem("ccsem"))
        dsem = DSem(sem("dsem"))
        usems = [DSem(sem(f"usem{i}")) for i in range(3)]
        k.dsems = [csem, osem, asem, asem2, ccsem, dsem] + usems

        def barrier():
            engs = [PE, ACT, DVE, POOL]
            dtoks = [(d_.s, d_.n) for d_ in k.dsems if d_.n > 0]
            for e in (PE, ACT, DVE):
                e.wait([(f.sem, f.cnt) for f in engs if f is not e and f.cnt > 0] + dtoks)
            BAR["toks"] = [(f.sem, f.cnt) for f in engs if f.cnt > 0] + dtoks

        h = sb("h", [128, KC, NT], F32)
        h_r = regs(KC, 3)
        xn = sb("xn", [128, KC, NT], BF16)
        xn_r = regs(KC, 3)
        wring = sb("wring", [128, NSLOT, SLOT], BF16)
        wring_r = regs(NSLOT)
        ones_bf = sb("ones_bf", [128, 128], BF16)
        ones_f = sb("ones_f", [128, 128], F32)
        ident_f = sb("ident_f", [128, 128], F32)
        ident_bf = sb("ident_bf", [128, 128], BF16)
        epsc = sb("epsc", [128, 1], F32)
        onec = sb("onec", [128, 1], F32)
        condT = sb("condT", [128, KC, 2], F32)
        scT = sb("scT", [128, KC, 2], BF16)
        adab = sb("adab", [128, 2, 18], F32)
        modloc = sb("modloc", [128, 2, 18, 2], F32)
        gains = sb("gains", [128, 7, KC], F32)
        modT = sb("modT", [128, 2, 72, 2], F32)
        coefA = sb("coefA", [128, 2, 3, KC, 2], F32)
        coefG = sb("coefG", [128, 2, 3, KC, 2], F32)
        qkg = sb("qkg", [128, 2], F32)
        rmat_f = sb("rmat_f", [128, 128], F32)
        rmat_bf = sb("rmat_bf", [128, 128], BF16)
        const_r = Reg()
        mod_r = Reg()
        k.coef_r = Reg()
        banks = [ps(f"bank{i}", [128, 512], F32) for i in range(8)]
        bank_r = [Reg(excl=True) for _ in range(8)]
        k.bank_i = 0

        def next_bank():
            b = k.bank_i
            k.bank_i = (b + 1) % 6
            return b

        DVE.op("memset", ones_bf[:], 1.0, writes=[const_r])
        DVE.op("memset", ones_f[:], 1.0, writes=[const_r])
        DVE.op("memset", epsc[:], EPS, writes=[const_r])
        DVE.op("memset", onec[:], 1.0, writes=[const_r])
        dma(SP, ident_f[:], I["ident"], csem)
        dma(SP, condT[:], I["condT"], csem)
        dma(SP, adab[:], I["adabT"].rearrange("l p c -> p l c"), csem)
        dma(SP, gains[:], I["gainsT"], csem)
        dma(SP, qkg[:], I["qkg"], csem)
        dma(SP, rmat_f[:], I["rmat"], csem)
        for kk in range(KC):
            dma(SP, h[:, kk, :], I["xT"][kk * 128:(kk + 1) * 128, :], csem)
        ctok = (csem.s, csem.n)
        const_r.w = None
        DVE.op("tensor_copy", ident_bf[:], ident_f[:], deps=[ctok], writes=[const_r])
        DVE.op("tensor_copy", rmat_bf[:], rmat_f[:], deps=[ctok], writes=[const_r])
        ACT.op("activation", scT[:], condT[:], AF.Silu, deps=[ctok], writes=[const_r])
        for kk in range(KC):
            for t in range(3):
                h_r[kk][t].w = ctok

        plan = []
        k.w_issued = 0
        k.w_used = 0
        slot_use_tok = [None] * NSLOT

        def slot_view(s):
            return wring[:, s, :]

        def issue_next():
            n = k.w_issued
            if n >= len(plan):
                return
            s = n % NSLOT
            POOL._deps((), [wring_r[s]], ())
            for (dst_fn, src) in plan[n]:
                tok = dma(POOL, dst_fn(slot_view(s)), src, wsems[s], phase=False)
            wring_r[s].w = tok
            wring_r[s].r = {}
            k.w_issued += 1

        def wnext():
            n = k.w_used
            while k.w_issued < min(len(plan), n + NSLOT):
                issue_next()
            k.w_used += 1
            return n % NSLOT

        def v3(slot_ap, off, a, c):
            return slot_ap[:, off:off + a * c].rearrange("p (a c) -> p a c", a=a)

        def plan_mods(l):
            for blk in range(6):
                src = I["ada_w"][l, :, blk * 384:(blk + 1) * 384].rearrange("(k p) c -> p k c", p=128)
                plan.append([(lambda sl: v3(sl, 0, KC, 384), src)])

        def plan_ffn(li):
            for j in range(11):
                sa = I["ffn_w_in"][li, :, j * 256:(j + 1) * 256].rearrange("(k p) c -> p k c", p=128)
                sb_ = I["ffn_w_in"][li, :, DFF + j * 256:DFF + (j + 1) * 256].rearrange("(k p) c -> p k c", p=128)
                plan.append([(lambda sl: v3(sl, 0, KC, 256), sa), (lambda sl: v3(sl, 2048, KC, 256), sb_)])
            for dc in range(8):
                so = I["ffn_w_out"][li, :, dc * 128:(dc + 1) * 128].rearrange("(f p) c -> p f c", p=128)
                plan.append([(lambda sl: v3(sl, 0, FC, 128), so)])

        def plan_attn():
            for rep in range(2 if k.sub >= 2 else 1):
                for blk in range(3):
                    src = I["attn_w_qkv"][:, blk * 512:(blk + 1) * 512].rearrange("(k p) c -> p k c", p=128)
                    plan.append([(lambda sl: v3(sl, 0, KC, 512), src)])
            for blk in range(2 if k.sub >= 5 else 0):
                src = I["attn_w_o"][:, blk * 512:(blk + 1) * 512].rearrange("(k p) c -> p k c", p=128)
                plan.append([(lambda sl: v3(sl, 0, KC, 512), src)])

        def plan_dn():
            for blk in range(8 if k.dsub >= 2 else 0):
                src = I["dn_w_in"][:, blk * 512:(blk + 1) * 512].rearrange("(k p) c -> p k c", p=128)
                plan.append([(lambda sl: v3(sl, 0, KC, 512), src)])
            for hh in range(2 if k.dsub >= 3 else 0):
                src = I["dn_w_in_own"][:, hh * 512:(hh + 1) * 512].rearrange("(k p) c -> p k c", p=128)
                plan.append([(lambda sl: v3(sl, 0, KC, 512), src)])
            for blk in range(2 if k.dsub >= 4 else 0):
                src = I["dn_w_o"][:, blk * 512:(blk + 1) * 512].rearrange("(k p) c -> p k c", p=128)
                plan.append([(lambda sl: v3(sl, 0, KC, 512), src)])

        plan_mods(0)
        plan_mods(1)
        plan_ffn(0)
        if stage >= 2:
            plan_attn()
        if stage >= 3:
            plan_ffn(1)
        if stage >= 4:
            plan_ffn(2)
        if stage >= 5:
            plan_dn()
        if stage >= 6:
            plan_ffn(3)

        def do_mods_all():
            b = next_bank()
            for l in range(2):
                for blk in range(6):
                    s = wnext()
                    wv = v3(slot_view(s), 0, KC, 384)
                    for c3 in range(3):
                        cc = l * 18 + blk * 3 + c3
                        for kk in range(KC):
                            PE.op("matmul", banks[b][:, cc * 2:cc * 2 + 2], wv[:, kk, c3 * 128:(c3 + 1) * 128], scT[:, kk, :],
                                  start=(kk == 0), stop=(kk == KC - 1),
                                  reads=[wring_r[s], const_r], writes=[bank_r[b]] if (kk == 0 and cc == 0) else [],
                                  inc=(kk == KC - 1 and c3 == 2))
            bank_r[b].w = (PE.sem, PE.cnt)
            pv = banks[b][:, 0:72].rearrange("p (l c j) -> p l c j", l=2, j=2)
            for j in range(2):
                DVE.op("tensor_tensor", modloc[:, :, :, j], pv[:, :, :, j], adab[:, :, :], ALU.add,
                       reads=[bank_r[b], const_r], writes=[mod_r])
            t0_ = dma(SP, agin0.ap(), modloc[:].rearrange("p l c j -> p (l c j)"), asem, reads=[mod_r])
            POOL.wait([t0_])
            nc.gpsimd.collective_compute("AllGather", ALU.bypass, replica_groups=GROUPS,
                                         ins=[agin0.ap().opt()], outs=[agout0.ap().opt()]).then_inc(ccsem.s, 1)
            ccsem.n += 1
            cct0 = (ccsem.s, ccsem.n)
            SP._deps((), [mod_r], [cct0])
            for rr in range(4):
                tok = dma(SP, modT[:, :, 18 * rr:18 * rr + 18, :],
                          agout0.ap()[rr * 128:(rr + 1) * 128, :].rearrange("p (l c j) -> p l c j", l=2, j=2), asem)
            mod_r.w = tok
            mod_r.r = {}
            for l in range(2):
                for s3 in range(3):
                    for j in range(2):
                        DVE.op("scalar_tensor_tensor", coefA[:, l, s3, :, j], modT[:, l, (3 * s3 + 1) * 8:(3 * s3 + 2) * 8, j],
                               1.0, gains[:, l * 3 + s3, :], ALU.add, ALU.mult, reads=[const_r, mod_r], writes=[k.coef_r])
                    DVE.op("tensor_scalar", coefG[:, l, s3, :, :], modT[:, l, (3 * s3 + 2) * 8:(3 * s3 + 3) * 8, :],
                           (1.0 if s3 == 1 else 0.5), None, ALU.mult, reads=[mod_r], writes=[k.coef_r])
            mod_r.w = (DVE.sem, DVE.cnt)

        def modnorm(dst, dst_r, coefA_fn, coefB_fn, tmp_pool):
            sq, sq_r, rstd, rstd_r, tmp, tmp_r = tmp_pool
            for t, (t0, t1) in enumerate(TILES):
                b = next_bank()
                for kk in range(KC):
                    i = kk % 2
                    ACT.op("activation", sq[:, i, :], h[:, kk, t0:t1], AF.Square, reads=[h_r[kk][t]], writes=[sq_r[i]])
                    PE.op("matmul", banks[b][:, :], ones_bf[:], sq[:, i, :], start=(kk == 0), stop=(kk == KC - 1),
                          reads=[sq_r[i], const_r], writes=[bank_r[b]] if kk == 0 else [], inc=True)
                bank_r[b].w = (PE.sem, PE.cnt)
                ACT.op("activation", rstd[:, :], banks[b][:, :], AF.Ln, bias=epsc[:, 0:1], scale=1.0 / D,
                       reads=[bank_r[b], const_r], writes=[rstd_r])
                ACT.op("activation", rstd[:, :], rstd[:, :], AF.Exp, scale=-0.5, reads=[rstd_r], writes=[rstd_r])
                for kk in range(KC):
                    i = kk % 2
                    A = coefA_fn(kk, t)
                    B = coefB_fn(kk, t)
                    if B is None:
                        DVE.op("scalar_tensor_tensor", dst(kk, t), h[:, kk, t0:t1], A, rstd[:, :], ALU.mult, ALU.mult,
                               reads=[h_r[kk][t], rstd_r, mod_r, const_r], writes=[dst_r(kk, t)])
                    else:
                        DVE.op("scalar_tensor_tensor", tmp[:, i, :], h[:, kk, t0:t1], A, rstd[:, :], ALU.mult, ALU.mult,
                               reads=[h_r[kk][t], rstd_r, mod_r], writes=[tmp_r[i]])
                        DVE.op("tensor_scalar", dst(kk, t), tmp[:, i, :], B, None, ALU.add,
                               reads=[tmp_r[i], mod_r], writes=[dst_r(kk, t)])

        def cj(t):
            return 0 if t == 0 else 1

        def ffn(l, which):
            li = l * 2 + which
            s3 = 0 if which == 0 else 2
            with ExitStack() as ph:
                def psb(name, shape, dt):
                    k.uid += 1
                    return ph.enter_context(nc.sbuf_tensor(f"p_{name}_{k.uid}", list(shape), dt))
                g = psb("g", [128, FC, NT], BF16)
                g_r = regs(FC, 3)
                sq = psb("sq", [128, 2, 512], BF16)
                rstd = psb("rstd", [128, 512], F32)
                tmp = psb("tmp", [128, 2, 512], F32)
                sa = psb("sa", [128, 2, 512], F32)
                sa_r = regs(2)
                pool = (sq, regs(2), rstd, Reg(), tmp, regs(2))
                modnorm(lambda kk, t: xn[:, kk, TILES[t][0]:TILES[t][1]], lambda kk, t: xn_r[kk][t],
                        lambda kk, t: coefA[:, l, s3, kk, cj(t):cj(t) + 1],
                        lambda kk, t: modT[:, l, (3 * s3) * 8 + kk, cj(t):cj(t) + 1], pool)
                n = 0
                for j in range(11):
                    s = wnext()
                    wa = v3(slot_view(s), 0, KC, 256)
                    wb = v3(slot_view(s), 2048, KC, 256)
                    for fl in range(2):
                        f = 2 * j + fl
                        for t, (t0, t1) in enumerate(TILES):
                            ba = next_bank()
                            for kk in range(KC):
                                PE.op("matmul", banks[ba][:, :], wa[:, kk, fl * 128:(fl + 1) * 128], xn[:, kk, t0:t1],
                                      start=(kk == 0), stop=(kk == KC - 1), reads=[wring_r[s], xn_r[kk][t]],
                                      writes=[bank_r[ba]] if kk == 0 else [], inc=(kk == KC - 1))
                            bank_r[ba].w = (PE.sem, PE.cnt)
                            bb = next_bank()
                            for kk in range(KC):
                                PE.op("matmul", banks[bb][:, :], wb[:, kk, fl * 128:(fl + 1) * 128], xn[:, kk, t0:t1],
                                      start=(kk == 0), stop=(kk == KC - 1), reads=[wring_r[s], xn_r[kk][t]],
                                      writes=[bank_r[bb]] if kk == 0 else [], inc=(kk == KC - 1))
                            bank_r[bb].w = (PE.sem, PE.cnt)
                            i = n % 2
                            n += 1
                            ACT.op("activation", sa[:, i, :], banks[ba][:, :], AF.Silu, reads=[bank_r[ba]], writes=[sa_r[i]])
                            DVE.op("tensor_tensor", g[:, f, t0:t1], sa[:, i, :], banks[bb][:, :], ALU.mult,
                                   reads=[sa_r[i], bank_r[bb]], writes=[g_r[f][t]])
                for dc in range(8):
                    s = wnext()
                    wo = v3(slot_view(s), 0, FC, 128)
                    if True:
                        for t, (t0, t1) in enumerate(TILES):
                            b = next_bank()
                            for f in range(FC):
                                PE.op("matmul", banks[b][:, :], wo[:, f, :], g[:, f, t0:t1],
                                      start=(f == 0), stop=(f == FC - 1), reads=[wring_r[s], g_r[f][t]],
                                      writes=[bank_r[b]] if f == 0 else [], inc=(f == FC - 1))
                            bank_r[b].w = (PE.sem, PE.cnt)
                            DVE.op("scalar_tensor_tensor", h[:, dc, t0:t1], banks[b][:, :], coefG[:, l, s3, dc, cj(t):cj(t) + 1],
                                   h[:, dc, t0:t1], ALU.mult, ALU.add, reads=[bank_r[b], mod_r], writes=[h_r[dc][t]])
                barrier()


        SM_SCALE = 128.0 ** -0.5

        def attend(q_ap, q_regs, nq, chunks, out_ap, out_reg, P, P_r, rec, rec_r):
            n = len(chunks)
            LA = 3
            st = {}

            def emit_s(i):
                KT, kreg, V, vreg = chunks[i]
                b = next_bank()
                PE.op("matmul", banks[b][:, 0:nq], KT, q_ap, start=True, stop=True,
                      reads=[kreg] + q_regs, writes=[bank_r[b]])
                pi = k.p_i % 6
                k.p_i += 1
                ACT.op("activation", P[:, pi, 0:nq], banks[b][:, 0:nq], AF.Exp, scale=SM_SCALE,
                       reads=[bank_r[b]], writes=[P_r[pi]])
                st[i] = pi

            def emit_pv(i):
                KT, kreg, V, vreg = chunks[i]
                pi = st[i]
                PE.op("matmul", banks[6][:, 0:nq], V, P[:, pi, 0:nq], start=(i == 0), stop=(i == n - 1),
                      reads=[P_r[pi], vreg], writes=[bank_r[6]] if i == 0 else [], inc=False)
                PE.op("matmul", banks[7][:, 0:nq], ones_bf[:], P[:, pi, 0:nq], start=(i == 0), stop=(i == n - 1),
                      reads=[P_r[pi], const_r], writes=[bank_r[7]] if i == 0 else [], inc=True)

            for i in range(min(LA, n)):
                emit_s(i)
            for i in range(n):
                emit_pv(i)
                if i + LA < n:
                    emit_s(i + LA)
            bank_r[6].w = (PE.sem, PE.cnt)
            bank_r[7].w = (PE.sem, PE.cnt)
            DVE.op("reciprocal", rec[:, 0:nq], banks[7][:, 0:nq], reads=[bank_r[7]], writes=[rec_r])
            DVE.op("tensor_tensor", out_ap, banks[6][:, 0:nq], rec[:, 0:nq], ALU.mult,
                   reads=[bank_r[6], rec_r], writes=[out_reg])

        def attention(l):
            k.p_i = 0
            with ExitStack() as ph:
                def psb(name, shape, dt):
                    k.uid += 1
                    return ph.enter_context(nc.sbuf_tensor(f"p_{name}_{k.uid}", list(shape), dt))
                sq = psb("sq", [128, 2, 512], BF16)
                sq_r = regs(2)
                rstd = psb("rstd", [128, 512], F32)
                rstd_r = Reg()
                tmp = psb("tmp", [128, 2, 512], F32)
                pool = (sq, sq_r, rstd, rstd_r, tmp, regs(2))
                modnorm(lambda kk, t: xn[:, kk, TILES[t][0]:TILES[t][1]], lambda kk, t: xn_r[kk][t],
                        lambda kk, t: coefA[:, l, 1, kk, cj(t):cj(t) + 1],
                        lambda kk, t: modT[:, l, 3 * 8 + kk, cj(t):cj(t) + 1], pool)
                P = psb("P", [128, 6, 512], BF16)
                P_r = regs(6)
                rec = psb("rec", [128, 512], F32)
                rec_r = Reg()
                rs2 = psb("rs2", [128, 512], F32)
                rs2_r = Reg()

                def headnorm(b, gcol):
                    ACT.op("activation", sq[:, 0, :], banks[b][:, :], AF.Square, reads=[bank_r[b]], writes=[sq_r[0]])
                    b2 = next_bank()
                    PE.op("matmul", banks[b2][:, :], ones_bf[:], sq[:, 0, :], start=True, stop=True,
                          reads=[sq_r[0], const_r], writes=[bank_r[b2]])
                    ACT.op("activation", rs2[:, :], banks[b2][:, :], AF.Ln, bias=epsc[:, 0:1], scale=1.0 / 128,
                           reads=[bank_r[b2], const_r], writes=[rs2_r])
                    ACT.op("activation", rs2[:, :], rs2[:, :], AF.Exp, scale=-0.5, reads=[rs2_r], writes=[rs2_r])

                def proj_fm(wv, c4, t, s):
                    t0, t1 = TILES[t]
                    b = next_bank()
                    for kk in range(KC):
                        PE.op("matmul", banks[b][:, :], wv[:, kk, c4 * 128:(c4 + 1) * 128], xn[:, kk, t0:t1],
                              start=(kk == 0), stop=(kk == KC - 1), reads=[wring_r[s], xn_r[kk][t]],
                              writes=[bank_r[b]] if kk == 0 else [], inc=(kk == KC - 1))
                    bank_r[b].w = (PE.sem, PE.cnt)
                    return b

                with ExitStack() as ph2:
                    def psb2(name, shape, dt):
                        k.uid += 1
                        return ph2.enter_context(nc.sbuf_tensor(f"p_{name}_{k.uid}", list(shape), dt))
                    qTp = psb2("qTp", [128, 8, 512], BF16)
                    qTp_r = regs(8)
                    kTp = psb2("kTp", [128, 2, 512], BF16)
                    kTp_r = regs(2)
                    Vp = psb2("Vp", [128, 4, 256], BF16)
                    Vp_r = regs(4)
                    koutf = psb2("koutf", [128, 2, 512], F32)
                    koutf_r = regs(2)
                    voutf = psb2("voutf", [128, 4, 256], F32)
                    voutf_r = regs(4)
                    for blk in range(3):
                        s = wnext()
                        wv = v3(slot_view(s), 0, KC, 512)
                        for c4 in range(4):
                            ch = blk * 4 + c4
                            if ch < 10:
                                b = proj_fm(wv, c4, 0, s)
                                headnorm(b, None)
                                if ch < 8:
                                    DVE.op("scalar_tensor_tensor", qTp[:, ch, :], banks[b][:, :], qkg[:, 0:1], rs2[:, :],
                                           ALU.mult, ALU.mult, reads=[bank_r[b], rs2_r, const_r], writes=[qTp_r[ch]])
                                else:
                                    DVE.op("scalar_tensor_tensor", koutf[:, ch - 8, :], banks[b][:, :], qkg[:, 1:2], rs2[:, :],
                                           ALU.mult, ALU.mult, reads=[bank_r[b], rs2_r, const_r], writes=[koutf_r[ch - 8]])
                                    ACT.op("activation", kTp[:, ch - 8, :], koutf[:, ch - 8, :], AF.Copy,
                                           reads=[koutf_r[ch - 8]], writes=[kTp_r[ch - 8]])
                            elif ch == 10:
                                for c in range(4):
                                    b = next_bank()
                                    for kk in range(KC):
                                        PE.op("matmul", banks[b][:, 0:256], xn[:, kk, c * 128:(c + 1) * 128], wv[:, kk, 256:512],
                                              start=(kk == 0), stop=(kk == KC - 1), reads=[wring_r[s], xn_r[kk][0]],
                                              writes=[bank_r[b]] if kk == 0 else [], inc=(kk == KC - 1))
                                    bank_r[b].w = (PE.sem, PE.cnt)
                                    ACT.op("activation", voutf[:, c, :], banks[b][:, 0:256], AF.Copy, reads=[bank_r[b]], writes=[voutf_r[c]])
                                    DVE.op("tensor_copy", Vp[:, c, :], banks[b][:, 0:256], reads=[bank_r[b]], writes=[Vp_r[c]])
                    dma(SP, O["kout"], koutf[:], osem, reads=koutf_r)
                    dma(SP, O["vout"], voutf[:], osem, reads=voutf_r)
                    for s2 in range(2):
                        for hd in range(8):
                            kv = hd // 4
                            chunks = []
                            for j in range(2):
                                chunks.append((kTp[:, kv, s2 * 256 + j * 128: s2 * 256 + (j + 1) * 128], kTp_r[kv],
                                               Vp[:, 2 * s2 + j, kv * 128:(kv + 1) * 128], Vp_r[2 * s2 + j]))
                            attend(qTp[:, hd, s2 * 256:(s2 + 1) * 256], [qTp_r[hd]], 256, chunks,
                                   xn[:, hd, s2 * 256:(s2 + 1) * 256], xn_r[hd][0], P, P_r, rec, rec_r)
                    for e in (ACT, DVE, PE):
                        e.wait([(osem.s, osem.n)])
                    barrier()

                with ExitStack() as ph2:
                  if k.sub >= 2:
                      def psb2(name, shape, dt):
                          k.uid += 1
                          return ph2.enter_context(nc.sbuf_tensor(f"p_{name}_{k.uid}", list(shape), dt))
                      qTs = psb2("qTs", [128, 8, 1024], BF16)
                      qTs_r = regs(8, 2)
                      KTf = psb2("KTf", [128, 2, 4608], BF16)
                      KTf_r = Reg()
                      Vf = psb2("Vf", [128, 36, 256], BF16)
                      Vf_r = Reg()
                      ksT = psb2("ksT", [128, 2, 1024], BF16)
                      ksT_r = Reg()
                      vs = psb2("vs", [128, 8, 256], BF16)
                      vs_r = Reg()
                      ropeC = psb2("ropeC", [128, 1024], F32)
                      ropeS = psb2("ropeS", [128, 1024], F32)
                      rope_r = Reg()
                      qn = psb2("qn", [128, 512], F32)
                      qn_r = Reg()
                      qnb = psb2("qnb", [128, 512], BF16)
                      qnb_r = Reg()
                      t1 = psb2("t1", [128, 512], F32)
                      t1_r = Reg()
                      t2 = psb2("t2", [128, 512], F32)
                      t2_r = Reg()
                      dma(SP, ropeC[:], I["ropeC"], asem, writes=[rope_r])
                      dma(SP, ropeS[:], I["ropeS"], asem, writes=[rope_r])
                      rope_r.w = (asem.s, asem.n)
                      dma(POOL, KTf[:, :, 0:512], I["ckT"], asem2, writes=[KTf_r])
                      dma(POOL, Vf[:, 0:4, :], I["cv"], asem2, writes=[Vf_r])
                      for blk in range(3):
                          s = wnext()
                          wv = v3(slot_view(s), 0, KC, 512)
                          for c4 in range(4):
                              ch = blk * 4 + c4
                              if ch < 10:
                                  for t in (1, 2):
                                      c0 = (t - 1) * 512
                                      b = proj_fm(wv, c4, t, s)
                                      headnorm(b, None)
                                      gc = qkg[:, 0:1] if ch < 8 else qkg[:, 1:2]
                                      DVE.op("scalar_tensor_tensor", qn[:, :], banks[b][:, :], gc, rs2[:, :],
                                             ALU.mult, ALU.mult, reads=[bank_r[b], rs2_r, const_r], writes=[qn_r])
                                      ACT.op("activation", qnb[:, :], qn[:, :], AF.Copy, reads=[qn_r], writes=[qnb_r])
                                      b3 = next_bank()
                                      PE.op("matmul", banks[b3][:, :], rmat_bf[:], qnb[:, :], start=True, stop=True,
                                            reads=[qnb_r, const_r], writes=[bank_r[b3]])
                                      DVE.op("tensor_tensor", t1[:, :], qn[:, :], ropeC[:, c0:c0 + 512], ALU.mult,
                                             reads=[qn_r, rope_r], writes=[t1_r])
                                      DVE.op("tensor_tensor", t2[:, :], banks[b3][:, :], ropeS[:, c0:c0 + 512], ALU.mult,
                                             reads=[bank_r[b3], rope_r], writes=[t2_r])
                                      if ch < 8:
                                          DVE.op("tensor_tensor", qTs[:, ch, c0:c0 + 512], t1[:, :], t2[:, :], ALU.add,
                                                 reads=[t1_r, t2_r], writes=[qTs_r[ch][t - 1]])
                                      else:
                                          DVE.op("tensor_tensor", ksT[:, ch - 8, c0:c0 + 512], t1[:, :], t2[:, :], ALU.add,
                                                 reads=[t1_r, t2_r], writes=[ksT_r])
                              elif ch == 10:
                                  for c in range(4, 12):
                                      b = next_bank()
                                      t = 1 if c < 8 else 2
                                      for kk in range(KC):
                                          PE.op("matmul", banks[b][:, 0:256], xn[:, kk, c * 128:(c + 1) * 128], wv[:, kk, 256:512],
                                                start=(kk == 0), stop=(kk == KC - 1), reads=[wring_r[s], xn_r[kk][t]],
                                                writes=[bank_r[b]] if kk == 0 else [], inc=(kk == KC - 1))
                                      bank_r[b].w = (PE.sem, PE.cnt)
                                      ACT.op("activation", vs[:, c - 4, :], banks[b][:, 0:256], AF.Copy, reads=[bank_r[b]], writes=[vs_r])
                      if k.sub < 3:
                        return
                      ta = dma(SP, agin1.ap()[0:128, :], ksT[:].rearrange("p h t -> p (h t)"), asem, reads=[ksT_r])
                      tb = dma(SP, agin1.ap()[128:256, :], vs[:].rearrange("p c f -> p (c f)"), asem, reads=[vs_r])
                      POOL.wait([ta, tb])
                      nc.gpsimd.collective_compute("AllGather", ALU.bypass, replica_groups=GROUPS,
                                                   ins=[agin1.ap().opt()], outs=[agout1.ap().opt()]).then_inc(ccsem.s, 1)
                      ccsem.n += 1
                      cct = (ccsem.s, ccsem.n)
                      SP._deps((), [KTf_r, Vf_r], [cct])
                      for rr in range(4):
                          dma(SP, KTf[:, :, 512 + rr * 1024:512 + (rr + 1) * 1024],
                              agout1.ap()[rr * 256:rr * 256 + 128, :].rearrange("p (h t) -> p h t", h=2), asem)
                          dma(SP, Vf[:, 4 + rr * 8:4 + (rr + 1) * 8, :],
                              agout1.ap()[rr * 256 + 128:(rr + 1) * 256, :].rearrange("p (c f) -> p c f", c=8), asem)
                      kvtok = (asem.s, asem.n)
                      KTf_r.w = None
                      Vf_r.w = None
                      PE.wait([kvtok, (asem2.s, asem2.n)])
                      if dbg and k.dbgsrc == 'kv':
                          POOL.wait([kvtok, (asem2.s, asem2.n)])
                          for hh in range(2):
                              for j3 in range(3):
                                  dma(POOL, O["dbg"][(hh * 3 + j3) * 128:(hh * 3 + j3 + 1) * 128, :], KTf[:, hh, j3 * 1536:(j3 + 1) * 1536], osem)
                          dma(POOL, O["dbg"][768:896, :], Vf[:, 0:6, :].rearrange("p c f -> p (c f)"), osem)
                          dma(POOL, O["dbg"][896:1024, :], Vf[:, 30:36, :].rearrange("p c f -> p (c f)"), osem)
                          for e in (ACT, DVE, PE):
                              e.wait([(osem.s, osem.n)])
                          return
                      if k.sub < 4:
                          return
                      for t in (1, 2):
                          for hd in range(8):
                              kv = hd // 4
                              chunks = []
                              for c in range(36):
                                  chunks.append((KTf[:, kv, c * 128:(c + 1) * 128], KTf_r,
                                                 Vf[:, c, kv * 128:(kv + 1) * 128], Vf_r))
                              attend(qTs[:, hd, (t - 1) * 512:t * 512], [qTs_r[hd][t - 1]], 512, chunks,
                                     xn[:, hd, TILES[t][0]:TILES[t][1]], xn_r[hd][t], P, P_r, rec, rec_r)
                      barrier()

                if dbg and k.dbgsrc == 'at':
                    for kk in range(KC):
                        dma(POOL, O["dbg"][kk * 128:(kk + 1) * 128, :], xn[:, kk, :], osem, reads=xn_r[kk])
                    for e in (ACT, DVE, PE):
                        e.wait([(osem.s, osem.n)])
                for blk in range(2 if k.sub >= 5 else 0):
                    s = wnext()
                    wv = v3(slot_view(s), 0, KC, 512)
                    for c4 in range(4):
                        dc = blk * 4 + c4
                        for t, (t0, t1_) in enumerate(TILES):
                            b = next_bank()
                            for kk in range(KC):
                                PE.op("matmul", banks[b][:, :], wv[:, kk, c4 * 128:(c4 + 1) * 128], xn[:, kk, t0:t1_],
                                      start=(kk == 0), stop=(kk == KC - 1), reads=[wring_r[s], xn_r[kk][t]],
                                      writes=[bank_r[b]] if kk == 0 else [], inc=(kk == KC - 1))
                            bank_r[b].w = (PE.sem, PE.cnt)
                            DVE.op("scalar_tensor_tensor", h[:, dc, t0:t1_], banks[b][:, :], coefG[:, l, 1, dc, cj(t):cj(t) + 1],
                                   h[:, dc, t0:t1_], ALU.mult, ALU.add, reads=[bank_r[b], mod_r], writes=[h_r[dc][t]])
                barrier()


        def run_threads(gens):
            gens = list(gens)
            while gens:
                alive = []
                for g_ in gens:
                    try:
                        next(g_)
                        alive.append(g_)
                    except StopIteration:
                        pass
                gens = alive

        class Obj:
            pass

        def deltanet(l):
            with ExitStack() as ph:
                def psb(name, shape, dt):
                    k.uid += 1
                    return ph.enter_context(nc.sbuf_tensor(f"p_{name}_{k.uid}", list(shape), dt))
                masks = psb("masks", [128, 4, 128], F32)
                convp = psb("convp", [128, 24, 3], F32)
                convo = psb("convo", [128, 6, 3], F32)
                wabp = psb("wabp", [128, KC, 32], BF16)
                wabo = psb("wabo", [128, KC, 8], BF16)
                dtbp = psb("dtbp", [128, 16], F32)
                negAp = psb("negAp", [128, 16], F32)
                dtbo = psb("dtbo", [128, 4], F32)
                negAo = psb("negAo", [128, 4], F32)
                ong = psb("ong", [128, 1], F32)
                selt = psb("selt", [128, 4], F32)
                lmask = psb("lmask", [128, 7, 128], BF16)
                ident2 = psb("ident2", [128, 256], BF16)
                DVE.op("tensor_copy", ident2[:, 0:128], ident_f[:], reads=[const_r])
                DVE.op("tensor_copy", ident2[:, 128:256], ident_f[:], reads=[const_r])
                dn_r = Reg()
                dma(POOL, lmask[:], I["lmask"], asem2)
                dma(SP, masks[:], I["masks"], dsem)
                dma(SP, convp[:], I["convT"], dsem)
                dma(SP, convo[:], I["convT_own"], dsem)
                dma(SP, dtbp[:], I["dtb_p"].partition_broadcast(128), dsem)
                dma(SP, negAp[:], I["alog_p"].partition_broadcast(128), dsem)
                dma(SP, dtbo[:], I["dtb_o"].partition_broadcast(128), dsem)
                dma(SP, negAo[:], I["alog_o"].partition_broadcast(128), dsem)
                dma(SP, ong[:], I["ong"], dsem)
                dma(SP, selt[:], I["sel"].partition_broadcast(128), dsem)
                dma(POOL, wabp[:], I["wab_p"].rearrange("(k p) c -> p k c", p=128), asem2)
                dma(POOL, wabo[:], I["wab_o"].rearrange("(k p) c -> p k c", p=128), asem2)
                dtok = (dsem.s, dsem.n)
                dtok2 = (asem2.s, asem2.n)
                for (na, nA) in ((16, negAp), (4, negAo)):
                    ACT.op("activation", nA[:, :], nA[:, :], AF.Exp, deps=[dtok], writes=[dn_r])
                    DVE.op("tensor_scalar", nA[:, :], nA[:, :], -1.0, None, ALU.mult, writes=[dn_r])
                dn_r.w = None
                for e_ in (PE, ACT, DVE, POOL):
                    e_.wait([dtok, dtok2, (DVE.sem, DVE.cnt)])

                sq = psb("sq", [128, 2, 512], BF16)
                sq_r = regs(2)
                rstd = psb("rstd", [128, 512], F32)
                rstd_r = Reg()
                tmp = psb("tmp", [128, 2, 512], F32)
                pool = (sq, sq_r, rstd, rstd_r, tmp, regs(2))
                modnorm(lambda kk, t: xn[:, kk, TILES[t][0]:TILES[t][1]], lambda kk, t: xn_r[kk][t],
                        lambda kk, t: coefA[:, l, 1, kk, cj(t):cj(t) + 1],
                        lambda kk, t: modT[:, l, 3 * 8 + kk, cj(t):cj(t) + 1], pool)
                for x in range(2):
                    ta = dma(SP, agin2[x].ap().rearrange("(k p) t -> p k t", p=128), xn[:, 4 * x:4 * x + 4, 512:1536], asem,
                             reads=[xn_r[kk][t] for kk in range(4 * x, 4 * x + 4) for t in (1, 2)])
                POOL.wait([ta])
                for x in range(2):
                    nc.gpsimd.collective_compute("AllGather", ALU.bypass, replica_groups=GROUPS,
                                                 ins=[agin2[x].ap().opt()], outs=[agout2[x].ap().opt()]).then_inc(ccsem.s, 1)
                    ccsem.n += 1
                cct2 = (ccsem.s, ccsem.n)

                if k.dsub < 2:
                    for e_ in (PE, ACT, DVE):
                        e_.wait([cct2])
                    barrier()
                    return
                G = 4
                TS = []
                for g_ in range(G):
                    t = Obj()
                    t.cols = psb("cols", [128, 8], F32); t.cols_r = Reg()
                    t.dgd = psb("dgd", [128, 128], F32); t.dgd_r = Reg()
                    t.gcr = psb("gcr", [128, 128], F32); t.gcr_r = Reg()
                    t.egr = psb("egr", [128, 128], F32); t.egr_r = Reg()
                    t.ddT = psb("ddT", [128, 128], F32); t.ddT_r = Reg()
                    t.kq = psb("kq", [128, 256], BF16); t.kks_r = Reg()
                    t.kvt = psb("kvt", [128, 256], BF16); t.ktok_r = Reg()
                    t.A = [psb("A0", [128, 128], BF16), psb("A1", [128, 128], BF16)]; t.A_r = regs(2)
                    t.BP = [psb("BP0", [128, 256], BF16), psb("BP1", [128, 256], BF16)]; t.BP_r = regs(2)
                    t.TT = psb("TT", [128, 128], BF16); t.TT_r = Reg()
                    t.vb = psb("vb", [128, 128], BF16); t.vb_r = Reg()
                    t.kbg = psb("kbg", [128, 128], BF16); t.kbg_r = Reg()
                    TS.append(t)
                OS = []
                for par in range(2):
                    row = []
                    for g_ in range(G):
                        o = Obj()
                        o.wT = psb("wT", [128, 128], BF16)
                        o.u = psb("u", [128, 128], F32)
                        o.qkT = psb("qkT", [128, 128], BF16)
                        o.qdT = psb("qdT", [128, 128], BF16)
                        o.kd = psb("kd", [128, 128], BF16)
                        o.glc = psb("glc", [128, 1], F32)
                        o.r = Reg()
                        row.append(o)
                    OS.append(row)
                CH = []
                for c_ in range(2):
                    ch = Obj()
                    ch.S = psb("S", [128, 128], F32); ch.S_r = Reg()
                    ch.Sb = psb("Sb", [128, 128], BF16); ch.Sb_r = Reg()
                    ch.vn = psb("vn", [128, 128], BF16); ch.vn_r = Reg()
                    CH.append(ch)
                cvt = psb("cvt", [128, 2, 256], F32)
                cvt_r = regs(2)
                slf = psb("slf", [128, 2, 256], F32)
                slf_r = regs(2)
                gtmp = psb("gtmp", [128, 2, 32], F32)
                gtmp_r = regs(2)
                k.cv_i = 0

                def prep(sy):
                    t = TS[sy.g]
                    o = OS[sy.par][sy.g]
                    e = sy.e
                    tri = masks[:, e, :]
                    mA = masks[:, 2 + e, :]
                    last = 127 if e == 0 else 0
                    gcc = t.cols[:, 0:1]
                    b = next_bank()
                    PE.op("matmul", banks[b][:, 0:128], sy.kT, sy.kT, start=True, stop=True, reads=sy.qkv_r, writes=[bank_r[b]], inc=False)
                    PE.op("matmul", banks[b][:, 128:256], sy.kT, sy.qT, start=True, stop=True, reads=sy.qkv_r, inc=False)
                    PE.op("matmul", banks[b][:, 256:384], sy.kT, ident_bf[:], start=True, stop=True, reads=sy.qkv_r + [const_r], inc=False)
                    PE.op("matmul", banks[b][:, 384:512], sy.vT, ident_bf[:], start=True, stop=True, reads=sy.qkv_r + [const_r])
                    bank_r[b].w = (PE.sem, PE.cnt)
                    DVE.op("tensor_copy", t.kq[:, :], banks[b][:, 0:256], reads=[bank_r[b]], writes=[t.kks_r])
                    DVE.op("tensor_copy", t.kvt[:, :], banks[b][:, 256:512], reads=[bank_r[b]], writes=[t.ktok_r])
                    b = next_bank()
                    PE.op("matmul", banks[b][:, 0:1], tri, sy.gcol, start=True, stop=True, reads=[sy.g_r], writes=[bank_r[b]])
                    ACT.op("activation", gcc, banks[b][:, 0:1], AF.Copy, reads=[bank_r[b]], writes=[t.cols_r])
                    yield
                    DVE.op("tensor_scalar", t.dgd[:, :], ident_f[:], gcc, None, ALU.mult, reads=[t.cols_r, const_r], writes=[t.dgd_r])
                    yield
                    b = next_bank()
                    PE.op("matmul", banks[b][:, 0:128], ones_f[:], t.dgd[:, :], start=True, stop=True, reads=[t.dgd_r, const_r], writes=[bank_r[b]])
                    ACT.op("activation", t.gcr[:, :], banks[b][:, 0:128], AF.Copy, reads=[bank_r[b]], writes=[t.gcr_r])
                    ACT.op("activation", t.egr[:, :], banks[b][:, 0:128], AF.Exp, reads=[bank_r[b]], writes=[t.egr_r])
                    ACT.op("activation", t.cols[:, 1:2], gcc, AF.Exp, reads=[t.cols_r], writes=[t.cols_r])
                    ACT.op("activation", t.cols[:, 2:3], gcc, AF.Exp, scale=-1.0, bias=t.gcr[:, last:last + 1],
                           reads=[t.cols_r, t.gcr_r], writes=[t.cols_r])
                    yield
                    DVE.op("tensor_scalar", t.dgd[:, :], t.gcr[:, :], gcc, 0.0, ALU.subtract, ALU.max,
                           reads=[t.gcr_r, t.cols_r], writes=[t.dgd_r])
                    DVE.op("tensor_scalar", t.ddT[:, :], t.gcr[:, :], gcc, 0.0, ALU.subtract, ALU.min,
                           reads=[t.gcr_r, t.cols_r], writes=[t.ddT_r])
                    DVE.op("tensor_tensor", t.cols[:, 3:4], sy.bcol, t.cols[:, 1:2], ALU.mult, reads=[t.cols_r, sy.g_r], writes=[t.cols_r])
                    DVE.op("tensor_copy", o.glc[:, :], t.egr[:, last:last + 1], reads=[t.egr_r], writes=[o.r])
                    yield
                    ACT.op("activation", t.dgd[:, :], t.dgd[:, :], AF.Exp, scale=-1.0, reads=[t.dgd_r], writes=[t.dgd_r])
                    ACT.op("activation", t.ddT[:, :], t.ddT[:, :], AF.Exp, reads=[t.ddT_r], writes=[t.ddT_r])
                    yield
                    DVE.op("tensor_tensor", t.dgd[:, :], t.dgd[:, :], mA, ALU.mult, reads=[t.dgd_r], writes=[t.dgd_r])
                    DVE.op("scalar_tensor_tensor", t.A[0][:, :], t.kq[:, 0:128], sy.bcol, t.dgd[:, :], ALU.mult, ALU.mult,
                           reads=[t.kks_r, t.dgd_r, sy.g_r], writes=[t.A_r[0]])
                    DVE.op("tensor_tensor", t.ddT[:, :], t.ddT[:, :], tri, ALU.mult, reads=[t.ddT_r], writes=[t.ddT_r])
                    yield
                    b = next_bank()
                    PE.op("matmul", banks[b][:, 0:128], t.A[0][:, :], ident_bf[:], start=True, stop=True, reads=[t.A_r[0], const_r], writes=[bank_r[b]])
                    DVE.op("tensor_copy", t.A[1][:, :], banks[b][:, 0:128], reads=[bank_r[b]], writes=[t.A_r[1]])
                    POOL.op("tensor_tensor", o.qkT[:, :], t.kq[:, 128:256], t.ddT[:, :], ALU.mult, reads=[t.kks_r, t.ddT_r], writes=[o.r])
                    POOL.op("tensor_tensor", o.qdT[:, :], sy.qT, t.egr[:, :], ALU.mult, reads=sy.qkv_r + [t.egr_r], writes=[o.r])
                    POOL.op("tensor_scalar", t.vb[:, :], t.kvt[:, 128:256], sy.bcol, None, ALU.mult, reads=[t.ktok_r, sy.g_r], writes=[t.vb_r])
                    POOL.op("tensor_scalar", t.kbg[:, :], t.kvt[:, 0:128], t.cols[:, 3:4], None, ALU.mult, reads=[t.ktok_r, t.cols_r], writes=[t.kbg_r])
                    POOL.op("tensor_scalar", o.kd[:, :], t.kvt[:, 0:128], t.cols[:, 2:3], None, ALU.mult, reads=[t.ktok_r, t.cols_r], writes=[o.r])
                    yield
                    if e == 0:
                        Bin, Bin_r = t.A[1], t.A_r[1]
                    else:
                        Bin, Bin_r = t.A[0], t.A_r[0]
                    Tc, TTc, TQ_r = ident2[:, 0:128], ident2[:, 128:256], const_r
                    TQc = ident2[:, 0:256]
                    cur = 0
                    for li in range(7):
                        b = next_bank()
                        PE.op("matmul", banks[b][:, 0:128], Bin[:, :], Tc, start=True, stop=True, reads=[Bin_r, TQ_r], writes=[bank_r[b]])
                        DVE.op("scalar_tensor_tensor", t.TT[:, :], banks[b][:, 0:128], -1.0, lmask[:, li, :], ALU.mult, ALU.mult,
                               reads=[bank_r[b], dn_r], writes=[t.TT_r])
                        b2 = next_bank()
                        PE.op("matmul", banks[b2][:, 0:256], ident_bf[:], TQc, start=True, stop=False, reads=[TQ_r, const_r], writes=[bank_r[b2]], inc=False)
                        PE.op("matmul", banks[b2][:, 0:128], TTc, t.TT[:, :], start=False, stop=True, reads=[TQ_r, t.TT_r], inc=False)
                        PE.op("matmul", banks[b2][:, 128:256], t.TT[:, :], TTc, start=False, stop=True, reads=[TQ_r, t.TT_r])
                        bank_r[b2].w = (PE.sem, PE.cnt)
                        ACT.op("activation", t.BP[cur][:, 0:256], banks[b2][:, 0:256], AF.Copy, reads=[bank_r[b2]], writes=[t.BP_r[cur]])
                        Tc, TTc, TQ_r = t.BP[cur][:, 0:128], t.BP[cur][:, 128:256], t.BP_r[cur]
                        TQc = t.BP[cur][:, 0:256]
                        cur = 1 - cur
                        yield
                    TTf = TTc if e == 0 else Tc
                    TTf_r = TQ_r
                    b = next_bank()
                    PE.op("matmul", banks[b][:, 0:128], t.kbg[:, :], TTf, start=True, stop=True, reads=[t.kbg_r, TTf_r], writes=[bank_r[b]], inc=False)
                    PE.op("matmul", banks[b][:, 128:256], TTf, t.vb[:, :], start=True, stop=True, reads=[t.vb_r, TTf_r])
                    bank_r[b].w = (PE.sem, PE.cnt)
                    ACT.op("activation", o.wT[:, :], banks[b][:, 0:128], AF.Copy, reads=[bank_r[b]], writes=[o.r])
                    DVE.op("tensor_copy", o.u[:, :], banks[b][:, 128:256], reads=[bank_r[b]], writes=[o.r])
                    yield

                def scan(ch, steps):
                    for (o, oacc, oacc_r) in steps:
                        b = next_bank()
                        PE.op("matmul", banks[b][:, 0:128], o.wT[:, :], ch.Sb[:, :], start=True, stop=True, reads=[o.r, ch.Sb_r], writes=[bank_r[b]])
                        DVE.op("tensor_tensor", ch.vn[:, :], o.u[:, :], banks[b][:, 0:128], ALU.subtract, reads=[o.r, bank_r[b]], writes=[ch.vn_r])
                        yield
                        b = next_bank()
                        PE.op("matmul", banks[b][:, 0:128], ch.Sb[:, :], o.qdT[:, :], start=True, stop=False, reads=[o.r, ch.Sb_r], writes=[bank_r[b]], inc=False)
                        PE.op("matmul", banks[b][:, 0:128], ch.vn[:, :], o.qkT[:, :], start=False, stop=True, reads=[o.r, ch.vn_r], writes=[bank_r[b]])
                        DVE.op("tensor_tensor", oacc, oacc, banks[b][:, 0:128], ALU.add, reads=[bank_r[b]], writes=[oacc_r])
                        yield
                        b = next_bank()
                        PE.op("matmul", banks[b][:, 0:128], o.kd[:, :], ch.vn[:, :], start=True, stop=True, reads=[o.r, ch.vn_r], writes=[bank_r[b]])
                        DVE.op("scalar_tensor_tensor", ch.S[:, :], ch.S[:, :], o.glc[:, 0:1], banks[b][:, 0:128], ALU.mult, ALU.add,
                               reads=[o.r, bank_r[b]], writes=[ch.S_r])
                        ACT.op("activation", ch.Sb[:, :], ch.S[:, :], AF.Copy, reads=[ch.S_r], writes=[ch.Sb_r])
                        yield

                def proj_chunk(wv, c4, s, ublk, ublk_r, typ, conv_ap, dst, dst_r):
                    b = next_bank()
                    for kk in range(KC):
                        PE.op("matmul", banks[b][:, 0:258], wv[:, kk, c4 * 128:(c4 + 1) * 128], ublk[:, kk, :],
                              start=(kk == 0), stop=(kk == KC - 1), reads=[wring_r[s], ublk_r],
                              writes=[bank_r[b]] if kk == 0 else [], inc=(kk == KC - 1))
                    bank_r[b].w = (PE.sem, PE.cnt)
                    if typ == 3:
                        ACT.op("activation", dst, banks[b][:, 1:257], AF.Silu, reads=[bank_r[b]], writes=[dst_r])
                        return
                    i = k.cv_i % 2
                    k.cv_i += 1
                    ACT.op("activation", cvt[:, i, :], banks[b][:, 1:257], AF.Copy, scale=conv_ap[:, 1:2], reads=[bank_r[b], dn_r], writes=[cvt_r[i]])
                    DVE.op("scalar_tensor_tensor", cvt[:, i, :], banks[b][:, 0:256], conv_ap[:, 0:1], cvt[:, i, :], ALU.mult, ALU.add,
                           reads=[bank_r[b], dn_r], writes=[cvt_r[i]])
                    DVE.op("scalar_tensor_tensor", cvt[:, i, :], banks[b][:, 2:258], conv_ap[:, 2:3], cvt[:, i, :], ALU.mult, ALU.add,
                           reads=[bank_r[b], dn_r], writes=[cvt_r[i]])
                    if typ == 2:
                        ACT.op("activation", dst, cvt[:, i, :], AF.Silu, reads=[cvt_r[i]], writes=[dst_r])
                        return
                    ACT.op("activation", slf[:, i, :], cvt[:, i, :], AF.Silu, reads=[cvt_r[i]], writes=[slf_r[i]])
                    ACT.op("activation", sq[:, i, 0:256], slf[:, i, :], AF.Square, reads=[slf_r[i]], writes=[sq_r[i]])
                    b2 = next_bank()
                    PE.op("matmul", banks[b2][:, 0:256], ones_bf[:], sq[:, i, 0:256], start=True, stop=True, reads=[sq_r[i], const_r], writes=[bank_r[b2]])
                    ACT.op("activation", cvt[:, i, :], banks[b2][:, 0:256], AF.Ln, bias=epsc[:, 0:1], reads=[bank_r[b2], const_r], writes=[cvt_r[i]])
                    ACT.op("activation", cvt[:, i, :], cvt[:, i, :], AF.Exp, scale=-0.5, reads=[cvt_r[i]], writes=[cvt_r[i]])
                    DVE.op("scalar_tensor_tensor", dst, slf[:, i, :], (128.0 ** -0.5) if typ == 0 else 1.0, cvt[:, i, :], ALU.mult, ALU.mult,
                           reads=[slf_r[i], cvt_r[i]], writes=[dst_r])

                def gbeta(ublk, ublk_r, wab, ncol, dtb, negA, dst_fn):
                    na = ncol // 2
                    for i in range(2):
                        b = next_bank()
                        for kk in range(KC):
                            PE.op("matmul", banks[b][:, 0:ncol], ublk[:, kk, 1 + i * 128:1 + (i + 1) * 128], wab[:, kk, :],
                                  start=(kk == 0), stop=(kk == KC - 1), reads=[ublk_r, dn_r],
                                  writes=[bank_r[b]] if kk == 0 else [], inc=(kk == KC - 1))
                        bank_r[b].w = (PE.sem, PE.cnt)
                        d = dst_fn(i)
                        DVE.op("tensor_tensor", gtmp[:, i, 0:na], banks[b][:, 0:na], dtb[:, :], ALU.add, reads=[bank_r[b], dn_r], writes=[gtmp_r[i]])
                        ACT.op("activation", gtmp[:, i, na:ncol], banks[b][:, na:ncol], AF.Exp, scale=-1.0, reads=[bank_r[b]], writes=[gtmp_r[i]])
                        ACT.op("activation", gtmp[:, i, 0:na], gtmp[:, i, 0:na], AF.Exp, reads=[gtmp_r[i]], writes=[gtmp_r[i]])
                        ACT.op("activation", gtmp[:, i, 0:na], gtmp[:, i, 0:na], AF.Ln, bias=onec[:, 0:1], reads=[gtmp_r[i], const_r], writes=[gtmp_r[i]])
                        DVE.op("tensor_tensor", d[:, 0:na], gtmp[:, i, 0:na], negA[:, :], ALU.mult, reads=[gtmp_r[i], dn_r], writes=[k.gst_r])
                        DVE.op("tensor_scalar", gtmp[:, i, na:ncol], gtmp[:, i, na:ncol], 1.0, None, ALU.add, reads=[gtmp_r[i]], writes=[gtmp_r[i]])
                        DVE.op("reciprocal", d[:, na:ncol], gtmp[:, i, na:ncol], reads=[gtmp_r[i]], writes=[k.gst_r])

                def post(oacc, oacc_r, zs_fn, zs_r_fn, T, dst_fn, dst_r_fn):
                    for p0 in range(0, T, 512):
                        w = min(512, T - p0)
                        ACT.op("activation", sq[:, 0, 0:w], oacc[:, p0:p0 + w], AF.Square, reads=[oacc_r], writes=[sq_r[0]])
                        b = next_bank()
                        PE.op("matmul", banks[b][:, 0:w], ones_bf[:], sq[:, 0, 0:w], start=True, stop=True, reads=[sq_r[0], const_r], writes=[bank_r[b]])
                        ACT.op("activation", rstd[:, 0:w], banks[b][:, 0:w], AF.Ln, bias=epsc[:, 0:1], scale=1.0 / 128, reads=[bank_r[b], const_r], writes=[rstd_r])
                        ACT.op("activation", rstd[:, 0:w], rstd[:, 0:w], AF.Exp, scale=-0.5, reads=[rstd_r], writes=[rstd_r])
                        DVE.op("scalar_tensor_tensor", tmp[:, 0, 0:w], oacc[:, p0:p0 + w], ong[:, 0:1], rstd[:, 0:w], ALU.mult, ALU.mult,
                               reads=[oacc_r, rstd_r, dn_r], writes=[k.tmp0_r])
                        DVE.op("tensor_tensor", dst_fn(p0, w), tmp[:, 0, 0:w], zs_fn(p0, w), ALU.mult, reads=[k.tmp0_r, zs_r_fn(p0)], writes=[dst_r_fn(p0)])

                k.tmp0_r = Reg()
                k.gst_r = Reg()

                with ExitStack() as ph2:
                    def psb2(name, shape, dt):
                        k.uid += 1
                        return ph2.enter_context(nc.sbuf_tensor(f"p_{name}_{k.uid}", list(shape), dt))
                    upad = psb2("upad", [128, KC, 2, 258], BF16)
                    upad_r = Reg()
                    qkvz = psb2("qkvz", [128, 32, 512], BF16)
                    qkvz_r = regs(32)
                    gstp = psb2("gstp", [128, 4, 32], F32)
                    oaccp = psb2("oaccp", [128, 256], F32)
                    oaccp_r = Reg()
                    DVE.op("memset", upad[:], 0.0, writes=[upad_r])
                    for s2 in range(2):
                        ACT.op("activation", upad[:, :, s2, 1:257], xn[:, :, s2 * 256:(s2 + 1) * 256], AF.Copy,
                               reads=[xn_r[kk][0] for kk in range(KC)], writes=[upad_r])
                    for blk in range(8):
                        s = wnext()
                        wv = v3(slot_view(s), 0, KC, 512)
                        typ = blk // 2
                        for c4 in range(4):
                            hd = (blk % 2) * 4 + c4
                            ci = typ * 8 + hd
                            for s2 in range(2):
                                proj_chunk(wv, c4, s, upad[:, :, s2, :], upad_r, typ,
                                           convp[:, ci, :] if typ < 3 else None,
                                           qkvz[:, ci, s2 * 256:(s2 + 1) * 256], qkvz_r[ci])
                    for s2 in range(2):
                        gbeta(upad[:, :, s2, :], upad_r, wabp, 32, dtbp, negAp, lambda i, s2=s2: gstp[:, 2 * s2 + i, :])
                    if dbg and k.dbgsrc == 'dnu':
                        for kk in range(KC):
                            dma(POOL, O["dbg"][kk * 128:(kk + 1) * 128, 0:512], xn[:, kk, 0:512], osem, reads=[xn_r[kk][0]])
                            dma(POOL, O["dbg"][kk * 128:(kk + 1) * 128, 512:1024], xn[:, kk, 512:1024], osem, reads=[xn_r[kk][1]])
                            dma(SP, O["dbg"][kk * 128:(kk + 1) * 128, 1024:1536], h[:, kk, 0:512], osem, reads=[h_r[kk][0]])
                        for e_ in (ACT, DVE, PE):
                            e_.wait([(osem.s, osem.n)])
                        return
                    if dbg and k.dbgsrc == 'dn':
                        dma(POOL, O["dbg"][0:128, 0:512], qkvz[:, 0, :], osem, reads=[qkvz_r[0]])
                        dma(POOL, O["dbg"][0:128, 512:1024], qkvz[:, 8, :], osem, reads=[qkvz_r[8]])
                        dma(POOL, O["dbg"][0:128, 1024:1536], qkvz[:, 16, :], osem, reads=[qkvz_r[16]])
                        dma(POOL, O["dbg"][128:256, 0:512], qkvz[:, 24, :], osem, reads=[qkvz_r[24]])
                        dma(POOL, O["dbg"][128:256, 512:640], gstp[:].rearrange("p c f -> p (c f)"), osem, reads=[k.gst_r])
                    pending = None
                    jobs = [(s2, hd) for s2 in range(2) for hd in range(8)]
                    for j in range(len(jobs) + 1):
                        gens = []
                        if j < len(jobs):
                            s2, hd = jobs[j]
                            par = j % 2
                            systems = []
                            for g_, (e, c) in enumerate(((0, 0), (0, 1), (1, 1), (1, 0))):
                                sy = Obj()
                                sy.g, sy.par, sy.e = g_, par, e
                                col0 = s2 * 256 + c * 128
                                sy.qT = qkvz[:, 0 * 8 + hd, col0:col0 + 128]
                                sy.kT = qkvz[:, 1 * 8 + hd, col0:col0 + 128]
                                sy.vT = qkvz[:, 2 * 8 + hd, col0:col0 + 128]
                                sy.qkv_r = [qkvz_r[hd], qkvz_r[8 + hd], qkvz_r[16 + hd]]
                                sy.gcol = gstp[:, 2 * s2 + c, e * 8 + hd:e * 8 + hd + 1]
                                sy.bcol = gstp[:, 2 * s2 + c, 16 + e * 8 + hd:16 + e * 8 + hd + 1]
                                sy.g_r = k.gst_r
                                systems.append(sy)
                                gens.append(prep(sy))
                        if pending is not None:
                            gens.extend(pending)
                        run_threads(gens)
                        if pending is not None:
                            (ps2, phd, ppar) = k.pjob
                            dma(SP, O["sfo"][:, ps2 * 8 + phd, :], CH[0].S[:, :], osem, reads=[CH[0].S_r])
                            dma(SP, O["sbo"][:, ps2 * 8 + phd, :], CH[1].S[:, :], osem, reads=[CH[1].S_r])
                            if dbg and k.dbgsrc == 'dn' and (ps2, phd) == (0, 0):
                                dma(POOL, O["dbg"][256:384, 0:256], oaccp[:, :], osem, reads=[oaccp_r])
                                dma(POOL, O["dbg"][384:512, 0:128], CH[0].S[:, :], osem, reads=[CH[0].S_r])
                                for g_ in range(4):
                                    o_ = OS[ppar][g_]
                                    dma(POOL, O["dbg"][512:640, g_ * 128:(g_ + 1) * 128], o_.wT[:, :], osem, reads=[o_.r])
                                    dma(POOL, O["dbg"][640:768, g_ * 128:(g_ + 1) * 128], o_.u[:, :], osem, reads=[o_.r])
                                    dma(POOL, O["dbg"][768:896, g_ * 128:(g_ + 1) * 128], o_.qkT[:, :], osem, reads=[o_.r])
                                    dma(POOL, O["dbg"][896:1024, g_ * 128:(g_ + 1) * 128], o_.kd[:, :], osem, reads=[o_.r])
                                    dma(POOL, O["dbg"][512:640, 512 + g_ * 128:512 + (g_ + 1) * 128], o_.qdT[:, :], osem, reads=[o_.r])
                                    dma(POOL, O["dbg"][640:768, 512 + g_:512 + g_ + 1], o_.glc[:, :], osem, reads=[o_.r], allow_slow_non_contiguous=True)
                                for e_ in (ACT, DVE, PE):
                                    e_.wait([(osem.s, osem.n)])
                            post(oaccp, oaccp_r, lambda p0, w, ps2=ps2, phd=phd: qkvz[:, 24 + phd, ps2 * 256 + p0:ps2 * 256 + p0 + w],
                                 lambda p0, phd=phd: qkvz_r[24 + phd], 256,
                                 lambda p0, w, ps2=ps2, phd=phd: xn[:, phd, ps2 * 256 + p0:ps2 * 256 + p0 + w],
                                 lambda p0, phd=phd: xn_r[phd][0])
                            pending = None
                        if j < len(jobs):
                            for ch in CH:
                                DVE.op("memset", ch.S[:, :], 0.0, writes=[ch.S_r])
                                DVE.op("memset", ch.Sb[:, :], 0.0, writes=[ch.Sb_r])
                            DVE.op("memset", oaccp[:, :], 0.0, writes=[oaccp_r])
                            par = j % 2
                            stf = [(OS[par][0], oaccp[:, 0:128], oaccp_r), (OS[par][1], oaccp[:, 128:256], oaccp_r)]
                            stb = [(OS[par][2], oaccp[:, 128:256], oaccp_r), (OS[par][3], oaccp[:, 0:128], oaccp_r)]
                            pending = [scan(CH[0], stf), scan(CH[1], stb)]
                            k.pjob = (s2, hd, par)
                    for e_ in (ACT, DVE, PE):
                        e_.wait([(osem.s, osem.n)])
                    barrier()

                if k.dsub < 3:
                    return
                with ExitStack() as ph2:
                    def psb2(name, shape, dt):
                        k.uid += 1
                        return ph2.enter_context(nc.sbuf_tensor(f"p_{name}_{k.uid}", list(shape), dt))
                    TT_ = 4096
                    ub = [psb2(f"ub{i}", [128, KC, 258], BF16) for i in range(2)]
                    ub_r = regs(2)
                    qkvzs = psb2("qkvzs", [128, 3, TT_], BF16)
                    qkvzs_r = regs(3)
                    gsts = psb2("gsts", [128, 32, 8], F32)
                    oaccs = psb2("oaccs", [128, TT_], F32)
                    oaccs_r = Reg()
                    s0 = psb2("s0", [128, 2, 2, 128], F32)
                    s0_r = Reg()
                    dma(SP, s0[:, 0, :, :], I["s0f"], dsem, writes=[s0_r])
                    dma(SP, s0[:, 1, :, :], I["s0b"], dsem, writes=[s0_r])
                    s0_r.w = (dsem.s, dsem.n)

                    def load_ublk(tb):
                        i = tb % 2
                        rr, ls = tb // 4, (tb % 4) * 256 - 1
                        lo, hi = max(ls, 0), min(ls + 258, 1024)
                        us = usems[i]
                        SP._deps((), [ub_r[i]], [cct2])
                        SP.wait(BAR["toks"])
                        for x in range(2):
                            tok = dma(SP, ub[i][:, 4 * x:4 * x + 4, lo - ls:hi - ls],
                                      agout2[x].ap()[rr * 512:(rr + 1) * 512, lo:hi].rearrange("(k p) t -> p k t", p=128), us)
                        if ls < 0:
                            if rr > 0:
                                for x in range(2):
                                    tok = dma(SP, ub[i][:, 4 * x:4 * x + 4, 0:1],
                                              agout2[x].ap()[(rr - 1) * 512:rr * 512, 1023:1024].rearrange("(k p) t -> p k t", p=128), us,
                                              allow_slow_non_contiguous=True)
                            else:
                                ub_r[i].w = tok
                                ub_r[i].r = {}
                                tok = DVE.op("memset", ub[i][:, :, 0:1], 0.0, writes=[ub_r[i]])
                                return
                        if ls + 258 > 1024:
                            if rr < 3:
                                for x in range(2):
                                    tok = dma(SP, ub[i][:, 4 * x:4 * x + 4, 257:258],
                                              agout2[x].ap()[(rr + 1) * 512:(rr + 2) * 512, 0:1].rearrange("(k p) t -> p k t", p=128), us,
                                              allow_slow_non_contiguous=True)
                            else:
                                ub_r[i].w = tok
                                ub_r[i].r = {}
                                tok = DVE.op("memset", ub[i][:, :, 257:258], 0.0, writes=[ub_r[i]])
                                return
                        ub_r[i].w = tok
                        ub_r[i].r = {}

                    for hh in range(2):
                        s = wnext()
                        wv = v3(slot_view(s), 0, KC, 512)
                        load_ublk(0)
                        for tb in range(16):
                            if tb + 1 < 16:
                                load_ublk(tb + 1)
                            i = tb % 2
                            for typ in range(4):
                                if typ < 3:
                                    dst_, dst_r_ = qkvzs[:, typ, tb * 256:(tb + 1) * 256], qkvzs_r[typ]
                                else:
                                    zc = 512 + (tb % 4) * 256
                                    dst_, dst_r_ = xn[:, tb // 4, zc:zc + 256], xn_r[tb // 4][1 if (tb % 4) < 2 else 2]
                                proj_chunk(wv, typ, s, ub[i][:, :, :], ub_r[i], typ,
                                           convo[:, hh * 3 + typ, :] if typ < 3 else None, dst_, dst_r_)
                            if hh == 0:
                                gbeta(ub[i][:, :, :], ub_r[i], wabo, 8, dtbo, negAo, lambda ii, tb=tb: gsts[:, 2 * tb + ii, :])
                        DVE.op("memset", oaccs[:, :], 0.0, writes=[oaccs_r])
                        for e in range(2):
                            DVE.op("tensor_copy", CH[e].S[:, :], s0[:, e, hh, :], reads=[s0_r], writes=[CH[e].S_r])
                            ACT.op("activation", CH[e].Sb[:, :], s0[:, e, hh, :], AF.Copy, reads=[s0_r], writes=[CH[e].Sb_r])
                        pending = None
                        NCH = TT_ // 128
                        for gi in range(NCH // 2 + 1):
                            gens = []
                            if gi < NCH // 2:
                                par = gi % 2
                                order = ((0, 2 * gi), (0, 2 * gi + 1), (1, NCH - 1 - 2 * gi), (1, NCH - 2 - 2 * gi))
                                for g_, (e, c) in enumerate(order):
                                    sy = Obj()
                                    sy.g, sy.par, sy.e = g_, par, e
                                    sy.qT = qkvzs[:, 0, c * 128:(c + 1) * 128]
                                    sy.kT = qkvzs[:, 1, c * 128:(c + 1) * 128]
                                    sy.vT = qkvzs[:, 2, c * 128:(c + 1) * 128]
                                    sy.qkv_r = [qkvzs_r[0], qkvzs_r[1], qkvzs_r[2]]
                                    sy.gcol = gsts[:, c, e * 2 + hh:e * 2 + hh + 1]
                                    sy.bcol = gsts[:, c, 4 + e * 2 + hh:4 + e * 2 + hh + 1]
                                    sy.g_r = k.gst_r
                                    gens.append(prep(sy))
                            if pending is not None:
                                gens.extend(pending)
                            run_threads(gens)
                            pending = None
                            if gi < NCH // 2:
                                par = gi % 2
                                order = ((0, 2 * gi), (0, 2 * gi + 1), (1, NCH - 1 - 2 * gi), (1, NCH - 2 - 2 * gi))
                                stf = [(OS[par][g_], oaccs[:, c * 128:(c + 1) * 128], oaccs_r) for g_, (e, c) in enumerate(order) if e == 0]
                                stb = [(OS[par][g_], oaccs[:, c * 128:(c + 1) * 128], oaccs_r) for g_, (e, c) in enumerate(order) if e == 1]
                                pending = [scan(CH[0], stf), scan(CH[1], stb)]
                        def zfn(p0, w):
                            j_ = p0 // 512
                            return xn[:, j_ // 2, 512 + (j_ % 2) * 512:512 + (j_ % 2) * 512 + w]

                        def ofn(p0, w):
                            j_ = p0 // 512
                            return xn[:, 4 + j_ // 2, 512 + (j_ % 2) * 512:512 + (j_ % 2) * 512 + w]
                        post(oaccs, oaccs_r, zfn, lambda p0: xn_r[(p0 // 512) // 2][1 + (p0 // 512) % 2], TT_,
                             ofn, lambda p0: xn_r[4 + (p0 // 512) // 2][1 + (p0 // 512) % 2])
                        for j_ in range(8):
                            tg = dma(SP, agin3[hh].ap()[:, j_ * 512:(j_ + 1) * 512], ofn(j_ * 512, 512), asem,
                                     reads=[xn_r[4 + j_ // 2][1 + j_ % 2]])
                        POOL.wait([tg])
                        nc.gpsimd.collective_compute("AllGather", ALU.bypass, replica_groups=GROUPS,
                                                     ins=[agin3[hh].ap().opt()], outs=[agout3[hh].ap().opt()]).then_inc(ccsem.s, 1)
                        ccsem.n += 1
                    for e_ in (ACT, DVE, PE):
                        e_.wait([tg])
                    cct3 = (ccsem.s, ccsem.n)
                    barrier()

                if k.dsub < 4:
                    return
                with ExitStack() as ph2:
                    def psb2(name, shape, dt):
                        k.uid += 1
                        return ph2.enter_context(nc.sbuf_tensor(f"p_{name}_{k.uid}", list(shape), dt))
                    cand = [psb2(f"cand{i}", [128, 2, 4, 1024], BF16) for i in range(2)]
                    cand_r = regs(2)
                    for rr in range(4):
                        i = rr % 2
                        SP._deps((), [cand_r[i]], [cct3])
                        SP.wait(BAR["toks"])
                        for hh in range(2):
                            tok = dma(SP, cand[i][:, hh, :, :], agout3[hh].ap()[:, rr * 1024:(rr + 1) * 1024].rearrange("(k p) t -> p k t", p=128),
                                      usems[i])
                        cand_r[i].w = tok
                        cand_r[i].r = {}
                        for t in (1, 2):
                            for kk in range(KC):
                                src = cand[i][:, kk % 2, kk // 2, (t - 1) * 512:t * 512]
                                dstv = xn[:, kk, TILES[t][0]:TILES[t][1]]
                                if rr == 0:
                                    DVE.op("tensor_scalar", dstv, src, selt[:, 0:1], None, ALU.mult, reads=[cand_r[i], dn_r], writes=[xn_r[kk][t]])
                                else:
                                    DVE.op("scalar_tensor_tensor", dstv, src, selt[:, rr:rr + 1], dstv, ALU.mult, ALU.add,
                                           reads=[cand_r[i], dn_r], writes=[xn_r[kk][t]])
                    barrier()

                for blk in range(2):
                    s = wnext()
                    wv = v3(slot_view(s), 0, KC, 512)
                    for c4 in range(4):
                        dc = blk * 4 + c4
                        for t, (t0, t1_) in enumerate(TILES):
                            b = next_bank()
                            for kk in range(KC):
                                PE.op("matmul", banks[b][:, :], wv[:, kk, c4 * 128:(c4 + 1) * 128], xn[:, kk, t0:t1_],
                                      start=(kk == 0), stop=(kk == KC - 1), reads=[wring_r[s], xn_r[kk][t]],
                                      writes=[bank_r[b]] if kk == 0 else [], inc=(kk == KC - 1))
                            bank_r[b].w = (PE.sem, PE.cnt)
                            DVE.op("scalar_tensor_tensor", h[:, dc, t0:t1_], banks[b][:, :], coefG[:, l, 1, dc, cj(t):cj(t) + 1],
                                   h[:, dc, t0:t1_], ALU.mult, ALU.add, reads=[bank_r[b], mod_r], writes=[h_r[dc][t]])
                barrier()

        def final_out(dst_dram, normed=True):
            with ExitStack() as ph:
                def psb(name, shape, dt):
                    k.uid += 1
                    return ph.enter_context(nc.sbuf_tensor(f"p_{name}_{k.uid}", list(shape), dt))
                yo = psb("yo", [128, KC, NT], F32)
                yo_r = regs(KC, 3)
                sq = psb("sq", [128, 2, 512], BF16)
                rstd = psb("rstd", [128, 512], F32)
                tmp = psb("tmp", [128, 2, 512], F32)
                pool = (sq, regs(2), rstd, Reg(), tmp, regs(2))
                if normed:
                    modnorm(lambda kk, t: yo[:, kk, TILES[t][0]:TILES[t][1]], lambda kk, t: yo_r[kk][t],
                            lambda kk, t: gains[:, 6, kk:kk + 1], lambda kk, t: None, pool)
                    for kk in range(KC):
                        dma(SP, dst_dram[kk * 128:(kk + 1) * 128, :], yo[:, kk, :], osem, reads=yo_r[kk])
                else:
                    for kk in range(KC):
                        dma(SP, dst_dram[kk * 128:(kk + 1) * 128, :], h[:, kk, :], osem, reads=h_r[kk])
                for e in (SP, ACT, DVE, PE):
                    e.wait([(osem.s, osem.n)])
                barrier()

        do_mods_all()
        ffn(0, 0)
        if stage >= 2:
            attention(0)
        if stage >= 3:
            ffn(0, 1)
        if stage >= 4:
            ffn(1, 0)
        if stage >= 5:
            deltanet(1)
        if stage >= 6:
            ffn(1, 1)
        if dbg and k.dbgsrc == 'mod':
            dma(SP, O["dbg"][0:128, 0:288], modT[:].rearrange("p l c j -> p (l c j)"), osem, reads=[mod_r])
            dma(SP, O["dbg"][0:128, 288:384], coefA[:].rearrange("p l s k j -> p (l s k j)"), osem, reads=[mod_r])
            dma(SP, O["dbg"][0:128, 384:480], coefG[:].rearrange("p l s k j -> p (l s k j)"), osem, reads=[mod_r])
            for e_ in (ACT, DVE, PE):
                e_.wait([(osem.s, osem.n)])
        if dbg and k.dbgsrc == 'h':
            final_out(O["dbg"], normed=False)
        final_out(O["yT"], normed=True)
        assert k.w_used == len(plan), (k.w_used, len(plan))
        SP.wait([(osem.s, osem.n)])
    return nc


def _fm(v):
    return np.ascontiguousarray(v.reshape(KC, 128).T)


def make_inputs(core, inp):
    b, r = core // 4, core % 4
    f32 = np.float32
    xp = inp["x_prompt"][2 * core:2 * core + 2].reshape(512, D)
    xs = inp["x_sample"][b, r * 1024:(r + 1) * 1024]
    m = {}
    m["xT"] = np.ascontiguousarray(np.concatenate([xp, xs], axis=0).T)
    cond = np.stack([inp["c_ctx"], inp["c"][b]], axis=-1)
    m["condT"] = np.ascontiguousarray(cond.reshape(KC, 128, 2).transpose(1, 0, 2))
    m["ada_w"] = np.ascontiguousarray(inp["ada_w"][:, :, r * 2304:(r + 1) * 2304])
    m["adabT"] = np.ascontiguousarray(inp["ada_b"].reshape(2, 72, 128).transpose(0, 2, 1)[:, :, r * 18:(r + 1) * 18])
    gl = []
    for l in range(2):
        for nm in ("norm_ffn1", "norm_mix", "norm_ffn2"):
            gl.append(_fm(inp[nm][l]))
    gl.append(_fm(inp["final_norm"]))
    m["gainsT"] = np.ascontiguousarray(np.stack(gl, axis=1))
    m["ffn_w_in"] = np.ascontiguousarray(np.stack([inp["ffn1_w_in"][0], inp["ffn2_w_in"][0], inp["ffn1_w_in"][1], inp["ffn2_w_in"][1]]))
    m["ffn_w_out"] = np.ascontiguousarray(np.stack([inp["ffn1_w_out"][0], inp["ffn2_w_out"][0], inp["ffn1_w_out"][1], inp["ffn2_w_out"][1]]))
    m["ident"] = np.eye(128, dtype=f32)
    m["attn_w_qkv"] = inp["attn_w_qkv"][0]
    m["attn_w_o"] = inp["attn_w_o"][0]
    m["qkg"] = np.ascontiguousarray(np.stack([inp["attn_q_norm"][0], inp["attn_k_norm"][0]], axis=1))
    m["ckT"] = np.ascontiguousarray(inp["cache_k"][b, 0].transpose(2, 1, 0))
    m["cv"] = np.ascontiguousarray(inp["cache_v"][b, 0].reshape(4, 128, 256).transpose(1, 0, 2))
    C, S, Rm = rope_consts(r)
    m["ropeC"], m["ropeS"], m["rmat"] = C, S, Rm
    hs_ = [2 * r, 2 * r + 1]
    w_in = inp["dn_w_in"][0]
    m["dn_w_in"] = w_in
    m["dn_w_in_own"] = np.ascontiguousarray(np.concatenate(
        [w_in[:, ty * 1024 + hh * 128: ty * 1024 + (hh + 1) * 128] for hh in hs_ for ty in range(4)], axis=1))
    convT = np.ascontiguousarray(inp["dn_conv"][0].reshape(3, 24, 128).transpose(2, 1, 0))
    m["convT"] = convT
    m["convT_own"] = np.ascontiguousarray(np.stack([convT[:, ty * 8 + hh, :] for hh in hs_ for ty in range(3)], axis=1))
    wa, wb = inp["dn_w_a"][0], inp["dn_w_b"][0]
    m["wab_p"] = np.ascontiguousarray(np.concatenate([wa[0], wa[1], wb[0], wb[1]], axis=1))
    m["wab_o"] = np.ascontiguousarray(np.stack([wa[0][:, hs_[0]], wa[0][:, hs_[1]], wa[1][:, hs_[0]], wa[1][:, hs_[1]],
                                                wb[0][:, hs_[0]], wb[0][:, hs_[1]], wb[1][:, hs_[0]], wb[1][:, hs_[1]]], axis=1))
    dtb, alog = inp["dn_dt_bias"][0], inp["dn_a_log"][0]
    m["dtb_p"] = np.ascontiguousarray(dtb.reshape(16))
    m["alog_p"] = np.ascontiguousarray(alog.reshape(16))
    m["dtb_o"] = np.ascontiguousarray(np.array([dtb[0, hs_[0]], dtb[0, hs_[1]], dtb[1, hs_[0]], dtb[1, hs_[1]]], f32))
    m["alog_o"] = np.ascontiguousarray(np.array([alog[0, hs_[0]], alog[0, hs_[1]], alog[1, hs_[0]], alog[1, hs_[1]]], f32))
    m["ong"] = np.ascontiguousarray(inp["dn_out_norm"][0].reshape(128, 1))
    m["dn_w_o"] = inp["dn_w_o"][0]
    m["s0f"] = np.ascontiguousarray(inp["state_fwd"][b, 0, hs_].transpose(1, 0, 2))
    m["s0b"] = np.ascontiguousarray(inp["state_bwd"][b, 0, hs_].transpose(1, 0, 2))
    sel = np.zeros(4, f32)
    sel[r] = 1.0
    m["sel"] = sel
    p_ = np.arange(128)[:, None]
    j_ = np.arange(128)[None, :]
    m["masks"] = np.ascontiguousarray(np.stack([p_ <= j_, p_ >= j_, p_ > j_, p_ < j_], axis=1).astype(f32))
    lm = []
    for li in range(7):
        mm_ = 1 << li
        lm.append((p_ // (2 * mm_) == j_ // (2 * mm_)) & (p_ % (2 * mm_) >= mm_) & (j_ % (2 * mm_) < mm_))
    m["lmask"] = np.ascontiguousarray(np.stack(lm, axis=1).astype(f32))
    return m


def rope_consts(r):
    pos = np.arange(r * 1024, (r + 1) * 1024)
    row = (pos // 64).astype(np.float32)
    col = (pos % 64).astype(np.float32)
    nf = 32
    inv = (np.float32(10000.0) ** (-np.arange(nf, dtype=np.float32) / nf)).astype(np.float32)
    d = np.arange(128)
    axis = d // 64
    f = d % 32
    p = np.where(axis[:, None] == 0, row[None, :], col[None, :]).astype(np.float32)
    ang = (p * inv[f][:, None]).astype(np.float32)
    C = np.cos(ang).astype(np.float32)
    S = np.sin(ang).astype(np.float32)
    Rm = np.zeros((128, 128), np.float32)
    for m_ in range(128):
        if (m_ % 64) < 32:
            Rm[m_ + 32, m_] = -1.0
        else:
            Rm[m_ - 32, m_] = 1.0
    return C, S, Rm


_CACHE = {}


SHARED = ("ffn_w_in", "ffn_w_out", "ident", "attn_w_qkv", "attn_w_o", "dn_w_in", "dn_w_o", "convT", "wab_p",
          "dtb_p", "alog_p", "ong", "masks", "lmask", "gainsT", "qkg", "rmat")


def make_all_inputs(inp):
    in_maps = []
    for c in range(8):
        m = make_inputs(c, inp)
        if in_maps:
            for kk in SHARED:
                m[kk] = in_maps[0][kk]
        in_maps.append(m)
    return in_maps


def assemble(results):
    f32 = np.float32
    y_p = np.zeros((16, 256, D), f32)
    y_s = np.zeros((2, 4096, D), f32)
    nk = np.zeros((16, 1, 256, 2, 128), f32)
    nv = np.zeros((16, 1, 256, 2, 128), f32)
    sf = np.zeros((16, 1, 8, 128, 128), f32)
    sb = np.zeros((16, 1, 8, 128, 128), f32)
    for c in range(8):
        b, r = c // 4, c % 4
        res = results[c]
        y = np.asarray(res["yT"]).T
        y_p[2 * c:2 * c + 2] = y[:512].reshape(2, 256, D)
        y_s[b, r * 1024:(r + 1) * 1024] = y[512:]
        ko = np.asarray(res["kout"])
        vo = np.asarray(res["vout"]).transpose(1, 0, 2).reshape(512, 2, 128)
        for s2 in range(2):
            nk[2 * c + s2, 0] = ko[:, :, s2 * 256:(s2 + 1) * 256].transpose(2, 1, 0)
            nv[2 * c + s2, 0] = vo[s2 * 256:(s2 + 1) * 256]
        sf[2 * c:2 * c + 2, 0] = np.asarray(res["sfo"]).transpose(1, 0, 2).reshape(2, 8, 128, 128)
        sb[2 * c:2 * c + 2, 0] = np.asarray(res["sbo"]).transpose(1, 0, 2).reshape(2, 8, 128, 128)
    return (y_p, y_s, nk, nv, sf, sb)


def kernel(**inputs):
    inp = {k_: np.asarray(v) for k_, v in inputs.items()}
    nc = build_program()
    in_maps = make_all_inputs(inp)
    res = run_bass_kernel_spmd(nc, in_maps, core_ids=list(range(8)))
    return assemble(res.results)
```

```python
import numpy as np
from contextlib import ExitStack
import concourse.bass as bass
import concourse.mybir as mybir
from concourse.bass_utils import run_bass_kernel_spmd

F32 = mybir.dt.float32
BF16 = mybir.dt.bfloat16
AF = mybir.ActivationFunctionType
ALU = mybir.AluOpType

D = 1024
KC = 8
NT = 1536
TILES = [(0, 512), (512, 1024), (1024, 1536)]
DFF = 2816
FC = 22
EPS = 1e-6
GROUPS = [[0, 1, 2, 3], [4, 5, 6, 7]]
SLOT = 4096
NSLOT = 3


class Reg:
    __slots__ = ("w", "r", "excl")

    def __init__(self, excl=False):
        self.w = None
        self.r = {}
        self.excl = excl


def regs(*shape):
    if len(shape) == 1:
        return [Reg() for _ in range(shape[0])]
    return [regs(*shape[1:]) for _ in range(shape[0])]


class Eng:
    def __init__(self, h, sem):
        self.h = h
        self.sem = sem
        self.cnt = 0
        self.waited = {}

    def wait(self, toks):
        best = {}
        for t in toks:
            if t is None:
                continue
            if id(t[0]) not in best or best[id(t[0])][1] < t[1]:
                best[id(t[0])] = t
        for t in best.values():
            sem, val = t
            k = id(sem)
            if sem is self.sem and val > self.cnt:
                continue
            if self.waited.get(k, 0) < val:
                self.h.wait_ge(sem, val)
                self.waited[k] = val

    def _deps(self, reads, writes, deps):
        toks = list(deps)
        for R in reads:
            toks.append(R.w)
            if R.excl:
                toks.extend(R.r.values())
        for R in writes:
            toks.append(R.w)
            toks.extend(R.r.values())
        self.wait(toks)

    @staticmethod
    def _upd(tok, reads, writes):
        for R in reads:
            if R.excl:
                R.w = tok
                R.r = {}
                continue
            k = id(tok[0])
            if k not in R.r or R.r[k][1] < tok[1]:
                R.r[k] = tok
        for R in writes:
            R.w = tok
            R.r = {}

    def op(self, name, *a, reads=(), writes=(), deps=(), inc=True, **kw):
        self._deps(reads, writes, deps)
        ins = getattr(self.h, name)(*a, **kw)
        if inc:
            self.cnt += 1
            ins.then_inc(self.sem, 1)
            tok = (self.sem, self.cnt)
        else:
            tok = (self.sem, self.cnt + 1)
        self._upd(tok, reads, writes)
        return tok


class DSem:
    def __init__(self, s):
        self.s = s
        self.n = 0


BAR = {"toks": []}


def dma(q, out, in_, ds, reads=(), writes=(), deps=(), phase=True, **kw):
    if phase:
        q.wait(BAR["toks"])
    q._deps(reads, writes, deps)
    q.h.dma_start(out=out, in_=in_, **kw).then_inc(ds.s, 16)
    ds.n += 16
    tok = (ds.s, ds.n)
    Eng._upd(tok, reads, writes)
    return tok


class K:
    pass


def build_program(stage=99, dbg=False, sub=9, dbgsrc='h', dsub=9):
    nc = bass.Bass("TRN2", target_bir_lowering=False)
    BAR["toks"] = []
    k = K()
    k.nc = nc
    k.stage = stage
    k.uid = 0
    k.sub = sub
    k.dsub = dsub
    k.dbgsrc = dbgsrc

    def din(name, shape, dt=F32):
        return nc.dram_tensor(name, list(shape), dt, kind="ExternalInput").ap()

    def dout(name, shape, dt=F32):
        return nc.dram_tensor(name, list(shape), dt, kind="ExternalOutput").ap()

    I = {}
    I["xT"] = din("xT", [D, NT])
    I["condT"] = din("condT", [128, KC, 2])
    I["ada_w"] = din("ada_w", [2, D, 2304])
    I["adabT"] = din("adabT", [2, 128, 18])
    I["gainsT"] = din("gainsT", [128, 7, KC])
    I["ffn_w_in"] = din("ffn_w_in", [4, D, 2 * DFF])
    I["ffn_w_out"] = din("ffn_w_out", [4, DFF, D])
    I["ident"] = din("ident", [128, 128])
    I["attn_w_qkv"] = din("attn_w_qkv", [D, 1536])
    I["attn_w_o"] = din("attn_w_o", [D, D])
    I["qkg"] = din("qkg", [128, 2])
    I["ckT"] = din("ckT", [128, 2, 512])
    I["cv"] = din("cv", [128, 4, 256])
    I["ropeC"] = din("ropeC", [128, 1024])
    I["ropeS"] = din("ropeS", [128, 1024])
    I["rmat"] = din("rmat", [128, 128])
    I["dn_w_in"] = din("dn_w_in", [D, 4096])
    I["dn_w_in_own"] = din("dn_w_in_own", [D, 1024])
    I["convT"] = din("convT", [128, 24, 3])
    I["convT_own"] = din("convT_own", [128, 6, 3])
    I["wab_p"] = din("wab_p", [D, 32])
    I["wab_o"] = din("wab_o", [D, 8])
    I["dtb_p"] = din("dtb_p", [16])
    I["alog_p"] = din("alog_p", [16])
    I["dtb_o"] = din("dtb_o", [4])
    I["alog_o"] = din("alog_o", [4])
    I["ong"] = din("ong", [128, 1])
    I["dn_w_o"] = din("dn_w_o", [D, D])
    I["s0f"] = din("s0f", [128, 2, 128])
    I["s0b"] = din("s0b", [128, 2, 128])
    I["sel"] = din("sel", [4])
    I["masks"] = din("masks", [128, 4, 128])
    I["lmask"] = din("lmask", [128, 7, 128])
    O = {}
    O["yT"] = dout("yT", [D, NT])
    O["kout"] = dout("kout", [128, 2, 512])
    O["vout"] = dout("vout", [128, 4, 256])
    O["sfo"] = dout("sfo", [128, 16, 128])
    O["sbo"] = dout("sbo", [128, 16, 128])
    agin2 = [nc.dram_tensor(f"agin2_{x}", [512, 1024], BF16) for x in range(2)]
    agout2 = [nc.dram_tensor(f"agout2_{x}", [2048, 1024], BF16) for x in range(2)]
    agin3 = [nc.dram_tensor(f"agin3_{x}", [128, 4096], BF16) for x in range(2)]
    agout3 = [nc.dram_tensor(f"agout3_{x}", [512, 4096], BF16) for x in range(2)]
    agin1 = nc.dram_tensor("agin1", [256, 2048], BF16)
    agin0 = nc.dram_tensor("agin0", [128, 72], F32)
    agout0 = nc.dram_tensor("agout0", [512, 72], F32)
    agout1 = nc.dram_tensor("agout1", [1024, 2048], BF16)
    if dbg:
        O["dbg"] = dout("dbg", [D, NT])
    k.I, k.O = I, O

    es = ExitStack()
    with es:
        def sb(name, shape, dt):
            return es.enter_context(nc.sbuf_tensor("t_" + name, list(shape), dt))

        def ps(name, shape, dt):
            return es.enter_context(nc.psum_tensor(name, list(shape), dt))

        def sem(name):
            return es.enter_context(nc.semaphore(name))

        PE = Eng(nc.tensor, sem("s_pe"))
        ACT = Eng(nc.scalar, sem("s_act"))
        DVE = Eng(nc.vector, sem("s_dve"))
        POOL = Eng(nc.gpsimd, sem("s_pool"))
        SP = Eng(nc.sync, sem("s_sp"))
        k.PE, k.ACT, k.DVE, k.POOL, k.SP = PE, ACT, DVE, POOL, SP
        csem = DSem(sem("csem"))
        osem = DSem(sem("osem"))
        wsems = [DSem(sem(f"wsem{i}")) for i in range(NSLOT)]
        asem = DSem(sem("asem"))
        asem2 = DSem(sem("asem2"))
        ccsem = DSem(sem("ccsem"))
        dsem = DSem(sem("dsem"))
        usems = [DSem(sem(f"usem{i}")) for i in range(3)]
        k.dsems = [csem, osem, asem, asem2, ccsem, dsem] + usems

        def barrier():
            engs = [PE, ACT, DVE, POOL]
            dtoks = [(d_.s, d_.n) for d_ in k.dsems if d_.n > 0]
            for e in (PE, ACT, DVE):
                e.wait([(f.sem, f.cnt) for f in engs if f is not e and f.cnt > 0] + dtoks)
            BAR["toks"] = [(f.sem, f.cnt) for f in engs if f.cnt > 0] + dtoks

        h = sb("h", [128, KC, NT], F32)
        h_r = regs(KC, 3)
        xn = sb("xn", [128, KC, NT], BF16)
        xn_r = regs(KC, 3)
        wring = sb("wring", [128, NSLOT, SLOT], BF16)
        wring_r = regs(NSLOT)
        ones_bf = sb("ones_bf", [128, 128], BF16)
        ones_f = sb("ones_f", [128, 128], F32)
        ident_f = sb("ident_f", [128, 128], F32)
        ident_bf = sb("ident_bf", [128, 128], BF16)
        epsc = sb("epsc", [128, 1], F32)
        onec = sb("onec", [128, 1], F32)
        condT = sb("condT", [128, KC, 2], F32)
        scT = sb("scT", [128, KC, 2], BF16)
        adab = sb("adab", [128, 2, 18], F32)
        modloc = sb("modloc", [128, 2, 18, 2], F32)
        gains = sb("gains", [128, 7, KC], F32)
        modT = sb("modT", [128, 2, 72, 2], F32)
        coefA = sb("coefA", [128, 2, 3, KC, 2], F32)
        coefG = sb("coefG", [128, 2, 3, KC, 2], F32)
        qkg = sb("qkg", [128, 2], F32)
        rmat_f = sb("rmat_f", [128, 128], F32)
        rmat_bf = sb("rmat_bf", [128, 128], BF16)
        const_r = Reg()
        mod_r = Reg()
        k.coef_r = Reg()
        banks = [ps(f"bank{i}", [128, 512], F32) for i in range(8)]
        bank_r = [Reg(excl=True) for _ in range(8)]
        k.bank_i = 0

        def next_bank():
            b = k.bank_i
            k.bank_i = (b + 1) % 6
            return b

        DVE.op("memset", ones_bf[:], 1.0, writes=[const_r])
        DVE.op("memset", ones_f[:], 1.0, writes=[const_r])
        DVE.op("memset", epsc[:], EPS, writes=[const_r])
        DVE.op("memset", onec[:], 1.0, writes=[const_r])
        dma(SP, ident_f[:], I["ident"], csem)
        dma(SP, condT[:], I["condT"], csem)
        dma(SP, adab[:], I["adabT"].rearrange("l p c -> p l c"), csem)
        dma(SP, gains[:], I["gainsT"], csem)
        dma(SP, qkg[:], I["qkg"], csem)
        dma(SP, rmat_f[:], I["rmat"], csem)
        for kk in range(KC):
            dma(SP, h[:, kk, :], I["xT"][kk * 128:(kk + 1) * 128, :], csem)
        ctok = (csem.s, csem.n)
        const_r.w = None
        DVE.op("tensor_copy", ident_bf[:], ident_f[:], deps=[ctok], writes=[const_r])
        DVE.op("tensor_copy", rmat_bf[:], rmat_f[:], deps=[ctok], writes=[const_r])
        ACT.op("activation", scT[:], condT[:], AF.Silu, deps=[ctok], writes=[const_r])
        for kk in range(KC):
            for t in range(3):
                h_r[kk][t].w = ctok

        plan = []
        k.w_issued = 0
        k.w_used = 0
        slot_use_tok = [None] * NSLOT

        def slot_view(s):
            return wring[:, s, :]

        def issue_next():
            n = k.w_issued
            if n >= len(plan):
                return
            s = n % NSLOT
            POOL._deps((), [wring_r[s]], ())
            for (dst_fn, src) in plan[n]:
                tok = dma(POOL, dst_fn(slot_view(s)), src, wsems[s], phase=False)
            wring_r[s].w = tok
            wring_r[s].r = {}
            k.w_issued += 1

        def wnext():
            n = k.w_used
            while k.w_issued < min(len(plan), n + NSLOT):
                issue_next()
            k.w_used += 1
            return n % NSLOT

        def v3(slot_ap, off, a, c):
            return slot_ap[:, off:off + a * c].rearrange("p (a c) -> p a c", a=a)

        def plan_mods(l):
            for blk in range(6):
                src = I["ada_w"][l, :, blk * 384:(blk + 1) * 384].rearrange("(k p) c -> p k c", p=128)
                plan.append([(lambda sl: v3(sl, 0, KC, 384), src)])

        def plan_ffn(li):
            for j in range(11):
                sa = I["ffn_w_in"][li, :, j * 256:(j + 1) * 256].rearrange("(k p) c -> p k c", p=128)
                sb_ = I["ffn_w_in"][li, :, DFF + j * 256:DFF + (j + 1) * 256].rearrange("(k p) c -> p k c", p=128)
                plan.append([(lambda sl: v3(sl, 0, KC, 256), sa), (lambda sl: v3(sl, 2048, KC, 256), sb_)])
            for dc in range(8):
                so = I["ffn_w_out"][li, :, dc * 128:(dc + 1) * 128].rearrange("(f p) c -> p f c", p=128)
                plan.append([(lambda sl: v3(sl, 0, FC, 128), so)])

        def plan_attn():
            for rep in range(2 if k.sub >= 2 else 1):
                for blk in range(3):
                    src = I["attn_w_qkv"][:, blk * 512:(blk + 1) * 512].rearrange("(k p) c -> p k c", p=128)
                    plan.append([(lambda sl: v3(sl, 0, KC, 512), src)])
            for blk in range(2 if k.sub >= 5 else 0):
                src = I["attn_w_o"][:, blk * 512:(blk + 1) * 512].rearrange("(k p) c -> p k c", p=128)
                plan.append([(lambda sl: v3(sl, 0, KC, 512), src)])

        def plan_dn():
            for blk in range(8 if k.dsub >= 2 else 0):
                src = I["dn_w_in"][:, blk * 512:(blk + 1) * 512].rearrange("(k p) c -> p k c", p=128)
                plan.append([(lambda sl: v3(sl, 0, KC, 512), src)])
            for hh in range(2 if k.dsub >= 3 else 0):
                src = I["dn_w_in_own"][:, hh * 512:(hh + 1) * 512].rearrange("(k p) c -> p k c", p=128)
                plan.append([(lambda sl: v3(sl, 0, KC, 512), src)])
            for blk in range(2 if k.dsub >= 4 else 0):
                src = I["dn_w_o"][:, blk * 512:(blk + 1) * 512].rearrange("(k p) c -> p k c", p=128)
                plan.append([(lambda sl: v3(sl, 0, KC, 512), src)])

        plan_mods(0)
        plan_mods(1)
        plan_ffn(0)
        if stage >= 2:
            plan_attn()
        if stage >= 3:
            plan_ffn(1)
        if stage >= 4:
            plan_ffn(2)
        if stage >= 5:
            plan_dn()
        if stage >= 6:
            plan_ffn(3)

        def do_mods_all():
            b = next_bank()
            for l in range(2):
                for blk in range(6):
                    s = wnext()
                    wv = v3(slot_view(s), 0, KC, 384)
                    for c3 in range(3):
                        cc = l * 18 + blk * 3 + c3
                        for kk in range(KC):
                            PE.op("matmul", banks[b][:, cc * 2:cc * 2 + 2], wv[:, kk, c3 * 128:(c3 + 1) * 128], scT[:, kk, :],
                                  start=(kk == 0), stop=(kk == KC - 1),
                                  reads=[wring_r[s], const_r], writes=[bank_r[b]] if (kk == 0 and cc == 0) else [],
                                  inc=(kk == KC - 1 and c3 == 2))
            bank_r[b].w = (PE.sem, PE.cnt)
            pv = banks[b][:, 0:72].rearrange("p (l c j) -> p l c j", l=2, j=2)
            for j in range(2):
                DVE.op("tensor_tensor", modloc[:, :, :, j], pv[:, :, :, j], adab[:, :, :], ALU.add,
                       reads=[bank_r[b], const_r], writes=[mod_r])
            t0_ = dma(SP, agin0.ap(), modloc[:].rearrange("p l c j -> p (l c j)"), asem, reads=[mod_r])
            POOL.wait([t0_])
            nc.gpsimd.collective_compute("AllGather", ALU.bypass, replica_groups=GROUPS,
                                         ins=[agin0.ap().opt()], outs=[agout0.ap().opt()]).then_inc(ccsem.s, 1)
            ccsem.n += 1
            cct0 = (ccsem.s, ccsem.n)
            SP._deps((), [mod_r], [cct0])
            for rr in range(4):
                tok = dma(SP, modT[:, :, 18 * rr:18 * rr + 18, :],
                          agout0.ap()[rr * 128:(rr + 1) * 128, :].rearrange("p (l c j) -> p l c j", l=2, j=2), asem)
            mod_r.w = tok
            mod_r.r = {}
            for l in range(2):
                for s3 in range(3):
                    for j in range(2):
                        DVE.op("scalar_tensor_tensor", coefA[:, l, s3, :, j], modT[:, l, (3 * s3 + 1) * 8:(3 * s3 + 2) * 8, j],
                               1.0, gains[:, l * 3 + s3, :], ALU.add, ALU.mult, reads=[const_r, mod_r], writes=[k.coef_r])
                    DVE.op("tensor_scalar", coefG[:, l, s3, :, :], modT[:, l, (3 * s3 + 2) * 8:(3 * s3 + 3) * 8, :],
                           (1.0 if s3 == 1 else 0.5), None, ALU.mult, reads=[mod_r], writes=[k.coef_r])
            mod_r.w = (DVE.sem, DVE.cnt)

        def modnorm(dst, dst_r, coefA_fn, coefB_fn, tmp_pool):
            sq, sq_r, rstd, rstd_r, tmp, tmp_r = tmp_pool
            for t, (t0, t1) in enumerate(TILES):
                b = next_bank()
                for kk in range(KC):
                    i = kk % 2
                    ACT.op("activation", sq[:, i, :], h[:, kk, t0:t1], AF.Square, reads=[h_r[kk][t]], writes=[sq_r[i]])
                    PE.op("matmul", banks[b][:, :], ones_bf[:], sq[:, i, :], start=(kk == 0), stop=(kk == KC - 1),
                          reads=[sq_r[i], const_r], writes=[bank_r[b]] if kk == 0 else [], inc=True)
                bank_r[b].w = (PE.sem, PE.cnt)
                ACT.op("activation", rstd[:, :], banks[b][:, :], AF.Ln, bias=epsc[:, 0:1], scale=1.0 / D,
                       reads=[bank_r[b], const_r], writes=[rstd_r])
                ACT.op("activation", rstd[:, :], rstd[:, :], AF.Exp, scale=-0.5, reads=[rstd_r], writes=[rstd_r])
                for kk in range(KC):
                    i = kk % 2
                    A = coefA_fn(kk, t)
                    B = coefB_fn(kk, t)
                    if B is None:
                        DVE.op("scalar_tensor_tensor", dst(kk, t), h[:, kk, t0:t1], A, rstd[:, :], ALU.mult, ALU.mult,
                               reads=[h_r[kk][t], rstd_r, mod_r, const_r], writes=[dst_r(kk, t)])
                    else:
                        DVE.op("scalar_tensor_tensor", tmp[:, i, :], h[:, kk, t0:t1], A, rstd[:, :], ALU.mult, ALU.mult,
                               reads=[h_r[kk][t], rstd_r, mod_r], writes=[tmp_r[i]])
                        DVE.op("tensor_scalar", dst(kk, t), tmp[:, i, :], B, None, ALU.add,
                               reads=[tmp_r[i], mod_r], writes=[dst_r(kk, t)])

        def cj(t):
            return 0 if t == 0 else 1

        def ffn(l, which):
            li = l * 2 + which
            s3 = 0 if which == 0 else 2
            with ExitStack() as ph:
                def psb(name, shape, dt):
                    k.uid += 1
                    return ph.enter_context(nc.sbuf_tensor(f"p_{name}_{k.uid}", list(shape), dt))
                g = psb("g", [128, FC, NT], BF16)
                g_r = regs(FC, 3)
                sq = psb("sq", [128, 2, 512], BF16)
                rstd = psb("rstd", [128, 512], F32)
                tmp = psb("tmp", [128, 2, 512], F32)
                sa = psb("sa", [128, 2, 512], F32)
                sa_r = regs(2)
                pool = (sq, regs(2), rstd, Reg(), tmp, regs(2))
                modnorm(lambda kk, t: xn[:, kk, TILES[t][0]:TILES[t][1]], lambda kk, t: xn_r[kk][t],
                        lambda kk, t: coefA[:, l, s3, kk, cj(t):cj(t) + 1],
                        lambda kk, t: modT[:, l, (3 * s3) * 8 + kk, cj(t):cj(t) + 1], pool)
                n = 0
                for j in range(11):
                    s = wnext()
                    wa = v3(slot_view(s), 0, KC, 256)
                    wb = v3(slot_view(s), 2048, KC, 256)
                    for fl in range(2):
                        f = 2 * j + fl
                        for t, (t0, t1) in enumerate(TILES):
                            ba = next_bank()
                            for kk in range(KC):
                                PE.op("matmul", banks[ba][:, :], wa[:, kk, fl * 128:(fl + 1) * 128], xn[:, kk, t0:t1],
                                      start=(kk == 0), stop=(kk == KC - 1), reads=[wring_r[s], xn_r[kk][t]],
                                      writes=[bank_r[ba]] if kk == 0 else [], inc=(kk == KC - 1))
                            bank_r[ba].w = (PE.sem, PE.cnt)
                            bb = next_bank()
                            for kk in range(KC):
                                PE.op("matmul", banks[bb][:, :], wb[:, kk, fl * 128:(fl + 1) * 128], xn[:, kk, t0:t1],
                                      start=(kk == 0), stop=(kk == KC - 1), reads=[wring_r[s], xn_r[kk][t]],
                                      writes=[bank_r[bb]] if kk == 0 else [], inc=(kk == KC - 1))
                            bank_r[bb].w = (PE.sem, PE.cnt)
                            i = n % 2
                            n += 1
                            ACT.op("activation", sa[:, i, :], banks[ba][:, :], AF.Silu, reads=[bank_r[ba]], writes=[sa_r[i]])
                            DVE.op("tensor_tensor", g[:, f, t0:t1], sa[:, i, :], banks[bb][:, :], ALU.mult,
                                   reads=[sa_r[i], bank_r[bb]], writes=[g_r[f][t]])
                for dc in range(8):
                    s = wnext()
                    wo = v3(slot_view(s), 0, FC, 128)
                    if True:
                        for t, (t0, t1) in enumerate(TILES):
                            b = next_bank()
                            for f in range(FC):
                                PE.op("matmul", banks[b][:, :], wo[:, f, :], g[:, f, t0:t1],
                                      start=(f == 0), stop=(f == FC - 1), reads=[wring_r[s], g_r[f][t]],
                                      writes=[bank_r[b]] if f == 0 else [], inc=(f == FC - 1))
                            bank_r[b].w = (PE.sem, PE.cnt)
                            DVE.op("scalar_tensor_tensor", h[:, dc, t0:t1], banks[b][:, :], coefG[:, l, s3, dc, cj(t):cj(t) + 1],
                                   h[:, dc, t0:t1], ALU.mult, ALU.add, reads=[bank_r[b], mod_r], writes=[h_r[dc][t]])
                barrier()


        SM_SCALE = 128.0 ** -0.5

        def attend(q_ap, q_regs, nq, chunks, out_ap, out_reg, P, P_r, rec, rec_r):
            n = len(chunks)
            LA = 3
            st = {}

            def emit_s(i):
                KT, kreg, V, vreg = chunks[i]
                b = next_bank()
                PE.op("matmul", banks[b][:, 0:nq], KT, q_ap, start=True, stop=True,
                      reads=[kreg] + q_regs, writes=[bank_r[b]])
                pi = k.p_i % 6
                k.p_i += 1
                ACT.op("activation", P[:, pi, 0:nq], banks[b][:, 0:nq], AF.Exp, scale=SM_SCALE,
                       reads=[bank_r[b]], writes=[P_r[pi]])
                st[i] = pi

            def emit_pv(i):
                KT, kreg, V, vreg = chunks[i]
                pi = st[i]
                PE.op("matmul", banks[6][:, 0:nq], V, P[:, pi, 0:nq], start=(i == 0), stop=(i == n - 1),
                      reads=[P_r[pi], vreg], writes=[bank_r[6]] if i == 0 else [], inc=False)
                PE.op("matmul", banks[7][:, 0:nq], ones_bf[:], P[:, pi, 0:nq], start=(i == 0), stop=(i == n - 1),
                      reads=[P_r[pi], const_r], writes=[bank_r[7]] if i == 0 else [], inc=True)

            for i in range(min(LA, n)):
                emit_s(i)
            for i in range(n):
                emit_pv(i)
                if i + LA < n:
                    emit_s(i + LA)
            bank_r[6].w = (PE.sem, PE.cnt)
            bank_r[7].w = (PE.sem, PE.cnt)
            DVE.op("reciprocal", rec[:, 0:nq], banks[7][:, 0:nq], reads=[bank_r[7]], writes=[rec_r])
            DVE.op("tensor_tensor", out_ap, banks[6][:, 0:nq], rec[:, 0:nq], ALU.mult,
                   reads=[bank_r[6], rec_r], writes=[out_reg])

        def attention(l):
            k.p_i = 0
            with ExitStack() as ph:
                def psb(name, shape, dt):
                    k.uid += 1
                    return ph.enter_context(nc.sbuf_tensor(f"p_{name}_{k.uid}", list(shape), dt))
                sq = psb("sq", [128, 2, 512], BF16)
                sq_r = regs(2)
                rstd = psb("rstd", [128, 512], F32)
                rstd_r = Reg()
                tmp = psb("tmp", [128, 2, 512], F32)
                pool = (sq, sq_r, rstd, rstd_r, tmp, regs(2))
                modnorm(lambda kk, t: xn[:, kk, TILES[t][0]:TILES[t][1]], lambda kk, t: xn_r[kk][t],
                        lambda kk, t: coefA[:, l, 1, kk, cj(t):cj(t) + 1],
                        lambda kk, t: modT[:, l, 3 * 8 + kk, cj(t):cj(t) + 1], pool)
                P = psb("P", [128, 6, 512], BF16)
                P_r = regs(6)
                rec = psb("rec", [128, 512], F32)
                rec_r = Reg()
                rs2 = psb("rs2", [128, 512], F32)
                rs2_r = Reg()

                def headnorm(b, gcol):
                    ACT.op("activation", sq[:, 0, :], banks[b][:, :], AF.Square, reads=[bank_r[b]], writes=[sq_r[0]])
                    b2 = next_bank()
                    PE.op("matmul", banks[b2][:, :], ones_bf[:], sq[:, 0, :], start=True, stop=True,
                          reads=[sq_r[0], const_r], writes=[bank_r[b2]])
                    ACT.op("activation", rs2[:, :], banks[b2][:, :], AF.Ln, bias=epsc[:, 0:1], scale=1.0 / 128,
                           reads=[bank_r[b2], const_r], writes=[rs2_r])
                    ACT.op("activation", rs2[:, :], rs2[:, :], AF.Exp, scale=-0.5, reads=[rs2_r], writes=[rs2_r])

                def proj_fm(wv, c4, t, s):
                    t0, t1 = TILES[t]
                    b = next_bank()
                    for kk in range(KC):
                        PE.op("matmul", banks[b][:, :], wv[:, kk, c4 * 128:(c4 + 1) * 128], xn[:, kk, t0:t1],
                              start=(kk == 0), stop=(kk == KC - 1), reads=[wring_r[s], xn_r[kk][t]],
                              writes=[bank_r[b]] if kk == 0 else [], inc=(kk == KC - 1))
                    bank_r[b].w = (PE.sem, PE.cnt)
                    return b

                with ExitStack() as ph2:
                    def psb2(name, shape, dt):
                        k.uid += 1
                        return ph2.enter_context(nc.sbuf_tensor(f"p_{name}_{k.uid}", list(shape), dt))
                    qTp = psb2("qTp", [128, 8, 512], BF16)
                    qTp_r = regs(8)
                    kTp = psb2("kTp", [128, 2, 512], BF16)
                    kTp_r = regs(2)
                    Vp = psb2("Vp", [128, 4, 256], BF16)
                    Vp_r = regs(4)
                    koutf = psb2("koutf", [128, 2, 512], F32)
                    koutf_r = regs(2)
                    voutf = psb2("voutf", [128, 4, 256], F32)
                    voutf_r = regs(4)
                    for blk in range(3):
                        s = wnext()
                        wv = v3(slot_view(s), 0, KC, 512)
                        for c4 in range(4):
                            ch = blk * 4 + c4
                            if ch < 10:
                                b = proj_fm(wv, c4, 0, s)
                                headnorm(b, None)
                                if ch < 8:
                                    DVE.op("scalar_tensor_tensor", qTp[:, ch, :], banks[b][:, :], qkg[:, 0:1], rs2[:, :],
                                           ALU.mult, ALU.mult, reads=[bank_r[b], rs2_r, const_r], writes=[qTp_r[ch]])
                                else:
                                    DVE.op("scalar_tensor_tensor", koutf[:, ch - 8, :], banks[b][:, :], qkg[:, 1:2], rs2[:, :],
                                           ALU.mult, ALU.mult, reads=[bank_r[b], rs2_r, const_r], writes=[koutf_r[ch - 8]])
                                    ACT.op("activation", kTp[:, ch - 8, :], koutf[:, ch - 8, :], AF.Copy,
                                           reads=[koutf_r[ch - 8]], writes=[kTp_r[ch - 8]])
                            elif ch == 10:
                                for c in range(4):
                                    b = next_bank()
                                    for kk in range(KC):
                                        PE.op("matmul", banks[b][:, 0:256], xn[:, kk, c * 128:(c + 1) * 128], wv[:, kk, 256:512],
                                              start=(kk == 0), stop=(kk == KC - 1), reads=[wring_r[s], xn_r[kk][0]],
                                              writes=[bank_r[b]] if kk == 0 else [], inc=(kk == KC - 1))
                                    bank_r[b].w = (PE.sem, PE.cnt)
                                    ACT.op("activation", voutf[:, c, :], banks[b][:, 0:256], AF.Copy, reads=[bank_r[b]], writes=[voutf_r[c]])
                                    DVE.op("tensor_copy", Vp[:, c, :], banks[b][:, 0:256], reads=[bank_r[b]], writes=[Vp_r[c]])
                    dma(SP, O["kout"], koutf[:], osem, reads=koutf_r)
                    dma(SP, O["vout"], voutf[:], osem, reads=voutf_r)
                    for s2 in range(2):
                        for hd in range(8):
                            kv = hd // 4
                            chunks = []
                            for j in range(2):
                                chunks.append((kTp[:, kv, s2 * 256 + j * 128: s2 * 256 + (j + 1) * 128], kTp_r[kv],
                                               Vp[:, 2 * s2 + j, kv * 128:(kv + 1) * 128], Vp_r[2 * s2 + j]))
                            attend(qTp[:, hd, s2 * 256:(s2 + 1) * 256], [qTp_r[hd]], 256, chunks,
                                   xn[:, hd, s2 * 256:(s2 + 1) * 256], xn_r[hd][0], P, P_r, rec, rec_r)
                    for e in (ACT, DVE, PE):
                        e.wait([(osem.s, osem.n)])
                    barrier()

                with ExitStack() as ph2:
                  if k.sub >= 2:
                      def psb2(name, shape, dt):
                          k.uid += 1
                          return ph2.enter_context(nc.sbuf_tensor(f"p_{name}_{k.uid}", list(shape), dt))
                      qTs = psb2("qTs", [128, 8, 1024], BF16)
                      qTs_r = regs(8, 2)
                      KTf = psb2("KTf", [128, 2, 4608], BF16)
                      KTf_r = Reg()
                      Vf = psb2("Vf", [128, 36, 256], BF16)
                      Vf_r = Reg()
                      ksT = psb2("ksT", [128, 2, 1024], BF16)
                      ksT_r = Reg()
                      vs = psb2("vs", [128, 8, 256], BF16)
                      vs_r = Reg()
                      ropeC = psb2("ropeC", [128, 1024], F32)
                      ropeS = psb2("ropeS", [128, 1024], F32)
                      rope_r = Reg()
                      qn = psb2("qn", [128, 512], F32)
                      qn_r = Reg()
                      qnb = psb2("qnb", [128, 512], BF16)
                      qnb_r = Reg()
                      t1 = psb2("t1", [128, 512], F32)
                      t1_r = Reg()
                      t2 = psb2("t2", [128, 512], F32)
                      t2_r = Reg()
                      dma(SP, ropeC[:], I["ropeC"], asem, writes=[rope_r])
                      dma(SP, ropeS[:], I["ropeS"], asem, writes=[rope_r])
                      rope_r.w = (asem.s, asem.n)
                      dma(POOL, KTf[:, :, 0:512], I["ckT"], asem2, writes=[KTf_r])
                      dma(POOL, Vf[:, 0:4, :], I["cv"], asem2, writes=[Vf_r])
                      for blk in range(3):
                          s = wnext()
                          wv = v3(slot_view(s), 0, KC, 512)
                          for c4 in range(4):
                              ch = blk * 4 + c4
                              if ch < 10:
                                  for t in (1, 2):
                                      c0 = (t - 1) * 512
                                      b = proj_fm(wv, c4, t, s)
                                      headnorm(b, None)
                                      gc = qkg[:, 0:1] if ch < 8 else qkg[:, 1:2]
                                      DVE.op("scalar_tensor_tensor", qn[:, :], banks[b][:, :], gc, rs2[:, :],
                                             ALU.mult, ALU.mult, reads=[bank_r[b], rs2_r, const_r], writes=[qn_r])
                                      ACT.op("activation", qnb[:, :], qn[:, :], AF.Copy, reads=[qn_r], writes=[qnb_r])
                                      b3 = next_bank()
                                      PE.op("matmul", banks[b3][:, :], rmat_bf[:], qnb[:, :], start=True, stop=True,
                                            reads=[qnb_r, const_r], writes=[bank_r[b3]])
                                      DVE.op("tensor_tensor", t1[:, :], qn[:, :], ropeC[:, c0:c0 + 512], ALU.mult,
                                             reads=[qn_r, rope_r], writes=[t1_r])
                                      DVE.op("tensor_tensor", t2[:, :], banks[b3][:, :], ropeS[:, c0:c0 + 512], ALU.mult,
                                             reads=[bank_r[b3], rope_r], writes=[t2_r])
                                      if ch < 8:
                                          DVE.op("tensor_tensor", qTs[:, ch, c0:c0 + 512], t1[:, :], t2[:, :], ALU.add,
                                                 reads=[t1_r, t2_r], writes=[qTs_r[ch][t - 1]])
                                      else:
                                          DVE.op("tensor_tensor", ksT[:, ch - 8, c0:c0 + 512], t1[:, :], t2[:, :], ALU.add,
                                                 reads=[t1_r, t2_r], writes=[ksT_r])
                              elif ch == 10:
                                  for c in range(4, 12):
                                      b = next_bank()
                                      t = 1 if c < 8 else 2
                                      for kk in range(KC):
                                          PE.op("matmul", banks[b][:, 0:256], xn[:, kk, c * 128:(c + 1) * 128], wv[:, kk, 256:512],
                                                start=(kk == 0), stop=(kk == KC - 1), reads=[wring_r[s], xn_r[kk][t]],
                                                writes=[bank_r[b]] if kk == 0 else [], inc=(kk == KC - 1))
                                      bank_r[b].w = (PE.sem, PE.cnt)
                                      ACT.op("activation", vs[:, c - 4, :], banks[b][:, 0:256], AF.Copy, reads=[bank_r[b]], writes=[vs_r])
                      if k.sub < 3:
                        return
                      ta = dma(SP, agin1.ap()[0:128, :], ksT[:].rearrange("p h t -> p (h t)"), asem, reads=[ksT_r])
                      tb = dma(SP, agin1.ap()[128:256, :], vs[:].rearrange("p c f -> p (c f)"), asem, reads=[vs_r])
                      POOL.wait([ta, tb])
                      nc.gpsimd.collective_compute("AllGather", ALU.bypass, replica_groups=GROUPS,
                                                   ins=[agin1.ap().opt()], outs=[agout1.ap().opt()]).then_inc(ccsem.s, 1)
                      ccsem.n += 1
                      cct = (ccsem.s, ccsem.n)
                      SP._deps((), [KTf_r, Vf_r], [cct])
                      for rr in range(4):
                          dma(SP, KTf[:, :, 512 + rr * 1024:512 + (rr + 1) * 1024],
                              agout1.ap()[rr * 256:rr * 256 + 128, :].rearrange("p (h t) -> p h t", h=2), asem)
                          dma(SP, Vf[:, 4 + rr * 8:4 + (rr + 1) * 8, :],
                              agout1.ap()[rr * 256 + 128:(rr + 1) * 256, :].rearrange("p (c f) -> p c f", c=8), asem)
                      kvtok = (asem.s, asem.n)
                      KTf_r.w = None
                      Vf_r.w = None
                      PE.wait([kvtok, (asem2.s, asem2.n)])
                      if dbg and k.dbgsrc == 'kv':
                          POOL.wait([kvtok, (asem2.s, asem2.n)])
                          for hh in range(2):
                              for j3 in range(3):
                                  dma(POOL, O["dbg"][(hh * 3 + j3) * 128:(hh * 3 + j3 + 1) * 128, :], KTf[:, hh, j3 * 1536:(j3 + 1) * 1536], osem)
                          dma(POOL, O["dbg"][768:896, :], Vf[:, 0:6, :].rearrange("p c f -> p (c f)"), osem)
                          dma(POOL, O["dbg"][896:1024, :], Vf[:, 30:36, :].rearrange("p c f -> p (c f)"), osem)
                          for e in (ACT, DVE, PE):
                              e.wait([(osem.s, osem.n)])
                          return
                      if k.sub < 4:
                          return
                      for t in (1, 2):
                          for hd in range(8):
                              kv = hd // 4
                              chunks = []
                              for c in range(36):
                                  chunks.append((KTf[:, kv, c * 128:(c + 1) * 128], KTf_r,
                                                 Vf[:, c, kv * 128:(kv + 1) * 128], Vf_r))
                              attend(qTs[:, hd, (t - 1) * 512:t * 512], [qTs_r[hd][t - 1]], 512, chunks,
                                     xn[:, hd, TILES[t][0]:TILES[t][1]], xn_r[hd][t], P, P_r, rec, rec_r)
                      barrier()

                if dbg and k.dbgsrc == 'at':
                    for kk in range(KC):
                        dma(POOL, O["dbg"][kk * 128:(kk + 1) * 128, :], xn[:, kk, :], osem, reads=xn_r[kk])
                    for e in (ACT, DVE, PE):
                        e.wait([(osem.s, osem.n)])
                for blk in range(2 if k.sub >= 5 else 0):
                    s = wnext()
                    wv = v3(slot_view(s), 0, KC, 512)
                    for c4 in range(4):
                        dc = blk * 4 + c4
                        for t, (t0, t1_) in enumerate(TILES):
                            b = next_bank()
                            for kk in range(KC):
                                PE.op("matmul", banks[b][:, :], wv[:, kk, c4 * 128:(c4 + 1) * 128], xn[:, kk, t0:t1_],
                                      start=(kk == 0), stop=(kk == KC - 1), reads=[wring_r[s], xn_r[kk][t]],
                                      writes=[bank_r[b]] if kk == 0 else [], inc=(kk == KC - 1))
                            bank_r[b].w = (PE.sem, PE.cnt)
                            DVE.op("scalar_tensor_tensor", h[:, dc, t0:t1_], banks[b][:, :], coefG[:, l, 1, dc, cj(t):cj(t) + 1],
                                   h[:, dc, t0:t1_], ALU.mult, ALU.add, reads=[bank_r[b], mod_r], writes=[h_r[dc][t]])
                barrier()


        def run_threads(gens):
            gens = list(gens)
            while gens:
                alive = []
                for g_ in gens:
                    try:
                        next(g_)
                        alive.append(g_)
                    except StopIteration:
                        pass
                gens = alive

        class Obj:
            pass

        def deltanet(l):
            with ExitStack() as ph:
                def psb(name, shape, dt):
                    k.uid += 1
                    return ph.enter_context(nc.sbuf_tensor(f"p_{name}_{k.uid}", list(shape), dt))
                masks = psb("masks", [128, 4, 128], F32)
                convp = psb("convp", [128, 24, 3], F32)
                convo = psb("convo", [128, 6, 3], F32)
                wabp = psb("wabp", [128, KC, 32], BF16)
                wabo = psb("wabo", [128, KC, 8], BF16)
                dtbp = psb("dtbp", [128, 16], F32)
                negAp = psb("negAp", [128, 16], F32)
                dtbo = psb("dtbo", [128, 4], F32)
                negAo = psb("negAo", [128, 4], F32)
                ong = psb("ong", [128, 1], F32)
                selt = psb("selt", [128, 4], F32)
                lmask = psb("lmask", [128, 7, 128], BF16)
                ident2 = psb("ident2", [128, 256], BF16)
                DVE.op("tensor_copy", ident2[:, 0:128], ident_f[:], reads=[const_r])
                DVE.op("tensor_copy", ident2[:, 128:256], ident_f[:], reads=[const_r])
                dn_r = Reg()
                dma(POOL, lmask[:], I["lmask"], asem2)
                dma(SP, masks[:], I["masks"], dsem)
                dma(SP, convp[:], I["convT"], dsem)
                dma(SP, convo[:], I["convT_own"], dsem)
                dma(SP, dtbp[:], I["dtb_p"].partition_broadcast(128), dsem)
                dma(SP, negAp[:], I["alog_p"].partition_broadcast(128), dsem)
                dma(SP, dtbo[:], I["dtb_o"].partition_broadcast(128), dsem)
                dma(SP, negAo[:], I["alog_o"].partition_broadcast(128), dsem)
                dma(SP, ong[:], I["ong"], dsem)
                dma(SP, selt[:], I["sel"].partition_broadcast(128), dsem)
                dma(POOL, wabp[:], I["wab_p"].rearrange("(k p) c -> p k c", p=128), asem2)
                dma(POOL, wabo[:], I["wab_o"].rearrange("(k p) c -> p k c", p=128), asem2)
                dtok = (dsem.s, dsem.n)
                dtok2 = (asem2.s, asem2.n)
                for (na, nA) in ((16, negAp), (4, negAo)):
                    ACT.op("activation", nA[:, :], nA[:, :], AF.Exp, deps=[dtok], writes=[dn_r])
                    DVE.op("tensor_scalar", nA[:, :], nA[:, :], -1.0, None, ALU.mult, writes=[dn_r])
                dn_r.w = None
                for e_ in (PE, ACT, DVE, POOL):
                    e_.wait([dtok, dtok2, (DVE.sem, DVE.cnt)])

                sq = psb("sq", [128, 2, 512], BF16)
                sq_r = regs(2)
                rstd = psb("rstd", [128, 512], F32)
                rstd_r = Reg()
                tmp = psb("tmp", [128, 2, 512], F32)
                pool = (sq, sq_r, rstd, rstd_r, tmp, regs(2))
                modnorm(lambda kk, t: xn[:, kk, TILES[t][0]:TILES[t][1]], lambda kk, t: xn_r[kk][t],
                        lambda kk, t: coefA[:, l, 1, kk, cj(t):cj(t) + 1],
                        lambda kk, t: modT[:, l, 3 * 8 + kk, cj(t):cj(t) + 1], pool)
                for x in range(2):
                    ta = dma(SP, agin2[x].ap().rearrange("(k p) t -> p k t", p=128), xn[:, 4 * x:4 * x + 4, 512:1536], asem,
                             reads=[xn_r[kk][t] for kk in range(4 * x, 4 * x + 4) for t in (1, 2)])
                POOL.wait([ta])
                for x in range(2):
                    nc.gpsimd.collective_compute("AllGather", ALU.bypass, replica_groups=GROUPS,
                                                 ins=[agin2[x].ap().opt()], outs=[agout2[x].ap().opt()]).then_inc(ccsem.s, 1)
                    ccsem.n += 1
                cct2 = (ccsem.s, ccsem.n)

                if k.dsub < 2:
                    for e_ in (PE, ACT, DVE):
                        e_.wait([cct2])
                    barrier()
                    return
                G = 4
                TS = []
                for g_ in range(G):
                    t = Obj()
                    t.cols = psb("cols", [128, 8], F32); t.cols_r = Reg()
                    t.dgd = psb("dgd", [128, 128], F32); t.dgd_r = Reg()
                    t.gcr = psb("gcr", [128, 128], F32); t.gcr_r = Reg()
                    t.egr = psb("egr", [128, 128], F32); t.egr_r = Reg()
                    t.ddT = psb("ddT", [128, 128], F32); t.ddT_r = Reg()
                    t.kq = psb("kq", [128, 256], BF16); t.kks_r = Reg()
                    t.kvt = psb("kvt", [128, 256], BF16); t.ktok_r = Reg()
                    t.A = [psb("A0", [128, 128], BF16), psb("A1", [128, 128], BF16)]; t.A_r = regs(2)
                    t.BP = [psb("BP0", [128, 256], BF16), psb("BP1", [128, 256], BF16)]; t.BP_r = regs(2)
                    t.TT = psb("TT", [128, 128], BF16); t.TT_r = Reg()
                    t.vb = psb("vb", [128, 128], BF16); t.vb_r = Reg()
                    t.kbg = psb("kbg", [128, 128], BF16); t.kbg_r = Reg()
                    TS.append(t)
                OS = []
                for par in range(2):
                    row = []
                    for g_ in range(G):
                        o = Obj()
                        o.wT = psb("wT", [128, 128], BF16)
                        o.u = psb("u", [128, 128], F32)
                        o.qkT = psb("qkT", [128, 128], BF16)
                        o.qdT = psb("qdT", [128, 128], BF16)
                        o.kd = psb("kd", [128, 128], BF16)
                        o.glc = psb("glc", [128, 1], F32)
                        o.r = Reg()
                        row.append(o)
                    OS.append(row)
                CH = []
                for c_ in range(2):
                    ch = Obj()
                    ch.S = psb("S", [128, 128], F32); ch.S_r = Reg()
                    ch.Sb = psb("Sb", [128, 128], BF16); ch.Sb_r = Reg()
                    ch.vn = psb("vn", [128, 128], BF16); ch.vn_r = Reg()
                    CH.append(ch)
                cvt = psb("cvt", [128, 2, 256], F32)
                cvt_r = regs(2)
                slf = psb("slf", [128, 2, 256], F32)
                slf_r = regs(2)
                gtmp = psb("gtmp", [128, 2, 32], F32)
                gtmp_r = regs(2)
                k.cv_i = 0

                def prep(sy):
                    t = TS[sy.g]
                    o = OS[sy.par][sy.g]
                    e = sy.e
                    tri = masks[:, e, :]
                    mA = masks[:, 2 + e, :]
                    last = 127 if e == 0 else 0
                    gcc = t.cols[:, 0:1]
                    b = next_bank()
                    PE.op("matmul", banks[b][:, 0:128], sy.kT, sy.kT, start=True, stop=True, reads=sy.qkv_r, writes=[bank_r[b]], inc=False)
                    PE.op("matmul", banks[b][:, 128:256], sy.kT, sy.qT, start=True, stop=True, reads=sy.qkv_r, inc=False)
                    PE.op("matmul", banks[b][:, 256:384], sy.kT, ident_bf[:], start=True, stop=True, reads=sy.qkv_r + [const_r], inc=False)
                    PE.op("matmul", banks[b][:, 384:512], sy.vT, ident_bf[:], start=True, stop=True, reads=sy.qkv_r + [const_r])
                    bank_r[b].w = (PE.sem, PE.cnt)
                    DVE.op("tensor_copy", t.kq[:, :], banks[b][:, 0:256], reads=[bank_r[b]], writes=[t.kks_r])
                    DVE.op("tensor_copy", t.kvt[:, :], banks[b][:, 256:512], reads=[bank_r[b]], writes=[t.ktok_r])
                    b = next_bank()
                    PE.op("matmul", banks[b][:, 0:1], tri, sy.gcol, start=True, stop=True, reads=[sy.g_r], writes=[bank_r[b]])
                    ACT.op("activation", gcc, banks[b][:, 0:1], AF.Copy, reads=[bank_r[b]], writes=[t.cols_r])
                    yield
                    DVE.op("tensor_scalar", t.dgd[:, :], ident_f[:], gcc, None, ALU.mult, reads=[t.cols_r, const_r], writes=[t.dgd_r])
                    yield
                    b = next_bank()
                    PE.op("matmul", banks[b][:, 0:128], ones_f[:], t.dgd[:, :], start=True, stop=True, reads=[t.dgd_r, const_r], writes=[bank_r[b]])
                    ACT.op("activation", t.gcr[:, :], banks[b][:, 0:128], AF.Copy, reads=[bank_r[b]], writes=[t.gcr_r])
                    ACT.op("activation", t.egr[:, :], banks[b][:, 0:128], AF.Exp, reads=[bank_r[b]], writes=[t.egr_r])
                    ACT.op("activation", t.cols[:, 1:2], gcc, AF.Exp, reads=[t.cols_r], writes=[t.cols_r])
                    ACT.op("activation", t.cols[:, 2:3], gcc, AF.Exp, scale=-1.0, bias=t.gcr[:, last:last + 1],
                           reads=[t.cols_r, t.gcr_r], writes=[t.cols_r])
                    yield
                    DVE.op("tensor_scalar", t.dgd[:, :], t.gcr[:, :], gcc, 0.0, ALU.subtract, ALU.max,
                           reads=[t.gcr_r, t.cols_r], writes=[t.dgd_r])
                    DVE.op("tensor_scalar", t.ddT[:, :], t.gcr[:, :], gcc, 0.0, ALU.subtract, ALU.min,
                           reads=[t.gcr_r, t.cols_r], writes=[t.ddT_r])
                    DVE.op("tensor_tensor", t.cols[:, 3:4], sy.bcol, t.cols[:, 1:2], ALU.mult, reads=[t.cols_r, sy.g_r], writes=[t.cols_r])
                    DVE.op("tensor_copy", o.glc[:, :], t.egr[:, last:last + 1], reads=[t.egr_r], writes=[o.r])
                    yield
                    ACT.op("activation", t.dgd[:, :], t.dgd[:, :], AF.Exp, scale=-1.0, reads=[t.dgd_r], writes=[t.dgd_r])
                    ACT.op("activation", t.ddT[:, :], t.ddT[:, :], AF.Exp, reads=[t.ddT_r], writes=[t.ddT_r])
                    yield
                    DVE.op("tensor_tensor", t.dgd[:, :], t.dgd[:, :], mA, ALU.mult, reads=[t.dgd_r], writes=[t.dgd_r])
                    DVE.op("scalar_tensor_tensor", t.A[0][:, :], t.kq[:, 0:128], sy.bcol, t.dgd[:, :], ALU.mult, ALU.mult,
                           reads=[t.kks_r, t.dgd_r, sy.g_r], writes=[t.A_r[0]])
                    DVE.op("tensor_tensor", t.ddT[:, :], t.ddT[:, :], tri, ALU.mult, reads=[t.ddT_r], writes=[t.ddT_r])
                    yield
                    b = next_bank()
                    PE.op("matmul", banks[b][:, 0:128], t.A[0][:, :], ident_bf[:], start=True, stop=True, reads=[t.A_r[0], const_r], writes=[bank_r[b]])
                    DVE.op("tensor_copy", t.A[1][:, :], banks[b][:, 0:128], reads=[bank_r[b]], writes=[t.A_r[1]])
                    POOL.op("tensor_tensor", o.qkT[:, :], t.kq[:, 128:256], t.ddT[:, :], ALU.mult, reads=[t.kks_r, t.ddT_r], writes=[o.r])
                    POOL.op("tensor_tensor", o.qdT[:, :], sy.qT, t.egr[:, :], ALU.mult, reads=sy.qkv_r + [t.egr_r], writes=[o.r])
                    POOL.op("tensor_scalar", t.vb[:, :], t.kvt[:, 128:256], sy.bcol, None, ALU.mult, reads=[t.ktok_r, sy.g_r], writes=[t.vb_r])
                    POOL.op("tensor_scalar", t.kbg[:, :], t.kvt[:, 0:128], t.cols[:, 3:4], None, ALU.mult, reads=[t.ktok_r, t.cols_r], writes=[t.kbg_r])
                    POOL.op("tensor_scalar", o.kd[:, :], t.kvt[:, 0:128], t.cols[:, 2:3], None, ALU.mult, reads=[t.ktok_r, t.cols_r], writes=[o.r])
                    yield
                    if e == 0:
                        Bin, Bin_r = t.A[1], t.A_r[1]
                    else:
                        Bin, Bin_r = t.A[0], t.A_r[0]
                    Tc, TTc, TQ_r = ident2[:, 0:128], ident2[:, 128:256], const_r
                    TQc = ident2[:, 0:256]
                    cur = 0
                    for li in range(7):
                        b = next_bank()
                        PE.op("matmul", banks[b][:, 0:128], Bin[:, :], Tc, start=True, stop=True, reads=[Bin_r, TQ_r], writes=[bank_r[b]])
                        DVE.op("scalar_tensor_tensor", t.TT[:, :], banks[b][:, 0:128], -1.0, lmask[:, li, :], ALU.mult, ALU.mult,
                               reads=[bank_r[b], dn_r], writes=[t.TT_r])
                        b2 = next_bank()
                        PE.op("matmul", banks[b2][:, 0:256], ident_bf[:], TQc, start=True, stop=False, reads=[TQ_r, const_r], writes=[bank_r[b2]], inc=False)
                        PE.op("matmul", banks[b2][:, 0:128], TTc, t.TT[:, :], start=False, stop=False, reads=[TQ_r, t.TT_r], inc=False)
                        PE.op("matmul", banks[b2][:, 128:256], t.TT[:, :], TTc, start=False, stop=True, reads=[TQ_r, t.TT_r])
                        bank_r[b2].w = (PE.sem, PE.cnt)
                        ACT.op("activation", t.BP[cur][:, 0:256], banks[b2][:, 0:256], AF.Copy, reads=[bank_r[b2]], writes=[t.BP_r[cur]])
                        Tc, TTc, TQ_r = t.BP[cur][:, 0:128], t.BP[cur][:, 128:256], t.BP_r[cur]
                        TQc = t.BP[cur][:, 0:256]
                        cur = 1 - cur
                        yield
                    TTf = TTc if e == 0 else Tc
                    TTf_r = TQ_r
                    b = next_bank()
                    PE.op("matmul", banks[b][:, 0:128], t.kbg[:, :], TTf, start=True, stop=True, reads=[t.kbg_r, TTf_r], writes=[bank_r[b]], inc=False)
                    PE.op("matmul", banks[b][:, 128:256], TTf, t.vb[:, :], start=True, stop=True, reads=[t.vb_r, TTf_r])
                    bank_r[b].w = (PE.sem, PE.cnt)
                    ACT.op("activation", o.wT[:, :], banks[b][:, 0:128], AF.Copy, reads=[bank_r[b]], writes=[o.r])
                    DVE.op("tensor_copy", o.u[:, :], banks[b][:, 128:256], reads=[bank_r[b]], writes=[o.r])
                    yield

                def scan(ch, steps):
                    for (o, oacc, oacc_r) in steps:
                        b = next_bank()
                        PE.op("matmul", banks[b][:, 0:128], o.wT[:, :], ch.Sb[:, :], start=True, stop=True, reads=[o.r, ch.Sb_r], writes=[bank_r[b]])
                        DVE.op("tensor_tensor", ch.vn[:, :], o.u[:, :], banks[b][:, 0:128], ALU.subtract, reads=[o.r, bank_r[b]], writes=[ch.vn_r])
                        yield
                        b = next_bank()
                        PE.op("matmul", banks[b][:, 0:128], ch.Sb[:, :], o.qdT[:, :], start=True, stop=False, reads=[o.r, ch.Sb_r], writes=[bank_r[b]], inc=False)
                        PE.op("matmul", banks[b][:, 0:128], ch.vn[:, :], o.qkT[:, :], start=False, stop=True, reads=[o.r, ch.vn_r], writes=[bank_r[b]])
                        DVE.op("tensor_tensor", oacc, oacc, banks[b][:, 0:128], ALU.add, reads=[bank_r[b]], writes=[oacc_r])
                        yield
                        b = next_bank()
                        PE.op("matmul", banks[b][:, 0:128], o.kd[:, :], ch.vn[:, :], start=True, stop=True, reads=[o.r, ch.vn_r], writes=[bank_r[b]])
                        DVE.op("scalar_tensor_tensor", ch.S[:, :], ch.S[:, :], o.glc[:, 0:1], banks[b][:, 0:128], ALU.mult, ALU.add,
                               reads=[o.r, bank_r[b]], writes=[ch.S_r])
                        ACT.op("activation", ch.Sb[:, :], ch.S[:, :], AF.Copy, reads=[ch.S_r], writes=[ch.Sb_r])
                        yield

                def proj_chunk(wv, c4, s, ublk, ublk_r, typ, conv_ap, dst, dst_r):
                    b = next_bank()
                    for kk in range(KC):
                        PE.op("matmul", banks[b][:, 0:258], wv[:, kk, c4 * 128:(c4 + 1) * 128], ublk[:, kk, :],
                              start=(kk == 0), stop=(kk == KC - 1), reads=[wring_r[s], ublk_r],
                              writes=[bank_r[b]] if kk == 0 else [], inc=(kk == KC - 1))
                    bank_r[b].w = (PE.sem, PE.cnt)
                    if typ == 3:
                        ACT.op("activation", dst, banks[b][:, 1:257], AF.Silu, reads=[bank_r[b]], writes=[dst_r])
                        return
                    i = k.cv_i % 2
                    k.cv_i += 1
                    ACT.op("activation", cvt[:, i, :], banks[b][:, 1:257], AF.Copy, scale=conv_ap[:, 1:2], reads=[bank_r[b], dn_r], writes=[cvt_r[i]])
                    DVE.op("scalar_tensor_tensor", cvt[:, i, :], banks[b][:, 0:256], conv_ap[:, 0:1], cvt[:, i, :], ALU.mult, ALU.add,
                           reads=[bank_r[b], dn_r], writes=[cvt_r[i]])
                    DVE.op("scalar_tensor_tensor", cvt[:, i, :], banks[b][:, 2:258], conv_ap[:, 2:3], cvt[:, i, :], ALU.mult, ALU.add,
                           reads=[bank_r[b], dn_r], writes=[cvt_r[i]])
                    if typ == 2:
                        ACT.op("activation", dst, cvt[:, i, :], AF.Silu, reads=[cvt_r[i]], writes=[dst_r])
                        return
                    ACT.op("activation", slf[:, i, :], cvt[:, i, :], AF.Silu, reads=[cvt_r[i]], writes=[slf_r[i]])
                    ACT.op("activation", sq[:, i, 0:256], slf[:, i, :], AF.Square, reads=[slf_r[i]], writes=[sq_r[i]])
                    b2 = next_bank()
                    PE.op("matmul", banks[b2][:, 0:256], ones_bf[:], sq[:, i, 0:256], start=True, stop=True, reads=[sq_r[i], const_r], writes=[bank_r[b2]])
                    ACT.op("activation", cvt[:, i, :], banks[b2][:, 0:256], AF.Ln, bias=epsc[:, 0:1], reads=[bank_r[b2], const_r], writes=[cvt_r[i]])
                    ACT.op("activation", cvt[:, i, :], cvt[:, i, :], AF.Exp, scale=-0.5, reads=[cvt_r[i]], writes=[cvt_r[i]])
                    DVE.op("scalar_tensor_tensor", dst, slf[:, i, :], (128.0 ** -0.5) if typ == 0 else 1.0, cvt[:, i, :], ALU.mult, ALU.mult,
                           reads=[slf_r[i], cvt_r[i]], writes=[dst_r])

                def gbeta(ublk, ublk_r, wab, ncol, dtb, negA, dst_fn):
                    na = ncol // 2
                    for i in range(2):
                        b = next_bank()
                        for kk in range(KC):
                            PE.op("matmul", banks[b][:, 0:ncol], ublk[:, kk, 1 + i * 128:1 + (i + 1) * 128], wab[:, kk, :],
                                  start=(kk == 0), stop=(kk == KC - 1), reads=[ublk_r, dn_r],
                                  writes=[bank_r[b]] if kk == 0 else [], inc=(kk == KC - 1))
                        bank_r[b].w = (PE.sem, PE.cnt)
                        d = dst_fn(i)
                        DVE.op("tensor_tensor", gtmp[:, i, 0:na], banks[b][:, 0:na], dtb[:, :], ALU.add, reads=[bank_r[b], dn_r], writes=[gtmp_r[i]])
                        ACT.op("activation", gtmp[:, i, na:ncol], banks[b][:, na:ncol], AF.Exp, scale=-1.0, reads=[bank_r[b]], writes=[gtmp_r[i]])
                        ACT.op("activation", gtmp[:, i, 0:na], gtmp[:, i, 0:na], AF.Exp, reads=[gtmp_r[i]], writes=[gtmp_r[i]])
                        ACT.op("activation", gtmp[:, i, 0:na], gtmp[:, i, 0:na], AF.Ln, bias=onec[:, 0:1], reads=[gtmp_r[i], const_r], writes=[gtmp_r[i]])
                        DVE.op("tensor_tensor", d[:, 0:na], gtmp[:, i, 0:na], negA[:, :], ALU.mult, reads=[gtmp_r[i], dn_r], writes=[k.gst_r])
                        DVE.op("tensor_scalar", gtmp[:, i, na:ncol], gtmp[:, i, na:ncol], 1.0, None, ALU.add, reads=[gtmp_r[i]], writes=[gtmp_r[i]])
                        DVE.op("reciprocal", d[:, na:ncol], gtmp[:, i, na:ncol], reads=[gtmp_r[i]], writes=[k.gst_r])

                def post(oacc, oacc_r, zs_fn, zs_r_fn, T, dst_fn, dst_r_fn):
                    for p0 in range(0, T, 512):
                        w = min(512, T - p0)
                        ACT.op("activation", sq[:, 0, 0:w], oacc[:, p0:p0 + w], AF.Square, reads=[oacc_r], writes=[sq_r[0]])
                        b = next_bank()
                        PE.op("matmul", banks[b][:, 0:w], ones_bf[:], sq[:, 0, 0:w], start=True, stop=True, reads=[sq_r[0], const_r], writes=[bank_r[b]])
                        ACT.op("activation", rstd[:, 0:w], banks[b][:, 0:w], AF.Ln, bias=epsc[:, 0:1], scale=1.0 / 128, reads=[bank_r[b], const_r], writes=[rstd_r])
                        ACT.op("activation", rstd[:, 0:w], rstd[:, 0:w], AF.Exp, scale=-0.5, reads=[rstd_r], writes=[rstd_r])
                        DVE.op("scalar_tensor_tensor", tmp[:, 0, 0:w], oacc[:, p0:p0 + w], ong[:, 0:1], rstd[:, 0:w], ALU.mult, ALU.mult,
                               reads=[oacc_r, rstd_r, dn_r], writes=[k.tmp0_r])
                        DVE.op("tensor_tensor", dst_fn(p0, w), tmp[:, 0, 0:w], zs_fn(p0, w), ALU.mult, reads=[k.tmp0_r, zs_r_fn(p0)], writes=[dst_r_fn(p0)])

                k.tmp0_r = Reg()
                k.gst_r = Reg()

                with ExitStack() as ph2:
                    def psb2(name, shape, dt):
                        k.uid += 1
                        return ph2.enter_context(nc.sbuf_tensor(f"p_{name}_{k.uid}", list(shape), dt))
                    upad = psb2("upad", [128, KC, 2, 258], BF16)
                    upad_r = Reg()
                    qkvz = psb2("qkvz", [128, 32, 512], BF16)
                    qkvz_r = regs(32)
                    gstp = psb2("gstp", [128, 4, 32], F32)
                    oaccp = psb2("oaccp", [128, 256], F32)
                    oaccp_r = Reg()
                    DVE.op("memset", upad[:], 0.0, writes=[upad_r])
                    for s2 in range(2):
                        ACT.op("activation", upad[:, :, s2, 1:257], xn[:, :, s2 * 256:(s2 + 1) * 256], AF.Copy,
                               reads=[xn_r[kk][0] for kk in range(KC)], writes=[upad_r])
                    for blk in range(8):
                        s = wnext()
                        wv = v3(slot_view(s), 0, KC, 512)
                        typ = blk // 2
                        for c4 in range(4):
                            hd = (blk % 2) * 4 + c4
                            ci = typ * 8 + hd
                            for s2 in range(2):
                                proj_chunk(wv, c4, s, upad[:, :, s2, :], upad_r, typ,
                                           convp[:, ci, :] if typ < 3 else None,
                                           qkvz[:, ci, s2 * 256:(s2 + 1) * 256], qkvz_r[ci])
                    for s2 in range(2):
                        gbeta(upad[:, :, s2, :], upad_r, wabp, 32, dtbp, negAp, lambda i, s2=s2: gstp[:, 2 * s2 + i, :])
                    if dbg and k.dbgsrc == 'dnu':
                        for kk in range(KC):
                            dma(POOL, O["dbg"][kk * 128:(kk + 1) * 128, 0:512], xn[:, kk, 0:512], osem, reads=[xn_r[kk][0]])
                            dma(POOL, O["dbg"][kk * 128:(kk + 1) * 128, 512:1024], xn[:, kk, 512:1024], osem, reads=[xn_r[kk][1]])
                            dma(SP, O["dbg"][kk * 128:(kk + 1) * 128, 1024:1536], h[:, kk, 0:512], osem, reads=[h_r[kk][0]])
                        for e_ in (ACT, DVE, PE):
                            e_.wait([(osem.s, osem.n)])
                        return
                    if dbg and k.dbgsrc == 'dn':
                        dma(POOL, O["dbg"][0:128, 0:512], qkvz[:, 0, :], osem, reads=[qkvz_r[0]])
                        dma(POOL, O["dbg"][0:128, 512:1024], qkvz[:, 8, :], osem, reads=[qkvz_r[8]])
                        dma(POOL, O["dbg"][0:128, 1024:1536], qkvz[:, 16, :], osem, reads=[qkvz_r[16]])
                        dma(POOL, O["dbg"][128:256, 0:512], qkvz[:, 24, :], osem, reads=[qkvz_r[24]])
                        dma(POOL, O["dbg"][128:256, 512:640], gstp[:].rearrange("p c f -> p (c f)"), osem, reads=[k.gst_r])
                    pending = None
                    jobs = [(s2, hd) for s2 in range(2) for hd in range(8)]
                    for j in range(len(jobs) + 1):
                        gens = []
                        if j < len(jobs):
                            s2, hd = jobs[j]
                            par = j % 2
                            systems = []
                            for g_, (e, c) in enumerate(((0, 0), (0, 1), (1, 1), (1, 0))):
                                sy = Obj()
                                sy.g, sy.par, sy.e = g_, par, e
                                col0 = s2 * 256 + c * 128
                                sy.qT = qkvz[:, 0 * 8 + hd, col0:col0 + 128]
                                sy.kT = qkvz[:, 1 * 8 + hd, col0:col0 + 128]
                                sy.vT = qkvz[:, 2 * 8 + hd, col0:col0 + 128]
                                sy.qkv_r = [qkvz_r[hd], qkvz_r[8 + hd], qkvz_r[16 + hd]]
                                sy.gcol = gstp[:, 2 * s2 + c, e * 8 + hd:e * 8 + hd + 1]
                                sy.bcol = gstp[:, 2 * s2 + c, 16 + e * 8 + hd:16 + e * 8 + hd + 1]
                                sy.g_r = k.gst_r
                                systems.append(sy)
                                gens.append(prep(sy))
                        if pending is not None:
                            gens.extend(pending)
                        run_threads(gens)
                        if pending is not None:
                            (ps2, phd, ppar) = k.pjob
                            dma(SP, O["sfo"][:, ps2 * 8 + phd, :], CH[0].S[:, :], osem, reads=[CH[0].S_r])
                            dma(SP, O["sbo"][:, ps2 * 8 + phd, :], CH[1].S[:, :], osem, reads=[CH[1].S_r])
                            if dbg and k.dbgsrc == 'dn' and (ps2, phd) == (0, 0):
                                dma(POOL, O["dbg"][256:384, 0:256], oaccp[:, :], osem, reads=[oaccp_r])
                                dma(POOL, O["dbg"][384:512, 0:128], CH[0].S[:, :], osem, reads=[CH[0].S_r])
                                for g_ in range(4):
                                    o_ = OS[ppar][g_]
                                    dma(POOL, O["dbg"][512:640, g_ * 128:(g_ + 1) * 128], o_.wT[:, :], osem, reads=[o_.r])
                                    dma(POOL, O["dbg"][640:768, g_ * 128:(g_ + 1) * 128], o_.u[:, :], osem, reads=[o_.r])
                                    dma(POOL, O["dbg"][768:896, g_ * 128:(g_ + 1) * 128], o_.qkT[:, :], osem, reads=[o_.r])
                                    dma(POOL, O["dbg"][896:1024, g_ * 128:(g_ + 1) * 128], o_.kd[:, :], osem, reads=[o_.r])
                                    dma(POOL, O["dbg"][512:640, 512 + g_ * 128:512 + (g_ + 1) * 128], o_.qdT[:, :], osem, reads=[o_.r])
                                    dma(POOL, O["dbg"][640:768, 512 + g_:512 + g_ + 1], o_.glc[:, :], osem, reads=[o_.r], allow_slow_non_contiguous=True)
                                for e_ in (ACT, DVE, PE):
                                    e_.wait([(osem.s, osem.n)])
                            post(oaccp, oaccp_r, lambda p0, w, ps2=ps2, phd=phd: qkvz[:, 24 + phd, ps2 * 256 + p0:ps2 * 256 + p0 + w],
                                 lambda p0, phd=phd: qkvz_r[24 + phd], 256,
                                 lambda p0, w, ps2=ps2, phd=phd: xn[:, phd, ps2 * 256 + p0:ps2 * 256 + p0 + w],
                                 lambda p0, phd=phd: xn_r[phd][0])
                            pending = None
                        if j < len(jobs):
                            for ch in CH:
                                DVE.op("memset", ch.S[:, :], 0.0, writes=[ch.S_r])
                                DVE.op("memset", ch.Sb[:, :], 0.0, writes=[ch.Sb_r])
                            DVE.op("memset", oaccp[:, :], 0.0, writes=[oaccp_r])
                            par = j % 2
                            stf = [(OS[par][0], oaccp[:, 0:128], oaccp_r), (OS[par][1], oaccp[:, 128:256], oaccp_r)]
                            stb = [(OS[par][2], oaccp[:, 128:256], oaccp_r), (OS[par][3], oaccp[:, 0:128], oaccp_r)]
                            pending = [scan(CH[0], stf), scan(CH[1], stb)]
                            k.pjob = (s2, hd, par)
                    for e_ in (ACT, DVE, PE):
                        e_.wait([(osem.s, osem.n)])
                    barrier()

                if k.dsub < 3:
                    return
                with ExitStack() as ph2:
                    def psb2(name, shape, dt):
                        k.uid += 1
                        return ph2.enter_context(nc.sbuf_tensor(f"p_{name}_{k.uid}", list(shape), dt))
                    TT_ = 4096
                    ub = [psb2(f"ub{i}", [128, KC, 258], BF16) for i in range(2)]
                    ub_r = regs(2)
                    qkvzs = psb2("qkvzs", [128, 3, TT_], BF16)
                    qkvzs_r = regs(3)
                    gsts = psb2("gsts", [128, 32, 8], F32)
                    oaccs = psb2("oaccs", [128, TT_], F32)
                    oaccs_r = Reg()
                    s0 = psb2("s0", [128, 2, 2, 128], F32)
                    s0_r = Reg()
                    dma(SP, s0[:, 0, :, :], I["s0f"], dsem, writes=[s0_r])
                    dma(SP, s0[:, 1, :, :], I["s0b"], dsem, writes=[s0_r])
                    s0_r.w = (dsem.s, dsem.n)

                    def load_ublk(tb):
                        i = tb % 2
                        rr, ls = tb // 4, (tb % 4) * 256 - 1
                        lo, hi = max(ls, 0), min(ls + 258, 1024)
                        us = usems[i]
                        SP._deps((), [ub_r[i]], [cct2])
                        SP.wait(BAR["toks"])
                        for x in range(2):
                            tok = dma(SP, ub[i][:, 4 * x:4 * x + 4, lo - ls:hi - ls],
                                      agout2[x].ap()[rr * 512:(rr + 1) * 512, lo:hi].rearrange("(k p) t -> p k t", p=128), us)
                        if ls < 0:
                            if rr > 0:
                                for x in range(2):
                                    tok = dma(SP, ub[i][:, 4 * x:4 * x + 4, 0:1],
                                              agout2[x].ap()[(rr - 1) * 512:rr * 512, 1023:1024].rearrange("(k p) t -> p k t", p=128), us,
                                              allow_slow_non_contiguous=True)
                            else:
                                ub_r[i].w = tok
                                ub_r[i].r = {}
                                tok = DVE.op("memset", ub[i][:, :, 0:1], 0.0, writes=[ub_r[i]])
                                return
                        if ls + 258 > 1024:
                            if rr < 3:
                                for x in range(2):
                                    tok = dma(SP, ub[i][:, 4 * x:4 * x + 4, 257:258],
                                              agout2[x].ap()[(rr + 1) * 512:(rr + 2) * 512, 0:1].rearrange("(k p) t -> p k t", p=128), us,
                                              allow_slow_non_contiguous=True)
                            else:
                                ub_r[i].w = tok
                                ub_r[i].r = {}
                                tok = DVE.op("memset", ub[i][:, :, 257:258], 0.0, writes=[ub_r[i]])
                                return
                        ub_r[i].w = tok
                        ub_r[i].r = {}

                    for hh in range(2):
                        s = wnext()
                        wv = v3(slot_view(s), 0, KC, 512)
                        load_ublk(0)
                        for tb in range(16):
                            if tb + 1 < 16:
                                load_ublk(tb + 1)
                            i = tb % 2
                            for typ in range(4):
                                if typ < 3:
                                    dst_, dst_r_ = qkvzs[:, typ, tb * 256:(tb + 1) * 256], qkvzs_r[typ]
                                else:
                                    zc = 512 + (tb % 4) * 256
                                    dst_, dst_r_ = xn[:, tb // 4, zc:zc + 256], xn_r[tb // 4][1 if (tb % 4) < 2 else 2]
                                proj_chunk(wv, typ, s, ub[i][:, :, :], ub_r[i], typ,
                                           convo[:, hh * 3 + typ, :] if typ < 3 else None, dst_, dst_r_)
                            if hh == 0:
                                gbeta(ub[i][:, :, :], ub_r[i], wabo, 8, dtbo, negAo, lambda ii, tb=tb: gsts[:, 2 * tb + ii, :])
                        DVE.op("memset", oaccs[:, :], 0.0, writes=[oaccs_r])
                        for e in range(2):
                            DVE.op("tensor_copy", CH[e].S[:, :], s0[:, e, hh, :], reads=[s0_r], writes=[CH[e].S_r])
                            ACT.op("activation", CH[e].Sb[:, :], s0[:, e, hh, :], AF.Copy, reads=[s0_r], writes=[CH[e].Sb_r])
                        pending = None
                        NCH = TT_ // 128
                        for gi in range(NCH // 2 + 1):
                            gens = []
                            if gi < NCH // 2:
                                par = gi % 2
                                order = ((0, 2 * gi), (0, 2 * gi + 1), (1, NCH - 1 - 2 * gi), (1, NCH - 2 - 2 * gi))
                                for g_, (e, c) in enumerate(order):
                                    sy = Obj()
                                    sy.g, sy.par, sy.e = g_, par, e
                                    sy.qT = qkvzs[:, 0, c * 128:(c + 1) * 128]
                                    sy.kT = qkvzs[:, 1, c * 128:(c + 1) * 128]
                                    sy.vT = qkvzs[:, 2, c * 128:(c + 1) * 128]
                                    sy.qkv_r = [qkvzs_r[0], qkvzs_r[1], qkvzs_r[2]]
                                    sy.gcol = gsts[:, c, e * 2 + hh:e * 2 + hh + 1]
                                    sy.bcol = gsts[:, c, 4 + e * 2 + hh:4 + e * 2 + hh + 1]
                                    sy.g_r = k.gst_r
                                    gens.append(prep(sy))
                            if pending is not None:
                                gens.extend(pending)
                            run_threads(gens)
                            pending = None
                            if gi < NCH // 2:
                                par = gi % 2
                                order = ((0, 2 * gi), (0, 2 * gi + 1), (1, NCH - 1 - 2 * gi), (1, NCH - 2 - 2 * gi))
                                stf = [(OS[par][g_], oaccs[:, c * 128:(c + 1) * 128], oaccs_r) for g_, (e, c) in enumerate(order) if e == 0]
                                stb = [(OS[par][g_], oaccs[:, c * 128:(c + 1) * 128], oaccs_r) for g_, (e, c) in enumerate(order) if e == 1]
                                pending = [scan(CH[0], stf), scan(CH[1], stb)]
                        def zfn(p0, w):
                            j_ = p0 // 512
                            return xn[:, j_ // 2, 512 + (j_ % 2) * 512:512 + (j_ % 2) * 512 + w]

                        def ofn(p0, w):
                            j_ = p0 // 512
                            return xn[:, 4 + j_ // 2, 512 + (j_ % 2) * 512:512 + (j_ % 2) * 512 + w]
                        post(oaccs, oaccs_r, zfn, lambda p0: xn_r[(p0 // 512) // 2][1 + (p0 // 512) % 2], TT_,
                             ofn, lambda p0: xn_r[4 + (p0 // 512) // 2][1 + (p0 // 512) % 2])
                        for j_ in range(8):
                            tg = dma(SP, agin3[hh].ap()[:, j_ * 512:(j_ + 1) * 512], ofn(j_ * 512, 512), asem,
                                     reads=[xn_r[4 + j_ // 2][1 + j_ % 2]])
                        POOL.wait([tg])
                        nc.gpsimd.collective_compute("AllGather", ALU.bypass, replica_groups=GROUPS,
                                                     ins=[agin3[hh].ap().opt()], outs=[agout3[hh].ap().opt()]).then_inc(ccsem.s, 1)
                        ccsem.n += 1
                    for e_ in (ACT, DVE, PE):
                        e_.wait([tg])
                    cct3 = (ccsem.s, ccsem.n)
                    barrier()

                if k.dsub < 4:
                    return
                with ExitStack() as ph2:
                    def psb2(name, shape, dt):
                        k.uid += 1
                        return ph2.enter_context(nc.sbuf_tensor(f"p_{name}_{k.uid}", list(shape), dt))
                    cand = [psb2(f"cand{i}", [128, 2, 4, 1024], BF16) for i in range(2)]
                    cand_r = regs(2)
                    for rr in range(4):
                        i = rr % 2
                        SP._deps((), [cand_r[i]], [cct3])
                        SP.wait(BAR["toks"])
                        for hh in range(2):
                            tok = dma(SP, cand[i][:, hh, :, :], agout3[hh].ap()[:, rr * 1024:(rr + 1) * 1024].rearrange("(k p) t -> p k t", p=128),
                                      usems[i])
                        cand_r[i].w = tok
                        cand_r[i].r = {}
                        for t in (1, 2):
                            for kk in range(KC):
                                src = cand[i][:, kk % 2, kk // 2, (t - 1) * 512:t * 512]
                                dstv = xn[:, kk, TILES[t][0]:TILES[t][1]]
                                if rr == 0:
                                    DVE.op("tensor_scalar", dstv, src, selt[:, 0:1], None, ALU.mult, reads=[cand_r[i], dn_r], writes=[xn_r[kk][t]])
                                else:
                                    DVE.op("scalar_tensor_tensor", dstv, src, selt[:, rr:rr + 1], dstv, ALU.mult, ALU.add,
                                           reads=[cand_r[i], dn_r], writes=[xn_r[kk][t]])
                    barrier()

                for blk in range(2):
                    s = wnext()
                    wv = v3(slot_view(s), 0, KC, 512)
                    for c4 in range(4):
                        dc = blk * 4 + c4
                        for t, (t0, t1_) in enumerate(TILES):
                            b = next_bank()
                            for kk in range(KC):
                                PE.op("matmul", banks[b][:, :], wv[:, kk, c4 * 128:(c4 + 1) * 128], xn[:, kk, t0:t1_],
                                      start=(kk == 0), stop=(kk == KC - 1), reads=[wring_r[s], xn_r[kk][t]],
                                      writes=[bank_r[b]] if kk == 0 else [], inc=(kk == KC - 1))
                            bank_r[b].w = (PE.sem, PE.cnt)
                            DVE.op("scalar_tensor_tensor", h[:, dc, t0:t1_], banks[b][:, :], coefG[:, l, 1, dc, cj(t):cj(t) + 1],
                                   h[:, dc, t0:t1_], ALU.mult, ALU.add, reads=[bank_r[b], mod_r], writes=[h_r[dc][t]])
                barrier()

        def final_out(dst_dram, normed=True):
            with ExitStack() as ph:
                def psb(name, shape, dt):
                    k.uid += 1
                    return ph.enter_context(nc.sbuf_tensor(f"p_{name}_{k.uid}", list(shape), dt))
                yo = psb("yo", [128, KC, NT], F32)
                yo_r = regs(KC, 3)
                sq = psb("sq", [128, 2, 512], BF16)
                rstd = psb("rstd", [128, 512], F32)
                tmp = psb("tmp", [128, 2, 512], F32)
                pool = (sq, regs(2), rstd, Reg(), tmp, regs(2))
                if normed:
                    modnorm(lambda kk, t: yo[:, kk, TILES[t][0]:TILES[t][1]], lambda kk, t: yo_r[kk][t],
                            lambda kk, t: gains[:, 6, kk:kk + 1], lambda kk, t: None, pool)
                    for kk in range(KC):
                        dma(SP, dst_dram[kk * 128:(kk + 1) * 128, :], yo[:, kk, :], osem, reads=yo_r[kk])
                else:
                    for kk in range(KC):
                        dma(SP, dst_dram[kk * 128:(kk + 1) * 128, :], h[:, kk, :], osem, reads=h_r[kk])
                for e in (SP, ACT, DVE, PE):
                    e.wait([(osem.s, osem.n)])
                barrier()

        do_mods_all()
        ffn(0, 0)
        if stage >= 2:
            attention(0)
        if stage >= 3:
            ffn(0, 1)
        if stage >= 4:
            ffn(1, 0)
        if stage >= 5:
            deltanet(1)
        if stage >= 6:
            ffn(1, 1)
        if dbg and k.dbgsrc == 'mod':
            dma(SP, O["dbg"][0:128, 0:288], modT[:].rearrange("p l c j -> p (l c j)"), osem, reads=[mod_r])
            dma(SP, O["dbg"][0:128, 288:384], coefA[:].rearrange("p l s k j -> p (l s k j)"), osem, reads=[mod_r])
            dma(SP, O["dbg"][0:128, 384:480], coefG[:].rearrange("p l s k j -> p (l s k j)"), osem, reads=[mod_r])
            for e_ in (ACT, DVE, PE):
                e_.wait([(osem.s, osem.n)])
        if dbg and k.dbgsrc == 'h':
            final_out(O["dbg"], normed=False)
        final_out(O["yT"], normed=True)
        assert k.w_used == len(plan), (k.w_used, len(plan))
        SP.wait([(osem.s, osem.n)])
    return nc


def _fm(v):
    return np.ascontiguousarray(v.reshape(KC, 128).T)


def make_inputs(core, inp):
    b, r = core // 4, core % 4
    f32 = np.float32
    xp = inp["x_prompt"][2 * core:2 * core + 2].reshape(512, D)
    xs = inp["x_sample"][b, r * 1024:(r + 1) * 1024]
    m = {}
    m["xT"] = np.ascontiguousarray(np.concatenate([xp, xs], axis=0).T)
    cond = np.stack([inp["c_ctx"], inp["c"][b]], axis=-1)
    m["condT"] = np.ascontiguousarray(cond.reshape(KC, 128, 2).transpose(1, 0, 2))
    m["ada_w"] = np.ascontiguousarray(inp["ada_w"][:, :, r * 2304:(r + 1) * 2304])
    m["adabT"] = np.ascontiguousarray(inp["ada_b"].reshape(2, 72, 128).transpose(0, 2, 1)[:, :, r * 18:(r + 1) * 18])
    gl = []
    for l in range(2):
        for nm in ("norm_ffn1", "norm_mix", "norm_ffn2"):
            gl.append(_fm(inp[nm][l]))
    gl.append(_fm(inp["final_norm"]))
    m["gainsT"] = np.ascontiguousarray(np.stack(gl, axis=1))
    m["ffn_w_in"] = np.ascontiguousarray(np.stack([inp["ffn1_w_in"][0], inp["ffn2_w_in"][0], inp["ffn1_w_in"][1], inp["ffn2_w_in"][1]]))
    m["ffn_w_out"] = np.ascontiguousarray(np.stack([inp["ffn1_w_out"][0], inp["ffn2_w_out"][0], inp["ffn1_w_out"][1], inp["ffn2_w_out"][1]]))
    m["ident"] = np.eye(128, dtype=f32)
    m["attn_w_qkv"] = inp["attn_w_qkv"][0]
    m["attn_w_o"] = inp["attn_w_o"][0]
    m["qkg"] = np.ascontiguousarray(np.stack([inp["attn_q_norm"][0], inp["attn_k_norm"][0]], axis=1))
    m["ckT"] = np.ascontiguousarray(inp["cache_k"][b, 0].transpose(2, 1, 0))
    m["cv"] = np.ascontiguousarray(inp["cache_v"][b, 0].reshape(4, 128, 256).transpose(1, 0, 2))
    C, S, Rm = rope_consts(r)
    m["ropeC"], m["ropeS"], m["rmat"] = C, S, Rm
    hs_ = [2 * r, 2 * r + 1]
    w_in = inp["dn_w_in"][0]
    m["dn_w_in"] = w_in
    m["dn_w_in_own"] = np.ascontiguousarray(np.concatenate(
        [w_in[:, ty * 1024 + hh * 128: ty * 1024 + (hh + 1) * 128] for hh in hs_ for ty in range(4)], axis=1))
    convT = np.ascontiguousarray(inp["dn_conv"][0].reshape(3, 24, 128).transpose(2, 1, 0))
    m["convT"] = convT
    m["convT_own"] = np.ascontiguousarray(np.stack([convT[:, ty * 8 + hh, :] for hh in hs_ for ty in range(3)], axis=1))
    wa, wb = inp["dn_w_a"][0], inp["dn_w_b"][0]
    m["wab_p"] = np.ascontiguousarray(np.concatenate([wa[0], wa[1], wb[0], wb[1]], axis=1))
    m["wab_o"] = np.ascontiguousarray(np.stack([wa[0][:, hs_[0]], wa[0][:, hs_[1]], wa[1][:, hs_[0]], wa[1][:, hs_[1]],
                                                wb[0][:, hs_[0]], wb[0][:, hs_[1]], wb[1][:, hs_[0]], wb[1][:, hs_[1]]], axis=1))
    dtb, alog = inp["dn_dt_bias"][0], inp["dn_a_log"][0]
    m["dtb_p"] = np.ascontiguousarray(dtb.reshape(16))
    m["alog_p"] = np.ascontiguousarray(alog.reshape(16))
    m["dtb_o"] = np.ascontiguousarray(np.array([dtb[0, hs_[0]], dtb[0, hs_[1]], dtb[1, hs_[0]], dtb[1, hs_[1]]], f32))
    m["alog_o"] = np.ascontiguousarray(np.array([alog[0, hs_[0]], alog[0, hs_[1]], alog[1, hs_[0]], alog[1, hs_[1]]], f32))
    m["ong"] = np.ascontiguousarray(inp["dn_out_norm"][0].reshape(128, 1))
    m["dn_w_o"] = inp["dn_w_o"][0]
    m["s0f"] = np.ascontiguousarray(inp["state_fwd"][b, 0, hs_].transpose(1, 0, 2))
    m["s0b"] = np.ascontiguousarray(inp["state_bwd"][b, 0, hs_].transpose(1, 0, 2))
    sel = np.zeros(4, f32)
    sel[r] = 1.0
    m["sel"] = sel
    p_ = np.arange(128)[:, None]
    j_ = np.arange(128)[None, :]
    m["masks"] = np.ascontiguousarray(np.stack([p_ <= j_, p_ >= j_, p_ > j_, p_ < j_], axis=1).astype(f32))
    lm = []
    for li in range(7):
        mm_ = 1 << li
        lm.append((p_ // (2 * mm_) == j_ // (2 * mm_)) & (p_ % (2 * mm_) >= mm_) & (j_ % (2 * mm_) < mm_))
    m["lmask"] = np.ascontiguousarray(np.stack(lm, axis=1).astype(f32))
    return m


def rope_consts(r):
    pos = np.arange(r * 1024, (r + 1) * 1024)
    row = (pos // 64).astype(np.float32)
    col = (pos % 64).astype(np.float32)
    nf = 32
    inv = (np.float32(10000.0) ** (-np.arange(nf, dtype=np.float32) / nf)).astype(np.float32)
    d = np.arange(128)
    axis = d // 64
    f = d % 32
    p = np.where(axis[:, None] == 0, row[None, :], col[None, :]).astype(np.float32)
    ang = (p * inv[f][:, None]).astype(np.float32)
    C = np.cos(ang).astype(np.float32)
    S = np.sin(ang).astype(np.float32)
    Rm = np.zeros((128, 128), np.float32)
    for m_ in range(128):
        if (m_ % 64) < 32:
            Rm[m_ + 32, m_] = -1.0
        else:
            Rm[m_ - 32, m_] = 1.0
    return C, S, Rm


_CACHE = {}


SHARED = ("ffn_w_in", "ffn_w_out", "ident", "attn_w_qkv", "attn_w_o", "dn_w_in", "dn_w_o", "convT", "wab_p",
          "dtb_p", "alog_p", "ong", "masks", "lmask", "gainsT", "qkg", "rmat")


def make_all_inputs(inp):
    in_maps = []
    for c in range(8):
        m = make_inputs(c, inp)
        if in_maps:
            for kk in SHARED:
                m[kk] = in_maps[0][kk]
        in_maps.append(m)
    return in_maps


def assemble(results):
    f32 = np.float32
    y_p = np.zeros((16, 256, D), f32)
    y_s = np.zeros((2, 4096, D), f32)
    nk = np.zeros((16, 1, 256, 2, 128), f32)
    nv = np.zeros((16, 1, 256, 2, 128), f32)
    sf = np.zeros((16, 1, 8, 128, 128), f32)
    sb = np.zeros((16, 1, 8, 128, 128), f32)
    for c in range(8):
        b, r = c // 4, c % 4
        res = results[c]
        y = np.asarray(res["yT"]).T
        y_p[2 * c:2 * c + 2] = y[:512].reshape(2, 256, D)
        y_s[b, r * 1024:(r + 1) * 1024] = y[512:]
        ko = np.asarray(res["kout"])
        vo = np.asarray(res["vout"]).transpose(1, 0, 2).reshape(512, 2, 128)
        for s2 in range(2):
            nk[2 * c + s2, 0] = ko[:, :, s2 * 256:(s2 + 1) * 256].transpose(2, 1, 0)
            nv[2 * c + s2, 0] = vo[s2 * 256:(s2 + 1) * 256]
        sf[2 * c:2 * c + 2, 0] = np.asarray(res["sfo"]).transpose(1, 0, 2).reshape(2, 8, 128, 128)
        sb[2 * c:2 * c + 2, 0] = np.asarray(res["sbo"]).transpose(1, 0, 2).reshape(2, 8, 128, 128)
    return (y_p, y_s, nk, nv, sf, sb)


def kernel(**inputs):
    inp = {k_: np.asarray(v) for k_, v in inputs.items()}
    nc = build_program()
    in_maps = make_all_inputs(inp)
    res = run_bass_kernel_spmd(nc, in_maps, core_ids=list(range(8)))
    return assemble(res.results)
```

```python
import numpy as np
from contextlib import ExitStack
import concourse.bass as bass
import concourse.mybir as mybir
from concourse.bass_utils import run_bass_kernel_spmd

F32 = mybir.dt.float32
BF16 = mybir.dt.bfloat16
AF = mybir.ActivationFunctionType
ALU = mybir.AluOpType

D = 1024
KC = 8
NT = 1536
TILES = [(0, 512), (512, 1024), (1024, 1536)]
DFF = 2816
FC = 22
EPS = 1e-6
GROUPS = [[0, 1, 2, 3], [4, 5, 6, 7]]
SLOT = 4096
NSLOT = 3


class Reg:
    __slots__ = ("w", "r", "excl")

    def __init__(self, excl=False):
        self.w = None
        self.r = {}
        self.excl = excl


def regs(*shape):
    if len(shape) == 1:
        return [Reg() for _ in range(shape[0])]
    return [regs(*shape[1:]) for _ in range(shape[0])]


class Eng:
    def __init__(self, h, sem):
        self.h = h
        self.sem = sem
        self.cnt = 0
        self.waited = {}

    def wait(self, toks):
        best = {}
        for t in toks:
            if t is None:
                continue
            if id(t[0]) not in best or best[id(t[0])][1] < t[1]:
                best[id(t[0])] = t
        for t in best.values():
            sem, val = t
            k = id(sem)
            if sem is self.sem and val > self.cnt:
                continue
            if self.waited.get(k, 0) < val:
                self.h.wait_ge(sem, val)
                self.waited[k] = val

    def _deps(self, reads, writes, deps):
        toks = list(deps)
        for R in reads:
            toks.append(R.w)
            if R.excl:
                toks.extend(R.r.values())
        for R in writes:
            toks.append(R.w)
            toks.extend(R.r.values())
        self.wait(toks)

    @staticmethod
    def _upd(tok, reads, writes):
        for R in reads:
            if R.excl:
                R.w = tok
                R.r = {}
                continue
            k = id(tok[0])
            if k not in R.r or R.r[k][1] < tok[1]:
                R.r[k] = tok
        for R in writes:
            R.w = tok
            R.r = {}

    def op(self, name, *a, reads=(), writes=(), deps=(), inc=True, **kw):
        self._deps(reads, writes, deps)
        ins = getattr(self.h, name)(*a, **kw)
        if inc:
            self.cnt += 1
            ins.then_inc(self.sem, 1)
            tok = (self.sem, self.cnt)
        else:
            tok = (self.sem, self.cnt + 1)
        self._upd(tok, reads, writes)
        return tok


class DSem:
    def __init__(self, s):
        self.s = s
        self.n = 0


BAR = {"toks": []}


def dma(q, out, in_, ds, reads=(), writes=(), deps=(), phase=True, **kw):
    if phase:
        q.wait(BAR["toks"])
    q._deps(reads, writes, deps)
    q.h.dma_start(out=out, in_=in_, **kw).then_inc(ds.s, 16)
    ds.n += 16
    tok = (ds.s, ds.n)
    Eng._upd(tok, reads, writes)
    return tok


class K:
    pass


def build_program(stage=99, dbg=False, sub=9, dbgsrc='h', dsub=9):
    nc = bass.Bass("TRN2", target_bir_lowering=False)
    BAR["toks"] = []
    k = K()
    k.nc = nc
    k.stage = stage
    k.uid = 0
    k.sub = sub
    k.dsub = dsub
    k.dbgsrc = dbgsrc

    def din(name, shape, dt=F32):
        return nc.dram_tensor(name, list(shape), dt, kind="ExternalInput").ap()

    def dout(name, shape, dt=F32):
        return nc.dram_tensor(name, list(shape), dt, kind="ExternalOutput").ap()

    I = {}
    I["xT"] = din("xT", [D, NT])
    I["condT"] = din("condT", [128, KC, 2])
    I["ada_w"] = din("ada_w", [2, D, 2304])
    I["adabT"] = din("adabT", [2, 128, 18])
    I["gainsT"] = din("gainsT", [128, 7, KC])
    I["ffn_w_in"] = din("ffn_w_in", [4, D, 2 * DFF])
    I["ffn_w_out"] = din("ffn_w_out", [4, DFF, D])
    I["ident"] = din("ident", [128, 128])
    I["attn_w_qkv"] = din("attn_w_qkv", [D, 1536])
    I["attn_w_o"] = din("attn_w_o", [D, D])
    I["qkg"] = din("qkg", [128, 2])
    I["ckT"] = din("ckT", [128, 2, 512])
    I["cv"] = din("cv", [128, 4, 256])
    I["ropeC"] = din("ropeC", [128, 1024])
    I["ropeS"] = din("ropeS", [128, 1024])
    I["rmat"] = din("rmat", [128, 128])
    I["dn_w_in"] = din("dn_w_in", [D, 4096])
    I["dn_w_in_own"] = din("dn_w_in_own", [D, 1024])
    I["convT"] = din("convT", [128, 24, 3])
    I["convT_own"] = din("convT_own", [128, 6, 3])
    I["wab_p"] = din("wab_p", [D, 32])
    I["wab_o"] = din("wab_o", [D, 8])
    I["dtb_p"] = din("dtb_p", [16])
    I["alog_p"] = din("alog_p", [16])
    I["dtb_o"] = din("dtb_o", [4])
    I["alog_o"] = din("alog_o", [4])
    I["ong"] = din("ong", [128, 1])
    I["dn_w_o"] = din("dn_w_o", [D, D])
    I["s0f"] = din("s0f", [128, 2, 128])
    I["s0b"] = din("s0b", [128, 2, 128])
    I["sel"] = din("sel", [4])
    I["masks"] = din("masks", [128, 4, 128])
    I["lmask"] = din("lmask", [128, 7, 128])
    O = {}
    O["yT"] = dout("yT", [D, NT])
    O["kout"] = dout("kout", [128, 2, 512])
    O["vout"] = dout("vout", [128, 4, 256])
    O["sfo"] = dout("sfo", [128, 16, 128])
    O["sbo"] = dout("sbo", [128, 16, 128])
    agin2 = [nc.dram_tensor(f"agin2_{x}", [512, 1024], BF16) for x in range(2)]
    agout2 = [nc.dram_tensor(f"agout2_{x}", [2048, 1024], BF16) for x in range(2)]
    agin3 = [nc.dram_tensor(f"agin3_{x}", [128, 4096], BF16) for x in range(2)]
    agout3 = [nc.dram_tensor(f"agout3_{x}", [512, 4096], BF16) for x in range(2)]
    agin1 = nc.dram_tensor("agin1", [256, 2048], BF16)
    agin0 = nc.dram_tensor("agin0", [128, 72], F32)
    agout0 = nc.dram_tensor("agout0", [512, 72], F32)
    agout1 = nc.dram_tensor("agout1", [1024, 2048], BF16)
    if dbg:
        O["dbg"] = dout("dbg", [D, NT])
    k.I, k.O = I, O

    es = ExitStack()
    with es:
        def sb(name, shape, dt):
            return es.enter_context(nc.sbuf_tensor("t_" + name, list(shape), dt))

        def ps(name, shape, dt):
            return es.enter_context(nc.psum_tensor(name, list(shape), dt))

        def sem(name):
            return es.enter_context(nc.semaphore(name))

        PE = Eng(nc.tensor, sem("s_pe"))
        ACT = Eng(nc.scalar, sem("s_act"))
        DVE = Eng(nc.vector, sem("s_dve"))
        POOL = Eng(nc.gpsimd, sem("s_pool"))
        SP = Eng(nc.sync, sem("s_sp"))
        k.PE, k.ACT, k.DVE, k.POOL, k.SP = PE, ACT, DVE, POOL, SP
        csem = DSem(sem("csem"))
        osem = DSem(sem("osem"))
        wsems = [DSem(sem(f"wsem{i}")) for i in range(NSLOT)]
        asem = DSem(sem("asem"))
        asem2 = DSem(sem("asem2"))
        ccsem = DSem(sem("ccsem"))
        dsem = DSem(sem("dsem"))
        usems = [DSem(sem(f"usem{i}")) for i in range(3)]
        k.dsems = [csem, osem, asem, asem2, ccsem, dsem] + usems

        def barrier():
            engs = [PE, ACT, DVE, POOL]
            dtoks = [(d_.s, d_.n) for d_ in k.dsems if d_.n > 0]
            for e in (PE, ACT, DVE):
                e.wait([(f.sem, f.cnt) for f in engs if f is not e and f.cnt > 0] + dtoks)
            BAR["toks"] = [(f.sem, f.cnt) for f in engs if f.cnt > 0] + dtoks

        h = sb("h", [128, KC, NT], F32)
        h_r = regs(KC, 3)
        xn = sb("xn", [128, KC, NT], BF16)
        xn_r = regs(KC, 3)
        wring = sb("wring", [128, NSLOT, SLOT], BF16)
        wring_r = regs(NSLOT)
        ones_bf = sb("ones_bf", [128, 128], BF16)
        ones_f = sb("ones_f", [128, 128], F32)
        ident_f = sb("ident_f", [128, 128], F32)
        ident_bf = sb("ident_bf", [128, 128], BF16)
        epsc = sb("epsc", [128, 1], F32)
        onec = sb("onec", [128, 1], F32)
        condT = sb("condT", [128, KC, 2], F32)
        scT = sb("scT", [128, KC, 2], BF16)
        adab = sb("adab", [128, 2, 18], F32)
        modloc = sb("modloc", [128, 2, 18, 2], F32)
        gains = sb("gains", [128, 7, KC], F32)
        modT = sb("modT", [128, 2, 72, 2], F32)
        coefA = sb("coefA", [128, 2, 3, KC, 2], F32)
        coefG = sb("coefG", [128, 2, 3, KC, 2], F32)
        qkg = sb("qkg", [128, 2], F32)
        rmat_f = sb("rmat_f", [128, 128], F32)
        rmat_bf = sb("rmat_bf", [128, 128], BF16)
        const_r = Reg()
        mod_r = Reg()
        k.coef_r = Reg()
        banks = [ps(f"bank{i}", [128, 512], F32) for i in range(8)]
        bank_r = [Reg(excl=True) for _ in range(8)]
        k.bank_i = 0
        k.nring = 6
        k.att_i = 0

        def next_bank():
            b = k.bank_i
            k.bank_i = (b + 1) % k.nring
            return b

        DVE.op("memset", ones_bf[:], 1.0, writes=[const_r])
        DVE.op("memset", ones_f[:], 1.0, writes=[const_r])
        DVE.op("memset", epsc[:], EPS, writes=[const_r])
        DVE.op("memset", onec[:], 1.0, writes=[const_r])
        dma(SP, ident_f[:], I["ident"], csem)
        dma(SP, condT[:], I["condT"], csem)
        dma(SP, adab[:], I["adabT"].rearrange("l p c -> p l c"), csem)
        dma(SP, gains[:], I["gainsT"], csem)
        dma(SP, qkg[:], I["qkg"], csem)
        dma(SP, rmat_f[:], I["rmat"], csem)
        for kk in range(KC):
            dma(SP, h[:, kk, :], I["xT"][kk * 128:(kk + 1) * 128, :], csem)
        ctok = (csem.s, csem.n)
        const_r.w = None
        DVE.op("tensor_copy", ident_bf[:], ident_f[:], deps=[ctok], writes=[const_r])
        DVE.op("tensor_copy", rmat_bf[:], rmat_f[:], deps=[ctok], writes=[const_r])
        ACT.op("activation", scT[:], condT[:], AF.Silu, deps=[ctok], writes=[const_r])
        for kk in range(KC):
            for t in range(3):
                h_r[kk][t].w = ctok

        plan = []
        k.w_issued = 0
        k.w_used = 0
        slot_use_tok = [None] * NSLOT

        def slot_view(s):
            return wring[:, s, :]

        def issue_next():
            n = k.w_issued
            if n >= len(plan):
                return
            s = n % NSLOT
            POOL._deps((), [wring_r[s]], ())
            for (dst_fn, src) in plan[n]:
                tok = dma(POOL, dst_fn(slot_view(s)), src, wsems[s], phase=False)
            wring_r[s].w = tok
            wring_r[s].r = {}
            k.w_issued += 1

        def wnext():
            n = k.w_used
            while k.w_issued < min(len(plan), n + NSLOT):
                issue_next()
            k.w_used += 1
            return n % NSLOT

        def v3(slot_ap, off, a, c):
            return slot_ap[:, off:off + a * c].rearrange("p (a c) -> p a c", a=a)

        def plan_mods(l):
            for blk in range(6):
                src = I["ada_w"][l, :, blk * 384:(blk + 1) * 384].rearrange("(k p) c -> p k c", p=128)
                plan.append([(lambda sl: v3(sl, 0, KC, 384), src)])

        def plan_ffn(li):
            for j in range(11):
                sa = I["ffn_w_in"][li, :, j * 256:(j + 1) * 256].rearrange("(k p) c -> p k c", p=128)
                sb_ = I["ffn_w_in"][li, :, DFF + j * 256:DFF + (j + 1) * 256].rearrange("(k p) c -> p k c", p=128)
                plan.append([(lambda sl: v3(sl, 0, KC, 256), sa), (lambda sl: v3(sl, 2048, KC, 256), sb_)])
            for dc in range(8):
                so = I["ffn_w_out"][li, :, dc * 128:(dc + 1) * 128].rearrange("(f p) c -> p f c", p=128)
                plan.append([(lambda sl: v3(sl, 0, FC, 128), so)])

        def plan_attn():
            for rep in range(2 if k.sub >= 2 else 1):
                for blk in range(3):
                    src = I["attn_w_qkv"][:, blk * 512:(blk + 1) * 512].rearrange("(k p) c -> p k c", p=128)
                    plan.append([(lambda sl: v3(sl, 0, KC, 512), src)])
            for blk in range(2 if k.sub >= 5 else 0):
                src = I["attn_w_o"][:, blk * 512:(blk + 1) * 512].rearrange("(k p) c -> p k c", p=128)
                plan.append([(lambda sl: v3(sl, 0, KC, 512), src)])

        def plan_dn():
            for blk in range(8 if k.dsub >= 2 else 0):
                src = I["dn_w_in"][:, blk * 512:(blk + 1) * 512].rearrange("(k p) c -> p k c", p=128)
                plan.append([(lambda sl: v3(sl, 0, KC, 512), src)])
            for hh in range(2 if k.dsub >= 3 else 0):
                src = I["dn_w_in_own"][:, hh * 512:(hh + 1) * 512].rearrange("(k p) c -> p k c", p=128)
                plan.append([(lambda sl: v3(sl, 0, KC, 512), src)])
            for blk in range(2 if k.dsub >= 4 else 0):
                src = I["dn_w_o"][:, blk * 512:(blk + 1) * 512].rearrange("(k p) c -> p k c", p=128)
                plan.append([(lambda sl: v3(sl, 0, KC, 512), src)])

        plan_mods(0)
        plan_mods(1)
        plan_ffn(0)
        if stage >= 2:
            plan_attn()
        if stage >= 3:
            plan_ffn(1)
        if stage >= 4:
            plan_ffn(2)
        if stage >= 5:
            plan_dn()
        if stage >= 6:
            plan_ffn(3)

        def do_mods_all():
            b = next_bank()
            for l in range(2):
                for blk in range(6):
                    s = wnext()
                    wv = v3(slot_view(s), 0, KC, 384)
                    for c3 in range(3):
                        cc = l * 18 + blk * 3 + c3
                        for kk in range(KC):
                            PE.op("matmul", banks[b][:, cc * 2:cc * 2 + 2], wv[:, kk, c3 * 128:(c3 + 1) * 128], scT[:, kk, :],
                                  start=(kk == 0), stop=(kk == KC - 1),
                                  reads=[wring_r[s], const_r], writes=[bank_r[b]] if (kk == 0 and cc == 0) else [],
                                  inc=(kk == KC - 1 and c3 == 2))
            bank_r[b].w = (PE.sem, PE.cnt)
            pv = banks[b][:, 0:72].rearrange("p (l c j) -> p l c j", l=2, j=2)
            for j in range(2):
                DVE.op("tensor_tensor", modloc[:, :, :, j], pv[:, :, :, j], adab[:, :, :], ALU.add,
                       reads=[bank_r[b], const_r], writes=[mod_r])
            t0_ = dma(SP, agin0.ap(), modloc[:].rearrange("p l c j -> p (l c j)"), asem, reads=[mod_r])
            POOL.wait([t0_])
            nc.gpsimd.collective_compute("AllGather", ALU.bypass, replica_groups=GROUPS,
                                         ins=[agin0.ap().opt()], outs=[agout0.ap().opt()]).then_inc(ccsem.s, 1)
            ccsem.n += 1
            cct0 = (ccsem.s, ccsem.n)
            SP._deps((), [mod_r], [cct0])
            for rr in range(4):
                tok = dma(SP, modT[:, :, 18 * rr:18 * rr + 18, :],
                          agout0.ap()[rr * 128:(rr + 1) * 128, :].rearrange("p (l c j) -> p l c j", l=2, j=2), asem)
            mod_r.w = tok
            mod_r.r = {}
            for l in range(2):
                for s3 in range(3):
                    for j in range(2):
                        DVE.op("scalar_tensor_tensor", coefA[:, l, s3, :, j], modT[:, l, (3 * s3 + 1) * 8:(3 * s3 + 2) * 8, j],
                               1.0, gains[:, l * 3 + s3, :], ALU.add, ALU.mult, reads=[const_r, mod_r], writes=[k.coef_r])
                    DVE.op("tensor_scalar", coefG[:, l, s3, :, :], modT[:, l, (3 * s3 + 2) * 8:(3 * s3 + 3) * 8, :],
                           (1.0 if s3 == 1 else 0.5), None, ALU.mult, reads=[mod_r], writes=[k.coef_r])
            mod_r.w = (DVE.sem, DVE.cnt)

        def modnorm(dst, dst_r, coefA_fn, coefB_fn, tmp_pool):
            sq, sq_r, rstd, rstd_r, tmp, tmp_r = tmp_pool
            for t, (t0, t1) in enumerate(TILES):
                b = next_bank()
                for kk in range(KC):
                    i = kk % 2
                    ACT.op("activation", sq[:, i, :], h[:, kk, t0:t1], AF.Square, reads=[h_r[kk][t]], writes=[sq_r[i]])
                    PE.op("matmul", banks[b][:, :], ones_bf[:], sq[:, i, :], start=(kk == 0), stop=(kk == KC - 1),
                          reads=[sq_r[i], const_r], writes=[bank_r[b]] if kk == 0 else [], inc=True)
                bank_r[b].w = (PE.sem, PE.cnt)
                ACT.op("activation", rstd[:, :], banks[b][:, :], AF.Ln, bias=epsc[:, 0:1], scale=1.0 / D,
                       reads=[bank_r[b], const_r], writes=[rstd_r])
                ACT.op("activation", rstd[:, :], rstd[:, :], AF.Exp, scale=-0.5, reads=[rstd_r], writes=[rstd_r])
                for kk in range(KC):
                    i = kk % 2
                    A = coefA_fn(kk, t)
                    B = coefB_fn(kk, t)
                    if B is None:
                        DVE.op("scalar_tensor_tensor", dst(kk, t), h[:, kk, t0:t1], A, rstd[:, :], ALU.mult, ALU.mult,
                               reads=[h_r[kk][t], rstd_r, mod_r, const_r], writes=[dst_r(kk, t)])
                    else:
                        DVE.op("scalar_tensor_tensor", tmp[:, i, :], h[:, kk, t0:t1], A, rstd[:, :], ALU.mult, ALU.mult,
                               reads=[h_r[kk][t], rstd_r, mod_r], writes=[tmp_r[i]])
                        DVE.op("tensor_scalar", dst(kk, t), tmp[:, i, :], B, None, ALU.add,
                               reads=[tmp_r[i], mod_r], writes=[dst_r(kk, t)])

        def cj(t):
            return 0 if t == 0 else 1

        def ffn(l, which):
            li = l * 2 + which
            s3 = 0 if which == 0 else 2
            with ExitStack() as ph:
                def psb(name, shape, dt):
                    k.uid += 1
                    return ph.enter_context(nc.sbuf_tensor(f"p_{name}_{k.uid}", list(shape), dt))
                g = psb("g", [128, FC, NT], BF16)
                g_r = regs(FC, 3)
                sq = psb("sq", [128, 2, 512], BF16)
                rstd = psb("rstd", [128, 512], F32)
                tmp = psb("tmp", [128, 2, 512], F32)
                sa = psb("sa", [128, 2, 512], F32)
                sa_r = regs(2)
                pool = (sq, regs(2), rstd, Reg(), tmp, regs(2))
                modnorm(lambda kk, t: xn[:, kk, TILES[t][0]:TILES[t][1]], lambda kk, t: xn_r[kk][t],
                        lambda kk, t: coefA[:, l, s3, kk, cj(t):cj(t) + 1],
                        lambda kk, t: modT[:, l, (3 * s3) * 8 + kk, cj(t):cj(t) + 1], pool)
                n = 0
                for j in range(11):
                    s = wnext()
                    wa = v3(slot_view(s), 0, KC, 256)
                    wb = v3(slot_view(s), 2048, KC, 256)
                    for fl in range(2):
                        f = 2 * j + fl
                        for t, (t0, t1) in enumerate(TILES):
                            ba = next_bank()
                            for kk in range(KC):
                                PE.op("matmul", banks[ba][:, :], wa[:, kk, fl * 128:(fl + 1) * 128], xn[:, kk, t0:t1],
                                      start=(kk == 0), stop=(kk == KC - 1), reads=[wring_r[s], xn_r[kk][t]],
                                      writes=[bank_r[ba]] if kk == 0 else [], inc=(kk == KC - 1))
                            bank_r[ba].w = (PE.sem, PE.cnt)
                            bb = next_bank()
                            for kk in range(KC):
                                PE.op("matmul", banks[bb][:, :], wb[:, kk, fl * 128:(fl + 1) * 128], xn[:, kk, t0:t1],
                                      start=(kk == 0), stop=(kk == KC - 1), reads=[wring_r[s], xn_r[kk][t]],
                                      writes=[bank_r[bb]] if kk == 0 else [], inc=(kk == KC - 1))
                            bank_r[bb].w = (PE.sem, PE.cnt)
                            i = n % 2
                            n += 1
                            ACT.op("activation", sa[:, i, :], banks[ba][:, :], AF.Silu, reads=[bank_r[ba]], writes=[sa_r[i]])
                            DVE.op("tensor_tensor", g[:, f, t0:t1], sa[:, i, :], banks[bb][:, :], ALU.mult,
                                   reads=[sa_r[i], bank_r[bb]], writes=[g_r[f][t]])
                for dc in range(8):
                    s = wnext()
                    wo = v3(slot_view(s), 0, FC, 128)
                    if True:
                        for t, (t0, t1) in enumerate(TILES):
                            b = next_bank()
                            for f in range(FC):
                                PE.op("matmul", banks[b][:, :], wo[:, f, :], g[:, f, t0:t1],
                                      start=(f == 0), stop=(f == FC - 1), reads=[wring_r[s], g_r[f][t]],
                                      writes=[bank_r[b]] if f == 0 else [], inc=(f == FC - 1))
                            bank_r[b].w = (PE.sem, PE.cnt)
                            DVE.op("scalar_tensor_tensor", h[:, dc, t0:t1], banks[b][:, :], coefG[:, l, s3, dc, cj(t):cj(t) + 1],
                                   h[:, dc, t0:t1], ALU.mult, ALU.add, reads=[bank_r[b], mod_r], writes=[h_r[dc][t]])
                barrier()


        SM_SCALE = 128.0 ** -0.5

        def attend(q_ap, q_regs, nq, chunks, out_ap, out_reg, P, P_r, rec, rec_r):
            n = len(chunks)
            LA = 3
            st = {}
            ab = 6 if k.att_i % 2 == 0 else 4
            k.att_i += 1

            def emit_s(i):
                KT, kreg, V, vreg = chunks[i]
                b = next_bank()
                PE.op("matmul", banks[b][:, 0:nq], KT, q_ap, start=True, stop=True,
                      reads=[kreg] + q_regs, writes=[bank_r[b]])
                pi = k.p_i % 6
                k.p_i += 1
                ACT.op("activation", P[:, pi, 0:nq], banks[b][:, 0:nq], AF.Exp, scale=SM_SCALE,
                       reads=[bank_r[b]], writes=[P_r[pi]])
                st[i] = pi

            def emit_pv(i):
                KT, kreg, V, vreg = chunks[i]
                pi = st[i]
                PE.op("matmul", banks[ab][:, 0:nq], V, P[:, pi, 0:nq], start=(i == 0), stop=(i == n - 1),
                      reads=[P_r[pi], vreg], writes=[bank_r[ab]] if i == 0 else [], inc=False)
                PE.op("matmul", banks[ab + 1][:, 0:nq], ones_bf[:], P[:, pi, 0:nq], start=(i == 0), stop=(i == n - 1),
                      reads=[P_r[pi], const_r], writes=[bank_r[ab + 1]] if i == 0 else [], inc=True)

            for i in range(min(LA, n)):
                emit_s(i)
            for i in range(n):
                emit_pv(i)
                if i + LA < n:
                    emit_s(i + LA)
            bank_r[ab].w = (PE.sem, PE.cnt)
            bank_r[ab + 1].w = (PE.sem, PE.cnt)
            DVE.op("reciprocal", rec[:, 0:nq], banks[ab + 1][:, 0:nq], reads=[bank_r[ab + 1]], writes=[rec_r])
            DVE.op("tensor_tensor", out_ap, banks[ab][:, 0:nq], rec[:, 0:nq], ALU.mult,
                   reads=[bank_r[ab], rec_r], writes=[out_reg])

        def attention(l):
            k.p_i = 0
            k.nring = 4
            k.bank_i = 0
            with ExitStack() as ph:
                def psb(name, shape, dt):
                    k.uid += 1
                    return ph.enter_context(nc.sbuf_tensor(f"p_{name}_{k.uid}", list(shape), dt))
                sq = psb("sq", [128, 2, 512], BF16)
                sq_r = regs(2)
                rstd = psb("rstd", [128, 512], F32)
                rstd_r = Reg()
                tmp = psb("tmp", [128, 2, 512], F32)
                pool = (sq, sq_r, rstd, rstd_r, tmp, regs(2))
                modnorm(lambda kk, t: xn[:, kk, TILES[t][0]:TILES[t][1]], lambda kk, t: xn_r[kk][t],
                        lambda kk, t: coefA[:, l, 1, kk, cj(t):cj(t) + 1],
                        lambda kk, t: modT[:, l, 3 * 8 + kk, cj(t):cj(t) + 1], pool)
                P = psb("P", [128, 6, 512], BF16)
                P_r = regs(6)
                rec = psb("rec", [128, 512], F32)
                rec_r = Reg()
                rs2 = psb("rs2", [128, 512], F32)
                rs2_r = Reg()

                def headnorm(b, gcol):
                    ACT.op("activation", sq[:, 0, :], banks[b][:, :], AF.Square, reads=[bank_r[b]], writes=[sq_r[0]])
                    b2 = next_bank()
                    PE.op("matmul", banks[b2][:, :], ones_bf[:], sq[:, 0, :], start=True, stop=True,
                          reads=[sq_r[0], const_r], writes=[bank_r[b2]])
                    ACT.op("activation", rs2[:, :], banks[b2][:, :], AF.Ln, bias=epsc[:, 0:1], scale=1.0 / 128,
                           reads=[bank_r[b2], const_r], writes=[rs2_r])
                    ACT.op("activation", rs2[:, :], rs2[:, :], AF.Exp, scale=-0.5, reads=[rs2_r], writes=[rs2_r])

                def proj_fm(wv, c4, t, s):
                    t0, t1 = TILES[t]
                    b = next_bank()
                    for kk in range(KC):
                        PE.op("matmul", banks[b][:, :], wv[:, kk, c4 * 128:(c4 + 1) * 128], xn[:, kk, t0:t1],
                              start=(kk == 0), stop=(kk == KC - 1), reads=[wring_r[s], xn_r[kk][t]],
                              writes=[bank_r[b]] if kk == 0 else [], inc=(kk == KC - 1))
                    bank_r[b].w = (PE.sem, PE.cnt)
                    return b

                with ExitStack() as ph2:
                    def psb2(name, shape, dt):
                        k.uid += 1
                        return ph2.enter_context(nc.sbuf_tensor(f"p_{name}_{k.uid}", list(shape), dt))
                    qTp = psb2("qTp", [128, 8, 512], BF16)
                    qTp_r = regs(8)
                    kTp = psb2("kTp", [128, 2, 512], BF16)
                    kTp_r = regs(2)
                    Vp = psb2("Vp", [128, 4, 256], BF16)
                    Vp_r = regs(4)
                    koutf = psb2("koutf", [128, 2, 512], F32)
                    koutf_r = regs(2)
                    voutf = psb2("voutf", [128, 4, 256], F32)
                    voutf_r = regs(4)
                    for blk in range(3):
                        s = wnext()
                        wv = v3(slot_view(s), 0, KC, 512)
                        for c4 in range(4):
                            ch = blk * 4 + c4
                            if ch < 10:
                                b = proj_fm(wv, c4, 0, s)
                                headnorm(b, None)
                                if ch < 8:
                                    DVE.op("scalar_tensor_tensor", qTp[:, ch, :], banks[b][:, :], qkg[:, 0:1], rs2[:, :],
                                           ALU.mult, ALU.mult, reads=[bank_r[b], rs2_r, const_r], writes=[qTp_r[ch]])
                                else:
                                    DVE.op("scalar_tensor_tensor", koutf[:, ch - 8, :], banks[b][:, :], qkg[:, 1:2], rs2[:, :],
                                           ALU.mult, ALU.mult, reads=[bank_r[b], rs2_r, const_r], writes=[koutf_r[ch - 8]])
                                    ACT.op("activation", kTp[:, ch - 8, :], koutf[:, ch - 8, :], AF.Copy,
                                           reads=[koutf_r[ch - 8]], writes=[kTp_r[ch - 8]])
                            elif ch == 10:
                                for c in range(4):
                                    b = next_bank()
                                    for kk in range(KC):
                                        PE.op("matmul", banks[b][:, 0:256], xn[:, kk, c * 128:(c + 1) * 128], wv[:, kk, 256:512],
                                              start=(kk == 0), stop=(kk == KC - 1), reads=[wring_r[s], xn_r[kk][0]],
                                              writes=[bank_r[b]] if kk == 0 else [], inc=(kk == KC - 1))
                                    bank_r[b].w = (PE.sem, PE.cnt)
                                    ACT.op("activation", voutf[:, c, :], banks[b][:, 0:256], AF.Copy, reads=[bank_r[b]], writes=[voutf_r[c]])
                                    DVE.op("tensor_copy", Vp[:, c, :], banks[b][:, 0:256], reads=[bank_r[b]], writes=[Vp_r[c]])
                    dma(SP, O["kout"], koutf[:], osem, reads=koutf_r)
                    dma(SP, O["vout"], voutf[:], osem, reads=voutf_r)
                    for s2 in range(2):
                        for hd in range(8):
                            kv = hd // 4
                            chunks = []
                            for j in range(2):
                                chunks.append((kTp[:, kv, s2 * 256 + j * 128: s2 * 256 + (j + 1) * 128], kTp_r[kv],
                                               Vp[:, 2 * s2 + j, kv * 128:(kv + 1) * 128], Vp_r[2 * s2 + j]))
                            attend(qTp[:, hd, s2 * 256:(s2 + 1) * 256], [qTp_r[hd]], 256, chunks,
                                   xn[:, hd, s2 * 256:(s2 + 1) * 256], xn_r[hd][0], P, P_r, rec, rec_r)
                    for e in (ACT, DVE, PE):
                        e.wait([(osem.s, osem.n)])
                    barrier()

                with ExitStack() as ph2:
                  if k.sub >= 2:
                      def psb2(name, shape, dt):
                          k.uid += 1
                          return ph2.enter_context(nc.sbuf_tensor(f"p_{name}_{k.uid}", list(shape), dt))
                      qTs = psb2("qTs", [128, 8, 1024], BF16)
                      qTs_r = regs(8, 2)
                      KTf = psb2("KTf", [128, 2, 4608], BF16)
                      KTf_r = Reg()
                      Vf = psb2("Vf", [128, 36, 256], BF16)
                      Vf_r = Reg()
                      ksT = psb2("ksT", [128, 2, 1024], BF16)
                      ksT_r = Reg()
                      vs = psb2("vs", [128, 8, 256], BF16)
                      vs_r = Reg()
                      ropeC = psb2("ropeC", [128, 1024], F32)
                      ropeS = psb2("ropeS", [128, 1024], F32)
                      rope_r = Reg()
                      qn = psb2("qn", [128, 512], F32)
                      qn_r = Reg()
                      qnb = psb2("qnb", [128, 512], BF16)
                      qnb_r = Reg()
                      t1 = psb2("t1", [128, 512], F32)
                      t1_r = Reg()
                      t2 = psb2("t2", [128, 512], F32)
                      t2_r = Reg()
                      dma(SP, ropeC[:], I["ropeC"], asem, writes=[rope_r])
                      dma(SP, ropeS[:], I["ropeS"], asem, writes=[rope_r])
                      rope_r.w = (asem.s, asem.n)
                      dma(POOL, KTf[:, :, 0:512], I["ckT"], asem2, writes=[KTf_r])
                      dma(POOL, Vf[:, 0:4, :], I["cv"], asem2, writes=[Vf_r])
                      for blk in range(3):
                          s = wnext()
                          wv = v3(slot_view(s), 0, KC, 512)
                          for c4 in range(4):
                              ch = blk * 4 + c4
                              if ch < 10:
                                  for t in (1, 2):
                                      c0 = (t - 1) * 512
                                      b = proj_fm(wv, c4, t, s)
                                      headnorm(b, None)
                                      gc = qkg[:, 0:1] if ch < 8 else qkg[:, 1:2]
                                      DVE.op("scalar_tensor_tensor", qn[:, :], banks[b][:, :], gc, rs2[:, :],
                                             ALU.mult, ALU.mult, reads=[bank_r[b], rs2_r, const_r], writes=[qn_r])
                                      ACT.op("activation", qnb[:, :], qn[:, :], AF.Copy, reads=[qn_r], writes=[qnb_r])
                                      b3 = next_bank()
                                      PE.op("matmul", banks[b3][:, :], rmat_bf[:], qnb[:, :], start=True, stop=True,
                                            reads=[qnb_r, const_r], writes=[bank_r[b3]])
                                      DVE.op("tensor_tensor", t1[:, :], qn[:, :], ropeC[:, c0:c0 + 512], ALU.mult,
                                             reads=[qn_r, rope_r], writes=[t1_r])
                                      DVE.op("tensor_tensor", t2[:, :], banks[b3][:, :], ropeS[:, c0:c0 + 512], ALU.mult,
                                             reads=[bank_r[b3], rope_r], writes=[t2_r])
                                      if ch < 8:
                                          DVE.op("tensor_tensor", qTs[:, ch, c0:c0 + 512], t1[:, :], t2[:, :], ALU.add,
                                                 reads=[t1_r, t2_r], writes=[qTs_r[ch][t - 1]])
                                      else:
                                          DVE.op("tensor_tensor", ksT[:, ch - 8, c0:c0 + 512], t1[:, :], t2[:, :], ALU.add,
                                                 reads=[t1_r, t2_r], writes=[ksT_r])
                              elif ch == 10:
                                  for c in range(4, 12):
                                      b = next_bank()
                                      t = 1 if c < 8 else 2
                                      for kk in range(KC):
                                          PE.op("matmul", banks[b][:, 0:256], xn[:, kk, c * 128:(c + 1) * 128], wv[:, kk, 256:512],
                                                start=(kk == 0), stop=(kk == KC - 1), reads=[wring_r[s], xn_r[kk][t]],
                                                writes=[bank_r[b]] if kk == 0 else [], inc=(kk == KC - 1))
                                      bank_r[b].w = (PE.sem, PE.cnt)
                                      ACT.op("activation", vs[:, c - 4, :], banks[b][:, 0:256], AF.Copy, reads=[bank_r[b]], writes=[vs_r])
                      if k.sub < 3:
                        return
                      ta = dma(SP, agin1.ap()[0:128, :], ksT[:].rearrange("p h t -> p (h t)"), asem, reads=[ksT_r])
                      tb = dma(SP, agin1.ap()[128:256, :], vs[:].rearrange("p c f -> p (c f)"), asem, reads=[vs_r])
                      POOL.wait([ta, tb])
                      nc.gpsimd.collective_compute("AllGather", ALU.bypass, replica_groups=GROUPS,
                                                   ins=[agin1.ap().opt()], outs=[agout1.ap().opt()]).then_inc(ccsem.s, 1)
                      ccsem.n += 1
                      cct = (ccsem.s, ccsem.n)
                      SP._deps((), [KTf_r, Vf_r], [cct])
                      for rr in range(4):
                          dma(SP, KTf[:, :, 512 + rr * 1024:512 + (rr + 1) * 1024],
                              agout1.ap()[rr * 256:rr * 256 + 128, :].rearrange("p (h t) -> p h t", h=2), asem)
                          dma(SP, Vf[:, 4 + rr * 8:4 + (rr + 1) * 8, :],
                              agout1.ap()[rr * 256 + 128:(rr + 1) * 256, :].rearrange("p (c f) -> p c f", c=8), asem)
                      kvtok = (asem.s, asem.n)
                      KTf_r.w = None
                      Vf_r.w = None
                      PE.wait([kvtok, (asem2.s, asem2.n)])
                      if dbg and k.dbgsrc == 'kv':
                          POOL.wait([kvtok, (asem2.s, asem2.n)])
                          for hh in range(2):
                              for j3 in range(3):
                                  dma(POOL, O["dbg"][(hh * 3 + j3) * 128:(hh * 3 + j3 + 1) * 128, :], KTf[:, hh, j3 * 1536:(j3 + 1) * 1536], osem)
                          dma(POOL, O["dbg"][768:896, :], Vf[:, 0:6, :].rearrange("p c f -> p (c f)"), osem)
                          dma(POOL, O["dbg"][896:1024, :], Vf[:, 30:36, :].rearrange("p c f -> p (c f)"), osem)
                          for e in (ACT, DVE, PE):
                              e.wait([(osem.s, osem.n)])
                          return
                      if k.sub < 4:
                          return
                      for t in (1, 2):
                          for hd in range(8):
                              kv = hd // 4
                              chunks = []
                              for c in range(36):
                                  chunks.append((KTf[:, kv, c * 128:(c + 1) * 128], KTf_r,
                                                 Vf[:, c, kv * 128:(kv + 1) * 128], Vf_r))
                              attend(qTs[:, hd, (t - 1) * 512:t * 512], [qTs_r[hd][t - 1]], 512, chunks,
                                     xn[:, hd, TILES[t][0]:TILES[t][1]], xn_r[hd][t], P, P_r, rec, rec_r)
                      barrier()

                if dbg and k.dbgsrc == 'at':
                    for kk in range(KC):
                        dma(POOL, O["dbg"][kk * 128:(kk + 1) * 128, :], xn[:, kk, :], osem, reads=xn_r[kk])
                    for e in (ACT, DVE, PE):
                        e.wait([(osem.s, osem.n)])
                k.nring = 6
                k.bank_i = 0
                for blk in range(2 if k.sub >= 5 else 0):
                    s = wnext()
                    wv = v3(slot_view(s), 0, KC, 512)
                    for c4 in range(4):
                        dc = blk * 4 + c4
                        for t, (t0, t1_) in enumerate(TILES):
                            b = next_bank()
                            for kk in range(KC):
                                PE.op("matmul", banks[b][:, :], wv[:, kk, c4 * 128:(c4 + 1) * 128], xn[:, kk, t0:t1_],
                                      start=(kk == 0), stop=(kk == KC - 1), reads=[wring_r[s], xn_r[kk][t]],
                                      writes=[bank_r[b]] if kk == 0 else [], inc=(kk == KC - 1))
                            bank_r[b].w = (PE.sem, PE.cnt)
                            DVE.op("scalar_tensor_tensor", h[:, dc, t0:t1_], banks[b][:, :], coefG[:, l, 1, dc, cj(t):cj(t) + 1],
                                   h[:, dc, t0:t1_], ALU.mult, ALU.add, reads=[bank_r[b], mod_r], writes=[h_r[dc][t]])
                barrier()


        def run_threads(gens):
            gens = list(gens)
            while gens:
                alive = []
                for g_ in gens:
                    try:
                        next(g_)
                        alive.append(g_)
                    except StopIteration:
                        pass
                gens = alive

        class Obj:
            pass

        def deltanet(l):
            with ExitStack() as ph:
                def psb(name, shape, dt):
                    k.uid += 1
                    return ph.enter_context(nc.sbuf_tensor(f"p_{name}_{k.uid}", list(shape), dt))
                masks = psb("masks", [128, 4, 128], F32)
                convp = psb("convp", [128, 24, 3], F32)
                convo = psb("convo", [128, 6, 3], F32)
                wabp = psb("wabp", [128, KC, 32], BF16)
                wabo = psb("wabo", [128, KC, 8], BF16)
                dtbp = psb("dtbp", [128, 16], F32)
                negAp = psb("negAp", [128, 16], F32)
                dtbo = psb("dtbo", [128, 4], F32)
                negAo = psb("negAo", [128, 4], F32)
                ong = psb("ong", [128, 1], F32)
                selt = psb("selt", [128, 4], F32)
                lmask = psb("lmask", [128, 7, 128], BF16)
                ident2 = psb("ident2", [128, 256], BF16)
                DVE.op("tensor_copy", ident2[:, 0:128], ident_f[:], reads=[const_r])
                DVE.op("tensor_copy", ident2[:, 128:256], ident_f[:], reads=[const_r])
                dn_r = Reg()
                dma(POOL, lmask[:], I["lmask"], asem2)
                dma(SP, masks[:], I["masks"], dsem)
                dma(SP, convp[:], I["convT"], dsem)
                dma(SP, convo[:], I["convT_own"], dsem)
                dma(SP, dtbp[:], I["dtb_p"].partition_broadcast(128), dsem)
                dma(SP, negAp[:], I["alog_p"].partition_broadcast(128), dsem)
                dma(SP, dtbo[:], I["dtb_o"].partition_broadcast(128), dsem)
                dma(SP, negAo[:], I["alog_o"].partition_broadcast(128), dsem)
                dma(SP, ong[:], I["ong"], dsem)
                dma(SP, selt[:], I["sel"].partition_broadcast(128), dsem)
                dma(POOL, wabp[:], I["wab_p"].rearrange("(k p) c -> p k c", p=128), asem2)
                dma(POOL, wabo[:], I["wab_o"].rearrange("(k p) c -> p k c", p=128), asem2)
                dtok = (dsem.s, dsem.n)
                dtok2 = (asem2.s, asem2.n)
                for (na, nA) in ((16, negAp), (4, negAo)):
                    ACT.op("activation", nA[:, :], nA[:, :], AF.Exp, deps=[dtok], writes=[dn_r])
                    DVE.op("tensor_scalar", nA[:, :], nA[:, :], -1.0, None, ALU.mult, writes=[dn_r])
                dn_r.w = None
                for e_ in (PE, ACT, DVE, POOL):
                    e_.wait([dtok, dtok2, (DVE.sem, DVE.cnt)])

                sq = psb("sq", [128, 2, 512], BF16)
                sq_r = regs(2)
                rstd = psb("rstd", [128, 512], F32)
                rstd_r = Reg()
                tmp = psb("tmp", [128, 2, 512], F32)
                pool = (sq, sq_r, rstd, rstd_r, tmp, regs(2))
                modnorm(lambda kk, t: xn[:, kk, TILES[t][0]:TILES[t][1]], lambda kk, t: xn_r[kk][t],
                        lambda kk, t: coefA[:, l, 1, kk, cj(t):cj(t) + 1],
                        lambda kk, t: modT[:, l, 3 * 8 + kk, cj(t):cj(t) + 1], pool)
                for x in range(2):
                    ta = dma(SP, agin2[x].ap().rearrange("(k p) t -> p k t", p=128), xn[:, 4 * x:4 * x + 4, 512:1536], asem,
                             reads=[xn_r[kk][t] for kk in range(4 * x, 4 * x + 4) for t in (1, 2)])
                POOL.wait([ta])
                for x in range(2):
                    nc.gpsimd.collective_compute("AllGather", ALU.bypass, replica_groups=GROUPS,
                                                 ins=[agin2[x].ap().opt()], outs=[agout2[x].ap().opt()]).then_inc(ccsem.s, 1)
                    ccsem.n += 1
                cct2 = (ccsem.s, ccsem.n)

                if k.dsub < 2:
                    for e_ in (PE, ACT, DVE):
                        e_.wait([cct2])
                    barrier()
                    return
                G = 4
                TS = []
                for g_ in range(G):
                    t = Obj()
                    t.cols = psb("cols", [128, 8], F32); t.cols_r = Reg()
                    t.dgd = psb("dgd", [128, 128], F32); t.dgd_r = Reg()
                    t.gcr = psb("gcr", [128, 128], F32); t.gcr_r = Reg()
                    t.egr = psb("egr", [128, 128], F32); t.egr_r = Reg()
                    t.ddT = psb("ddT", [128, 128], F32); t.ddT_r = Reg()
                    t.kq = psb("kq", [128, 256], BF16); t.kks_r = Reg()
                    t.kvt = psb("kvt", [128, 256], BF16); t.ktok_r = Reg()
                    t.A = [psb("A0", [128, 128], BF16), psb("A1", [128, 128], BF16)]; t.A_r = regs(2)
                    t.BP = [psb("BP0", [128, 256], BF16), psb("BP1", [128, 256], BF16)]; t.BP_r = regs(2)
                    t.TT = psb("TT", [128, 128], BF16); t.TT_r = Reg()
                    t.vb = psb("vb", [128, 128], BF16); t.vb_r = Reg()
                    t.kbg = psb("kbg", [128, 128], BF16); t.kbg_r = Reg()
                    TS.append(t)
                OS = []
                for par in range(2):
                    row = []
                    for g_ in range(G):
                        o = Obj()
                        o.wT = psb("wT", [128, 128], BF16)
                        o.u = psb("u", [128, 128], F32)
                        o.qkT = psb("qkT", [128, 128], BF16)
                        o.qdT = psb("qdT", [128, 128], BF16)
                        o.kd = psb("kd", [128, 128], BF16)
                        o.glc = psb("glc", [128, 1], F32)
                        o.r = Reg()
                        row.append(o)
                    OS.append(row)
                CH = []
                for c_ in range(2):
                    ch = Obj()
                    ch.S = psb("S", [128, 128], F32); ch.S_r = Reg()
                    ch.Sb = psb("Sb", [128, 128], BF16); ch.Sb_r = Reg()
                    ch.vn = psb("vn", [128, 128], BF16); ch.vn_r = Reg()
                    CH.append(ch)
                cvt = psb("cvt", [128, 2, 256], F32)
                cvt_r = regs(2)
                slf = psb("slf", [128, 2, 256], F32)
                slf_r = regs(2)
                gtmp = psb("gtmp", [128, 2, 32], F32)
                gtmp_r = regs(2)
                k.cv_i = 0

                def prep(sy):
                    t = TS[sy.g]
                    o = OS[sy.par][sy.g]
                    e = sy.e
                    tri = masks[:, e, :]
                    mA = masks[:, 2 + e, :]
                    last = 127 if e == 0 else 0
                    gcc = t.cols[:, 0:1]
                    b = next_bank()
                    PE.op("matmul", banks[b][:, 0:128], sy.kT, sy.kT, start=True, stop=True, reads=sy.qkv_r, writes=[bank_r[b]], inc=False)
                    PE.op("matmul", banks[b][:, 128:256], sy.kT, sy.qT, start=True, stop=True, reads=sy.qkv_r, inc=False)
                    PE.op("matmul", banks[b][:, 256:384], sy.kT, ident_bf[:], start=True, stop=True, reads=sy.qkv_r + [const_r], inc=False)
                    PE.op("matmul", banks[b][:, 384:512], sy.vT, ident_bf[:], start=True, stop=True, reads=sy.qkv_r + [const_r])
                    bank_r[b].w = (PE.sem, PE.cnt)
                    DVE.op("tensor_copy", t.kq[:, :], banks[b][:, 0:256], reads=[bank_r[b]], writes=[t.kks_r])
                    DVE.op("tensor_copy", t.kvt[:, :], banks[b][:, 256:512], reads=[bank_r[b]], writes=[t.ktok_r])
                    b = next_bank()
                    PE.op("matmul", banks[b][:, 0:1], tri, sy.gcol, start=True, stop=True, reads=[sy.g_r], writes=[bank_r[b]])
                    ACT.op("activation", gcc, banks[b][:, 0:1], AF.Copy, reads=[bank_r[b]], writes=[t.cols_r])
                    yield
                    DVE.op("tensor_scalar", t.dgd[:, :], ident_f[:], gcc, None, ALU.mult, reads=[t.cols_r, const_r], writes=[t.dgd_r])
                    yield
                    b = next_bank()
                    PE.op("matmul", banks[b][:, 0:128], ones_f[:], t.dgd[:, :], start=True, stop=True, reads=[t.dgd_r, const_r], writes=[bank_r[b]])
                    ACT.op("activation", t.gcr[:, :], banks[b][:, 0:128], AF.Copy, reads=[bank_r[b]], writes=[t.gcr_r])
                    ACT.op("activation", t.egr[:, :], banks[b][:, 0:128], AF.Exp, reads=[bank_r[b]], writes=[t.egr_r])
                    ACT.op("activation", t.cols[:, 1:2], gcc, AF.Exp, reads=[t.cols_r], writes=[t.cols_r])
                    ACT.op("activation", t.cols[:, 2:3], gcc, AF.Exp, scale=-1.0, bias=t.gcr[:, last:last + 1],
                           reads=[t.cols_r, t.gcr_r], writes=[t.cols_r])
                    yield
                    DVE.op("tensor_scalar", t.dgd[:, :], t.gcr[:, :], gcc, 0.0, ALU.subtract, ALU.max,
                           reads=[t.gcr_r, t.cols_r], writes=[t.dgd_r])
                    DVE.op("tensor_scalar", t.ddT[:, :], t.gcr[:, :], gcc, 0.0, ALU.subtract, ALU.min,
                           reads=[t.gcr_r, t.cols_r], writes=[t.ddT_r])
                    DVE.op("tensor_tensor", t.cols[:, 3:4], sy.bcol, t.cols[:, 1:2], ALU.mult, reads=[t.cols_r, sy.g_r], writes=[t.cols_r])
                    DVE.op("tensor_copy", o.glc[:, :], t.egr[:, last:last + 1], reads=[t.egr_r], writes=[o.r])
                    yield
                    ACT.op("activation", t.dgd[:, :], t.dgd[:, :], AF.Exp, scale=-1.0, reads=[t.dgd_r], writes=[t.dgd_r])
                    ACT.op("activation", t.ddT[:, :], t.ddT[:, :], AF.Exp, reads=[t.ddT_r], writes=[t.ddT_r])
                    yield
                    DVE.op("tensor_tensor", t.dgd[:, :], t.dgd[:, :], mA, ALU.mult, reads=[t.dgd_r], writes=[t.dgd_r])
                    DVE.op("scalar_tensor_tensor", t.A[0][:, :], t.kq[:, 0:128], sy.bcol, t.dgd[:, :], ALU.mult, ALU.mult,
                           reads=[t.kks_r, t.dgd_r, sy.g_r], writes=[t.A_r[0]])
                    DVE.op("tensor_tensor", t.ddT[:, :], t.ddT[:, :], tri, ALU.mult, reads=[t.ddT_r], writes=[t.ddT_r])
                    yield
                    b = next_bank()
                    PE.op("matmul", banks[b][:, 0:128], t.A[0][:, :], ident_bf[:], start=True, stop=True, reads=[t.A_r[0], const_r], writes=[bank_r[b]])
                    DVE.op("tensor_copy", t.A[1][:, :], banks[b][:, 0:128], reads=[bank_r[b]], writes=[t.A_r[1]])
                    POOL.op("tensor_tensor", o.qkT[:, :], t.kq[:, 128:256], t.ddT[:, :], ALU.mult, reads=[t.kks_r, t.ddT_r], writes=[o.r])
                    POOL.op("tensor_tensor", o.qdT[:, :], sy.qT, t.egr[:, :], ALU.mult, reads=sy.qkv_r + [t.egr_r], writes=[o.r])
                    POOL.op("tensor_scalar", t.vb[:, :], t.kvt[:, 128:256], sy.bcol, None, ALU.mult, reads=[t.ktok_r, sy.g_r], writes=[t.vb_r])
                    POOL.op("tensor_scalar", t.kbg[:, :], t.kvt[:, 0:128], t.cols[:, 3:4], None, ALU.mult, reads=[t.ktok_r, t.cols_r], writes=[t.kbg_r])
                    POOL.op("tensor_scalar", o.kd[:, :], t.kvt[:, 0:128], t.cols[:, 2:3], None, ALU.mult, reads=[t.ktok_r, t.cols_r], writes=[o.r])
                    yield
                    if e == 0:
                        Bin, Bin_r = t.A[1], t.A_r[1]
                    else:
                        Bin, Bin_r = t.A[0], t.A_r[0]
                    Tc, TTc, TQ_r = ident2[:, 0:128], ident2[:, 128:256], const_r
                    TQc = ident2[:, 0:256]
                    cur = 0
                    for li in range(7):
                        b = next_bank()
                        PE.op("matmul", banks[b][:, 0:128], Bin[:, :], Tc, start=True, stop=True, reads=[Bin_r, TQ_r], writes=[bank_r[b]])
                        DVE.op("scalar_tensor_tensor", t.TT[:, :], banks[b][:, 0:128], -1.0, lmask[:, li, :], ALU.mult, ALU.mult,
                               reads=[bank_r[b], dn_r], writes=[t.TT_r])
                        b2 = next_bank()
                        PE.op("matmul", banks[b2][:, 0:256], ident_bf[:], TQc, start=True, stop=False, reads=[TQ_r, const_r], writes=[bank_r[b2]], inc=False)
                        PE.op("matmul", banks[b2][:, 0:128], TTc, t.TT[:, :], start=False, stop=False, reads=[TQ_r, t.TT_r], inc=False)
                        PE.op("matmul", banks[b2][:, 128:256], t.TT[:, :], TTc, start=False, stop=True, reads=[TQ_r, t.TT_r])
                        bank_r[b2].w = (PE.sem, PE.cnt)
                        ACT.op("activation", t.BP[cur][:, 0:256], banks[b2][:, 0:256], AF.Copy, reads=[bank_r[b2]], writes=[t.BP_r[cur]])
                        Tc, TTc, TQ_r = t.BP[cur][:, 0:128], t.BP[cur][:, 128:256], t.BP_r[cur]
                        TQc = t.BP[cur][:, 0:256]
                        cur = 1 - cur
                        yield
                    TTf = TTc if e == 0 else Tc
                    TTf_r = TQ_r
                    b = next_bank()
                    PE.op("matmul", banks[b][:, 0:128], t.kbg[:, :], TTf, start=True, stop=True, reads=[t.kbg_r, TTf_r], writes=[bank_r[b]], inc=False)
                    PE.op("matmul", banks[b][:, 128:256], TTf, t.vb[:, :], start=True, stop=True, reads=[t.vb_r, TTf_r])
                    bank_r[b].w = (PE.sem, PE.cnt)
                    ACT.op("activation", o.wT[:, :], banks[b][:, 0:128], AF.Copy, reads=[bank_r[b]], writes=[o.r])
                    DVE.op("tensor_copy", o.u[:, :], banks[b][:, 128:256], reads=[bank_r[b]], writes=[o.r])
                    yield

                def scan(ch, steps):
                    for (o, oacc, oacc_r) in steps:
                        b = next_bank()
                        PE.op("matmul", banks[b][:, 0:128], o.wT[:, :], ch.Sb[:, :], start=True, stop=True, reads=[o.r, ch.Sb_r], writes=[bank_r[b]])
                        DVE.op("tensor_tensor", ch.vn[:, :], o.u[:, :], banks[b][:, 0:128], ALU.subtract, reads=[o.r, bank_r[b]], writes=[ch.vn_r])
                        yield
                        b = next_bank()
                        PE.op("matmul", banks[b][:, 0:128], ch.Sb[:, :], o.qdT[:, :], start=True, stop=False, reads=[o.r, ch.Sb_r], writes=[bank_r[b]], inc=False)
                        PE.op("matmul", banks[b][:, 0:128], ch.vn[:, :], o.qkT[:, :], start=False, stop=True, reads=[o.r, ch.vn_r], writes=[bank_r[b]])
                        DVE.op("tensor_tensor", oacc, oacc, banks[b][:, 0:128], ALU.add, reads=[bank_r[b]], writes=[oacc_r])
                        yield
                        b = next_bank()
                        PE.op("matmul", banks[b][:, 0:128], o.kd[:, :], ch.vn[:, :], start=True, stop=True, reads=[o.r, ch.vn_r], writes=[bank_r[b]])
                        DVE.op("scalar_tensor_tensor", ch.S[:, :], ch.S[:, :], o.glc[:, 0:1], banks[b][:, 0:128], ALU.mult, ALU.add,
                               reads=[o.r, bank_r[b]], writes=[ch.S_r])
                        ACT.op("activation", ch.Sb[:, :], ch.S[:, :], AF.Copy, reads=[ch.S_r], writes=[ch.Sb_r])
                        yield

                def proj_chunk(wv, c4, s, ublk, ublk_r, typ, conv_ap, dst, dst_r):
                    b = next_bank()
                    for kk in range(KC):
                        PE.op("matmul", banks[b][:, 0:258], wv[:, kk, c4 * 128:(c4 + 1) * 128], ublk[:, kk, :],
                              start=(kk == 0), stop=(kk == KC - 1), reads=[wring_r[s], ublk_r],
                              writes=[bank_r[b]] if kk == 0 else [], inc=(kk == KC - 1))
                    bank_r[b].w = (PE.sem, PE.cnt)
                    if typ == 3:
                        ACT.op("activation", dst, banks[b][:, 1:257], AF.Silu, reads=[bank_r[b]], writes=[dst_r])
                        return
                    i = k.cv_i % 2
                    k.cv_i += 1
                    ACT.op("activation", cvt[:, i, :], banks[b][:, 1:257], AF.Copy, scale=conv_ap[:, 1:2], reads=[bank_r[b], dn_r], writes=[cvt_r[i]])
                    DVE.op("scalar_tensor_tensor", cvt[:, i, :], banks[b][:, 0:256], conv_ap[:, 0:1], cvt[:, i, :], ALU.mult, ALU.add,
                           reads=[bank_r[b], dn_r], writes=[cvt_r[i]])
                    DVE.op("scalar_tensor_tensor", cvt[:, i, :], banks[b][:, 2:258], conv_ap[:, 2:3], cvt[:, i, :], ALU.mult, ALU.add,
                           reads=[bank_r[b], dn_r], writes=[cvt_r[i]])
                    if typ == 2:
                        ACT.op("activation", dst, cvt[:, i, :], AF.Silu, reads=[cvt_r[i]], writes=[dst_r])
                        return
                    ACT.op("activation", slf[:, i, :], cvt[:, i, :], AF.Silu, reads=[cvt_r[i]], writes=[slf_r[i]])
                    ACT.op("activation", sq[:, i, 0:256], slf[:, i, :], AF.Square, reads=[slf_r[i]], writes=[sq_r[i]])
                    b2 = next_bank()
                    PE.op("matmul", banks[b2][:, 0:256], ones_bf[:], sq[:, i, 0:256], start=True, stop=True, reads=[sq_r[i], const_r], writes=[bank_r[b2]])
                    ACT.op("activation", cvt[:, i, :], banks[b2][:, 0:256], AF.Ln, bias=epsc[:, 0:1], reads=[bank_r[b2], const_r], writes=[cvt_r[i]])
                    ACT.op("activation", cvt[:, i, :], cvt[:, i, :], AF.Exp, scale=-0.5, reads=[cvt_r[i]], writes=[cvt_r[i]])
                    DVE.op("scalar_tensor_tensor", dst, slf[:, i, :], (128.0 ** -0.5) if typ == 0 else 1.0, cvt[:, i, :], ALU.mult, ALU.mult,
                           reads=[slf_r[i], cvt_r[i]], writes=[dst_r])

                def gbeta(ublk, ublk_r, wab, ncol, dtb, negA, dst_fn):
                    na = ncol // 2
                    for i in range(2):
                        b = next_bank()
                        for kk in range(KC):
                            PE.op("matmul", banks[b][:, 0:ncol], ublk[:, kk, 1 + i * 128:1 + (i + 1) * 128], wab[:, kk, :],
                                  start=(kk == 0), stop=(kk == KC - 1), reads=[ublk_r, dn_r],
                                  writes=[bank_r[b]] if kk == 0 else [], inc=(kk == KC - 1))
                        bank_r[b].w = (PE.sem, PE.cnt)
                        d = dst_fn(i)
                        DVE.op("tensor_tensor", gtmp[:, i, 0:na], banks[b][:, 0:na], dtb[:, :], ALU.add, reads=[bank_r[b], dn_r], writes=[gtmp_r[i]])
                        ACT.op("activation", gtmp[:, i, na:ncol], banks[b][:, na:ncol], AF.Exp, scale=-1.0, reads=[bank_r[b]], writes=[gtmp_r[i]])
                        ACT.op("activation", gtmp[:, i, 0:na], gtmp[:, i, 0:na], AF.Exp, reads=[gtmp_r[i]], writes=[gtmp_r[i]])
                        ACT.op("activation", gtmp[:, i, 0:na], gtmp[:, i, 0:na], AF.Ln, bias=onec[:, 0:1], reads=[gtmp_r[i], const_r], writes=[gtmp_r[i]])
                        DVE.op("tensor_tensor", d[:, 0:na], gtmp[:, i, 0:na], negA[:, :], ALU.mult, reads=[gtmp_r[i], dn_r], writes=[k.gst_r])
                        DVE.op("tensor_scalar", gtmp[:, i, na:ncol], gtmp[:, i, na:ncol], 1.0, None, ALU.add, reads=[gtmp_r[i]], writes=[gtmp_r[i]])
                        DVE.op("reciprocal", d[:, na:ncol], gtmp[:, i, na:ncol], reads=[gtmp_r[i]], writes=[k.gst_r])

                def post(oacc, oacc_r, zs_fn, zs_r_fn, T, dst_fn, dst_r_fn):
                    for p0 in range(0, T, 512):
                        w = min(512, T - p0)
                        ACT.op("activation", sq[:, 0, 0:w], oacc[:, p0:p0 + w], AF.Square, reads=[oacc_r], writes=[sq_r[0]])
                        b = next_bank()
                        PE.op("matmul", banks[b][:, 0:w], ones_bf[:], sq[:, 0, 0:w], start=True, stop=True, reads=[sq_r[0], const_r], writes=[bank_r[b]])
                        ACT.op("activation", rstd[:, 0:w], banks[b][:, 0:w], AF.Ln, bias=epsc[:, 0:1], scale=1.0 / 128, reads=[bank_r[b], const_r], writes=[rstd_r])
                        ACT.op("activation", rstd[:, 0:w], rstd[:, 0:w], AF.Exp, scale=-0.5, reads=[rstd_r], writes=[rstd_r])
                        DVE.op("scalar_tensor_tensor", tmp[:, 0, 0:w], oacc[:, p0:p0 + w], ong[:, 0:1], rstd[:, 0:w], ALU.mult, ALU.mult,
                               reads=[oacc_r, rstd_r, dn_r], writes=[k.tmp0_r])
                        DVE.op("tensor_tensor", dst_fn(p0, w), tmp[:, 0, 0:w], zs_fn(p0, w), ALU.mult, reads=[k.tmp0_r, zs_r_fn(p0)], writes=[dst_r_fn(p0)])

                k.tmp0_r = Reg()
                k.gst_r = Reg()

                with ExitStack() as ph2:
                    def psb2(name, shape, dt):
                        k.uid += 1
                        return ph2.enter_context(nc.sbuf_tensor(f"p_{name}_{k.uid}", list(shape), dt))
                    upad = psb2("upad", [128, KC, 2, 258], BF16)
                    upad_r = Reg()
                    qkvz = psb2("qkvz", [128, 32, 512], BF16)
                    qkvz_r = regs(32)
                    gstp = psb2("gstp", [128, 4, 32], F32)
                    oaccp = psb2("oaccp", [128, 256], F32)
                    oaccp_r = Reg()
                    DVE.op("memset", upad[:], 0.0, writes=[upad_r])
                    for s2 in range(2):
                        ACT.op("activation", upad[:, :, s2, 1:257], xn[:, :, s2 * 256:(s2 + 1) * 256], AF.Copy,
                               reads=[xn_r[kk][0] for kk in range(KC)], writes=[upad_r])
                    for blk in range(8):
                        s = wnext()
                        wv = v3(slot_view(s), 0, KC, 512)
                        typ = blk // 2
                        for c4 in range(4):
                            hd = (blk % 2) * 4 + c4
                            ci = typ * 8 + hd
                            for s2 in range(2):
                                proj_chunk(wv, c4, s, upad[:, :, s2, :], upad_r, typ,
                                           convp[:, ci, :] if typ < 3 else None,
                                           qkvz[:, ci, s2 * 256:(s2 + 1) * 256], qkvz_r[ci])
                    for s2 in range(2):
                        gbeta(upad[:, :, s2, :], upad_r, wabp, 32, dtbp, negAp, lambda i, s2=s2: gstp[:, 2 * s2 + i, :])
                    if dbg and k.dbgsrc == 'dnu':
                        for kk in range(KC):
                            dma(POOL, O["dbg"][kk * 128:(kk + 1) * 128, 0:512], xn[:, kk, 0:512], osem, reads=[xn_r[kk][0]])
                            dma(POOL, O["dbg"][kk * 128:(kk + 1) * 128, 512:1024], xn[:, kk, 512:1024], osem, reads=[xn_r[kk][1]])
                            dma(SP, O["dbg"][kk * 128:(kk + 1) * 128, 1024:1536], h[:, kk, 0:512], osem, reads=[h_r[kk][0]])
                        for e_ in (ACT, DVE, PE):
                            e_.wait([(osem.s, osem.n)])
                        return
                    if dbg and k.dbgsrc == 'dn':
                        dma(POOL, O["dbg"][0:128, 0:512], qkvz[:, 0, :], osem, reads=[qkvz_r[0]])
                        dma(POOL, O["dbg"][0:128, 512:1024], qkvz[:, 8, :], osem, reads=[qkvz_r[8]])
                        dma(POOL, O["dbg"][0:128, 1024:1536], qkvz[:, 16, :], osem, reads=[qkvz_r[16]])
                        dma(POOL, O["dbg"][128:256, 0:512], qkvz[:, 24, :], osem, reads=[qkvz_r[24]])
                        dma(POOL, O["dbg"][128:256, 512:640], gstp[:].rearrange("p c f -> p (c f)"), osem, reads=[k.gst_r])
                    pending = None
                    jobs = [(s2, hd) for s2 in range(2) for hd in range(8)]
                    for j in range(len(jobs) + 1):
                        gens = []
                        if j < len(jobs):
                            s2, hd = jobs[j]
                            par = j % 2
                            systems = []
                            for g_, (e, c) in enumerate(((0, 0), (0, 1), (1, 1), (1, 0))):
                                sy = Obj()
                                sy.g, sy.par, sy.e = g_, par, e
                                col0 = s2 * 256 + c * 128
                                sy.qT = qkvz[:, 0 * 8 + hd, col0:col0 + 128]
                                sy.kT = qkvz[:, 1 * 8 + hd, col0:col0 + 128]
                                sy.vT = qkvz[:, 2 * 8 + hd, col0:col0 + 128]
                                sy.qkv_r = [qkvz_r[hd], qkvz_r[8 + hd], qkvz_r[16 + hd]]
                                sy.gcol = gstp[:, 2 * s2 + c, e * 8 + hd:e * 8 + hd + 1]
                                sy.bcol = gstp[:, 2 * s2 + c, 16 + e * 8 + hd:16 + e * 8 + hd + 1]
                                sy.g_r = k.gst_r
                                systems.append(sy)
                                gens.append(prep(sy))
                        if pending is not None:
                            gens.extend(pending)
                        run_threads(gens)
                        if pending is not None:
                            (ps2, phd, ppar) = k.pjob
                            dma(SP, O["sfo"][:, ps2 * 8 + phd, :], CH[0].S[:, :], osem, reads=[CH[0].S_r])
                            dma(SP, O["sbo"][:, ps2 * 8 + phd, :], CH[1].S[:, :], osem, reads=[CH[1].S_r])
                            if dbg and k.dbgsrc == 'dn' and (ps2, phd) == (0, 0):
                                dma(POOL, O["dbg"][256:384, 0:256], oaccp[:, :], osem, reads=[oaccp_r])
                                dma(POOL, O["dbg"][384:512, 0:128], CH[0].S[:, :], osem, reads=[CH[0].S_r])
                                for g_ in range(4):
                                    o_ = OS[ppar][g_]
                                    dma(POOL, O["dbg"][512:640, g_ * 128:(g_ + 1) * 128], o_.wT[:, :], osem, reads=[o_.r])
                                    dma(POOL, O["dbg"][640:768, g_ * 128:(g_ + 1) * 128], o_.u[:, :], osem, reads=[o_.r])
                                    dma(POOL, O["dbg"][768:896, g_ * 128:(g_ + 1) * 128], o_.qkT[:, :], osem, reads=[o_.r])
                                    dma(POOL, O["dbg"][896:1024, g_ * 128:(g_ + 1) * 128], o_.kd[:, :], osem, reads=[o_.r])
                                    dma(POOL, O["dbg"][512:640, 512 + g_ * 128:512 + (g_ + 1) * 128], o_.qdT[:, :], osem, reads=[o_.r])
                                    dma(POOL, O["dbg"][640:768, 512 + g_:512 + g_ + 1], o_.glc[:, :], osem, reads=[o_.r], allow_slow_non_contiguous=True)
                                for e_ in (ACT, DVE, PE):
                                    e_.wait([(osem.s, osem.n)])
                            post(oaccp, oaccp_r, lambda p0, w, ps2=ps2, phd=phd: qkvz[:, 24 + phd, ps2 * 256 + p0:ps2 * 256 + p0 + w],
                                 lambda p0, phd=phd: qkvz_r[24 + phd], 256,
                                 lambda p0, w, ps2=ps2, phd=phd: xn[:, phd, ps2 * 256 + p0:ps2 * 256 + p0 + w],
                                 lambda p0, phd=phd: xn_r[phd][0])
                            pending = None
                        if j < len(jobs):
                            for ch in CH:
                                DVE.op("memset", ch.S[:, :], 0.0, writes=[ch.S_r])
                                DVE.op("memset", ch.Sb[:, :], 0.0, writes=[ch.Sb_r])
                            DVE.op("memset", oaccp[:, :], 0.0, writes=[oaccp_r])
                            par = j % 2
                            stf = [(OS[par][0], oaccp[:, 0:128], oaccp_r), (OS[par][1], oaccp[:, 128:256], oaccp_r)]
                            stb = [(OS[par][2], oaccp[:, 128:256], oaccp_r), (OS[par][3], oaccp[:, 0:128], oaccp_r)]
                            pending = [scan(CH[0], stf), scan(CH[1], stb)]
                            k.pjob = (s2, hd, par)
                    for e_ in (ACT, DVE, PE):
                        e_.wait([(osem.s, osem.n)])
                    barrier()

                if k.dsub < 3:
                    return
                with ExitStack() as ph2:
                    def psb2(name, shape, dt):
                        k.uid += 1
                        return ph2.enter_context(nc.sbuf_tensor(f"p_{name}_{k.uid}", list(shape), dt))
                    TT_ = 4096
                    ub = [psb2(f"ub{i}", [128, KC, 258], BF16) for i in range(2)]
                    ub_r = regs(2)
                    qkvzs = psb2("qkvzs", [128, 3, TT_], BF16)
                    qkvzs_r = regs(3)
                    gsts = psb2("gsts", [128, 32, 8], F32)
                    oaccs = psb2("oaccs", [128, TT_], F32)
                    oaccs_r = Reg()
                    s0 = psb2("s0", [128, 2, 2, 128], F32)
                    s0_r = Reg()
                    dma(SP, s0[:, 0, :, :], I["s0f"], dsem, writes=[s0_r])
                    dma(SP, s0[:, 1, :, :], I["s0b"], dsem, writes=[s0_r])
                    s0_r.w = (dsem.s, dsem.n)

                    def load_ublk(tb):
                        i = tb % 2
                        rr, ls = tb // 4, (tb % 4) * 256 - 1
                        lo, hi = max(ls, 0), min(ls + 258, 1024)
                        us = usems[i]
                        SP._deps((), [ub_r[i]], [cct2])
                        SP.wait(BAR["toks"])
                        for x in range(2):
                            tok = dma(SP, ub[i][:, 4 * x:4 * x + 4, lo - ls:hi - ls],
                                      agout2[x].ap()[rr * 512:(rr + 1) * 512, lo:hi].rearrange("(k p) t -> p k t", p=128), us)
                        if ls < 0:
                            if rr > 0:
                                for x in range(2):
                                    tok = dma(SP, ub[i][:, 4 * x:4 * x + 4, 0:1],
                                              agout2[x].ap()[(rr - 1) * 512:rr * 512, 1023:1024].rearrange("(k p) t -> p k t", p=128), us,
                                              allow_slow_non_contiguous=True)
                            else:
                                ub_r[i].w = tok
                                ub_r[i].r = {}
                                tok = DVE.op("memset", ub[i][:, :, 0:1], 0.0, writes=[ub_r[i]])
                                return
                        if ls + 258 > 1024:
                            if rr < 3:
                                for x in range(2):
                                    tok = dma(SP, ub[i][:, 4 * x:4 * x + 4, 257:258],
                                              agout2[x].ap()[(rr + 1) * 512:(rr + 2) * 512, 0:1].rearrange("(k p) t -> p k t", p=128), us,
                                              allow_slow_non_contiguous=True)
                            else:
                                ub_r[i].w = tok
                                ub_r[i].r = {}
                                tok = DVE.op("memset", ub[i][:, :, 257:258], 0.0, writes=[ub_r[i]])
                                return
                        ub_r[i].w = tok
                        ub_r[i].r = {}

                    for hh in range(2):
                        s = wnext()
                        wv = v3(slot_view(s), 0, KC, 512)
                        load_ublk(0)
                        for tb in range(16):
                            if tb + 1 < 16:
                                load_ublk(tb + 1)
                            i = tb % 2
                            for typ in range(4):
                                if typ < 3:
                                    dst_, dst_r_ = qkvzs[:, typ, tb * 256:(tb + 1) * 256], qkvzs_r[typ]
                                else:
                                    zc = 512 + (tb % 4) * 256
                                    dst_, dst_r_ = xn[:, tb // 4, zc:zc + 256], xn_r[tb // 4][1 if (tb % 4) < 2 else 2]
                                proj_chunk(wv, typ, s, ub[i][:, :, :], ub_r[i], typ,
                                           convo[:, hh * 3 + typ, :] if typ < 3 else None, dst_, dst_r_)
                            if hh == 0:
                                gbeta(ub[i][:, :, :], ub_r[i], wabo, 8, dtbo, negAo, lambda ii, tb=tb: gsts[:, 2 * tb + ii, :])
                        DVE.op("memset", oaccs[:, :], 0.0, writes=[oaccs_r])
                        for e in range(2):
                            DVE.op("tensor_copy", CH[e].S[:, :], s0[:, e, hh, :], reads=[s0_r], writes=[CH[e].S_r])
                            ACT.op("activation", CH[e].Sb[:, :], s0[:, e, hh, :], AF.Copy, reads=[s0_r], writes=[CH[e].Sb_r])
                        pending = None
                        NCH = TT_ // 128
                        for gi in range(NCH // 2 + 1):
                            gens = []
                            if gi < NCH // 2:
                                par = gi % 2
                                order = ((0, 2 * gi), (0, 2 * gi + 1), (1, NCH - 1 - 2 * gi), (1, NCH - 2 - 2 * gi))
                                for g_, (e, c) in enumerate(order):
                                    sy = Obj()
                                    sy.g, sy.par, sy.e = g_, par, e
                                    sy.qT = qkvzs[:, 0, c * 128:(c + 1) * 128]
                                    sy.kT = qkvzs[:, 1, c * 128:(c + 1) * 128]
                                    sy.vT = qkvzs[:, 2, c * 128:(c + 1) * 128]
                                    sy.qkv_r = [qkvzs_r[0], qkvzs_r[1], qkvzs_r[2]]
                                    sy.gcol = gsts[:, c, e * 2 + hh:e * 2 + hh + 1]
                                    sy.bcol = gsts[:, c, 4 + e * 2 + hh:4 + e * 2 + hh + 1]
                                    sy.g_r = k.gst_r
                                    gens.append(prep(sy))
                            if pending is not None:
                                gens.extend(pending)
                            run_threads(gens)
                            pending = None
                            if gi < NCH // 2:
                                par = gi % 2
                                order = ((0, 2 * gi), (0, 2 * gi + 1), (1, NCH - 1 - 2 * gi), (1, NCH - 2 - 2 * gi))
                                stf = [(OS[par][g_], oaccs[:, c * 128:(c + 1) * 128], oaccs_r) for g_, (e, c) in enumerate(order) if e == 0]
                                stb = [(OS[par][g_], oaccs[:, c * 128:(c + 1) * 128], oaccs_r) for g_, (e, c) in enumerate(order) if e == 1]
                                pending = [scan(CH[0], stf), scan(CH[1], stb)]
                        def zfn(p0, w):
                            j_ = p0 // 512
                            return xn[:, j_ // 2, 512 + (j_ % 2) * 512:512 + (j_ % 2) * 512 + w]

                        def ofn(p0, w):
                            j_ = p0 // 512
                            return xn[:, 4 + j_ // 2, 512 + (j_ % 2) * 512:512 + (j_ % 2) * 512 + w]
                        post(oaccs, oaccs_r, zfn, lambda p0: xn_r[(p0 // 512) // 2][1 + (p0 // 512) % 2], TT_,
                             ofn, lambda p0: xn_r[4 + (p0 // 512) // 2][1 + (p0 // 512) % 2])
                        for j_ in range(8):
                            tg = dma(SP, agin3[hh].ap()[:, j_ * 512:(j_ + 1) * 512], ofn(j_ * 512, 512), asem,
                                     reads=[xn_r[4 + j_ // 2][1 + j_ % 2]])
                        POOL.wait([tg])
                        nc.gpsimd.collective_compute("AllGather", ALU.bypass, replica_groups=GROUPS,
                                                     ins=[agin3[hh].ap().opt()], outs=[agout3[hh].ap().opt()]).then_inc(ccsem.s, 1)
                        ccsem.n += 1
                    for e_ in (ACT, DVE, PE):
                        e_.wait([tg])
                    cct3 = (ccsem.s, ccsem.n)
                    barrier()

                if k.dsub < 4:
                    return
                with ExitStack() as ph2:
                    def psb2(name, shape, dt):
                        k.uid += 1
                        return ph2.enter_context(nc.sbuf_tensor(f"p_{name}_{k.uid}", list(shape), dt))
                    cand = [psb2(f"cand{i}", [128, 2, 4, 1024], BF16) for i in range(2)]
                    cand_r = regs(2)
                    for rr in range(4):
                        i = rr % 2
                        SP._deps((), [cand_r[i]], [cct3])
                        SP.wait(BAR["toks"])
                        for hh in range(2):
                            tok = dma(SP, cand[i][:, hh, :, :], agout3[hh].ap()[:, rr * 1024:(rr + 1) * 1024].rearrange("(k p) t -> p k t", p=128),
                                      usems[i])
                        cand_r[i].w = tok
                        cand_r[i].r = {}
                        for t in (1, 2):
                            for kk in range(KC):
                                src = cand[i][:, kk % 2, kk // 2, (t - 1) * 512:t * 512]
                                dstv = xn[:, kk, TILES[t][0]:TILES[t][1]]
                                if rr == 0:
                                    DVE.op("tensor_scalar", dstv, src, selt[:, 0:1], None, ALU.mult, reads=[cand_r[i], dn_r], writes=[xn_r[kk][t]])
                                else:
                                    DVE.op("scalar_tensor_tensor", dstv, src, selt[:, rr:rr + 1], dstv, ALU.mult, ALU.add,
                                           reads=[cand_r[i], dn_r], writes=[xn_r[kk][t]])
                    barrier()

                for blk in range(2):
                    s = wnext()
                    wv = v3(slot_view(s), 0, KC, 512)
                    for c4 in range(4):
                        dc = blk * 4 + c4
                        for t, (t0, t1_) in enumerate(TILES):
                            b = next_bank()
                            for kk in range(KC):
                                PE.op("matmul", banks[b][:, :], wv[:, kk, c4 * 128:(c4 + 1) * 128], xn[:, kk, t0:t1_],
                                      start=(kk == 0), stop=(kk == KC - 1), reads=[wring_r[s], xn_r[kk][t]],
                                      writes=[bank_r[b]] if kk == 0 else [], inc=(kk == KC - 1))
                            bank_r[b].w = (PE.sem, PE.cnt)
                            DVE.op("scalar_tensor_tensor", h[:, dc, t0:t1_], banks[b][:, :], coefG[:, l, 1, dc, cj(t):cj(t) + 1],
                                   h[:, dc, t0:t1_], ALU.mult, ALU.add, reads=[bank_r[b], mod_r], writes=[h_r[dc][t]])
                barrier()

        def final_out(dst_dram, normed=True):
            with ExitStack() as ph:
                def psb(name, shape, dt):
                    k.uid += 1
                    return ph.enter_context(nc.sbuf_tensor(f"p_{name}_{k.uid}", list(shape), dt))
                yo = psb("yo", [128, KC, NT], F32)
                yo_r = regs(KC, 3)
                sq = psb("sq", [128, 2, 512], BF16)
                rstd = psb("rstd", [128, 512], F32)
                tmp = psb("tmp", [128, 2, 512], F32)
                pool = (sq, regs(2), rstd, Reg(), tmp, regs(2))
                if normed:
                    modnorm(lambda kk, t: yo[:, kk, TILES[t][0]:TILES[t][1]], lambda kk, t: yo_r[kk][t],
                            lambda kk, t: gains[:, 6, kk:kk + 1], lambda kk, t: None, pool)
                    for kk in range(KC):
                        dma(SP, dst_dram[kk * 128:(kk + 1) * 128, :], yo[:, kk, :], osem, reads=yo_r[kk])
                else:
                    for kk in range(KC):
                        dma(SP, dst_dram[kk * 128:(kk + 1) * 128, :], h[:, kk, :], osem, reads=h_r[kk])
                for e in (SP, ACT, DVE, PE):
                    e.wait([(osem.s, osem.n)])
                barrier()

        do_mods_all()
        ffn(0, 0)
        if stage >= 2:
            attention(0)
        if stage >= 3:
            ffn(0, 1)
        if stage >= 4:
            ffn(1, 0)
        if stage >= 5:
            deltanet(1)
        if stage >= 6:
            ffn(1, 1)
        if dbg and k.dbgsrc == 'mod':
            dma(SP, O["dbg"][0:128, 0:288], modT[:].rearrange("p l c j -> p (l c j)"), osem, reads=[mod_r])
            dma(SP, O["dbg"][0:128, 288:384], coefA[:].rearrange("p l s k j -> p (l s k j)"), osem, reads=[mod_r])
            dma(SP, O["dbg"][0:128, 384:480], coefG[:].rearrange("p l s k j -> p (l s k j)"), osem, reads=[mod_r])
            for e_ in (ACT, DVE, PE):
                e_.wait([(osem.s, osem.n)])
        if dbg and k.dbgsrc == 'h':
            final_out(O["dbg"], normed=False)
        final_out(O["yT"], normed=True)
        assert k.w_used == len(plan), (k.w_used, len(plan))
        SP.wait([(osem.s, osem.n)])
    return nc


def _fm(v):
    return np.ascontiguousarray(v.reshape(KC, 128).T)


def make_inputs(core, inp):
    b, r = core // 4, core % 4
    f32 = np.float32
    xp = inp["x_prompt"][2 * core:2 * core + 2].reshape(512, D)
    xs = inp["x_sample"][b, r * 1024:(r + 1) * 1024]
    m = {}
    m["xT"] = np.ascontiguousarray(np.concatenate([xp, xs], axis=0).T)
    cond = np.stack([inp["c_ctx"], inp["c"][b]], axis=-1)
    m["condT"] = np.ascontiguousarray(cond.reshape(KC, 128, 2).transpose(1, 0, 2))
    m["ada_w"] = np.ascontiguousarray(inp["ada_w"][:, :, r * 2304:(r + 1) * 2304])
    m["adabT"] = np.ascontiguousarray(inp["ada_b"].reshape(2, 72, 128).transpose(0, 2, 1)[:, :, r * 18:(r + 1) * 18])
    gl = []
    for l in range(2):
        for nm in ("norm_ffn1", "norm_mix", "norm_ffn2"):
            gl.append(_fm(inp[nm][l]))
    gl.append(_fm(inp["final_norm"]))
    m["gainsT"] = np.ascontiguousarray(np.stack(gl, axis=1))
    m["ffn_w_in"] = np.ascontiguousarray(np.stack([inp["ffn1_w_in"][0], inp["ffn2_w_in"][0], inp["ffn1_w_in"][1], inp["ffn2_w_in"][1]]))
    m["ffn_w_out"] = np.ascontiguousarray(np.stack([inp["ffn1_w_out"][0], inp["ffn2_w_out"][0], inp["ffn1_w_out"][1], inp["ffn2_w_out"][1]]))
    m["ident"] = np.eye(128, dtype=f32)
    m["attn_w_qkv"] = inp["attn_w_qkv"][0]
    m["attn_w_o"] = inp["attn_w_o"][0]
    m["qkg"] = np.ascontiguousarray(np.stack([inp["attn_q_norm"][0], inp["attn_k_norm"][0]], axis=1))
    m["ckT"] = np.ascontiguousarray(inp["cache_k"][b, 0].transpose(2, 1, 0))
    m["cv"] = np.ascontiguousarray(inp["cache_v"][b, 0].reshape(4, 128, 256).transpose(1, 0, 2))
    C, S, Rm = rope_consts(r)
    m["ropeC"], m["ropeS"], m["rmat"] = C, S, Rm
    hs_ = [2 * r, 2 * r + 1]
    w_in = inp["dn_w_in"][0]
    m["dn_w_in"] = w_in
    m["dn_w_in_own"] = np.ascontiguousarray(np.concatenate(
        [w_in[:, ty * 1024 + hh * 128: ty * 1024 + (hh + 1) * 128] for hh in hs_ for ty in range(4)], axis=1))
    convT = np.ascontiguousarray(inp["dn_conv"][0].reshape(3, 24, 128).transpose(2, 1, 0))
    m["convT"] = convT
    m["convT_own"] = np.ascontiguousarray(np.stack([convT[:, ty * 8 + hh, :] for hh in hs_ for ty in range(3)], axis=1))
    wa, wb = inp["dn_w_a"][0], inp["dn_w_b"][0]
    m["wab_p"] = np.ascontiguousarray(np.concatenate([wa[0], wa[1], wb[0], wb[1]], axis=1))
    m["wab_o"] = np.ascontiguousarray(np.stack([wa[0][:, hs_[0]], wa[0][:, hs_[1]], wa[1][:, hs_[0]], wa[1][:, hs_[1]],
                                                wb[0][:, hs_[0]], wb[0][:, hs_[1]], wb[1][:, hs_[0]], wb[1][:, hs_[1]]], axis=1))
    dtb, alog = inp["dn_dt_bias"][0], inp["dn_a_log"][0]
    m["dtb_p"] = np.ascontiguousarray(dtb.reshape(16))
    m["alog_p"] = np.ascontiguousarray(alog.reshape(16))
    m["dtb_o"] = np.ascontiguousarray(np.array([dtb[0, hs_[0]], dtb[0, hs_[1]], dtb[1, hs_[0]], dtb[1, hs_[1]]], f32))
    m["alog_o"] = np.ascontiguousarray(np.array([alog[0, hs_[0]], alog[0, hs_[1]], alog[1, hs_[0]], alog[1, hs_[1]]], f32))
    m["ong"] = np.ascontiguousarray(inp["dn_out_norm"][0].reshape(128, 1))
    m["dn_w_o"] = inp["dn_w_o"][0]
    m["s0f"] = np.ascontiguousarray(inp["state_fwd"][b, 0, hs_].transpose(1, 0, 2))
    m["s0b"] = np.ascontiguousarray(inp["state_bwd"][b, 0, hs_].transpose(1, 0, 2))
    sel = np.zeros(4, f32)
    sel[r] = 1.0
    m["sel"] = sel
    p_ = np.arange(128)[:, None]
    j_ = np.arange(128)[None, :]
    m["masks"] = np.ascontiguousarray(np.stack([p_ <= j_, p_ >= j_, p_ > j_, p_ < j_], axis=1).astype(f32))
    lm = []
    for li in range(7):
        mm_ = 1 << li
        lm.append((p_ // (2 * mm_) == j_ // (2 * mm_)) & (p_ % (2 * mm_) >= mm_) & (j_ % (2 * mm_) < mm_))
    m["lmask"] = np.ascontiguousarray(np.stack(lm, axis=1).astype(f32))
    return m


def rope_consts(r):
    pos = np.arange(r * 1024, (r + 1) * 1024)
    row = (pos // 64).astype(np.float32)
    col = (pos % 64).astype(np.float32)
    nf = 32
    inv = (np.float32(10000.0) ** (-np.arange(nf, dtype=np.float32) / nf)).astype(np.float32)
    d = np.arange(128)
    axis = d // 64
    f = d % 32
    p = np.where(axis[:, None] == 0, row[None, :], col[None, :]).astype(np.float32)
    ang = (p * inv[f][:, None]).astype(np.float32)
    C = np.cos(ang).astype(np.float32)
    S = np.sin(ang).astype(np.float32)
    Rm = np.zeros((128, 128), np.float32)
    for m_ in range(128):
        if (m_ % 64) < 32:
            Rm[m_ + 32, m_] = -1.0
        else:
            Rm[m_ - 32, m_] = 1.0
    return C, S, Rm


_CACHE = {}


SHARED = ("ffn_w_in", "ffn_w_out", "ident", "attn_w_qkv", "attn_w_o", "dn_w_in", "dn_w_o", "convT", "wab_p",
          "dtb_p", "alog_p", "ong", "masks", "lmask", "gainsT", "qkg", "rmat")


def make_all_inputs(inp):
    in_maps = []
    for c in range(8):
        m = make_inputs(c, inp)
        if in_maps:
            for kk in SHARED:
                m[kk] = in_maps[0][kk]
        in_maps.append(m)
    return in_maps


def assemble(results):
    f32 = np.float32
    y_p = np.zeros((16, 256, D), f32)
    y_s = np.zeros((2, 4096, D), f32)
    nk = np.zeros((16, 1, 256, 2, 128), f32)
    nv = np.zeros((16, 1, 256, 2, 128), f32)
    sf = np.zeros((16, 1, 8, 128, 128), f32)
    sb = np.zeros((16, 1, 8, 128, 128), f32)
    for c in range(8):
        b, r = c // 4, c % 4
        res = results[c]
        y = np.asarray(res["yT"]).T
        y_p[2 * c:2 * c + 2] = y[:512].reshape(2, 256, D)
        y_s[b, r * 1024:(r + 1) * 1024] = y[512:]
        ko = np.asarray(res["kout"])
        vo = np.asarray(res["vout"]).transpose(1, 0, 2).reshape(512, 2, 128)
        for s2 in range(2):
            nk[2 * c + s2, 0] = ko[:, :, s2 * 256:(s2 + 1) * 256].transpose(2, 1, 0)
            nv[2 * c + s2, 0] = vo[s2 * 256:(s2 + 1) * 256]
        sf[2 * c:2 * c + 2, 0] = np.asarray(res["sfo"]).transpose(1, 0, 2).reshape(2, 8, 128, 128)
        sb[2 * c:2 * c + 2, 0] = np.asarray(res["sbo"]).transpose(1, 0, 2).reshape(2, 8, 128, 128)
    return (y_p, y_s, nk, nv, sf, sb)


def kernel(**inputs):
    inp = {k_: np.asarray(v) for k_, v in inputs.items()}
    nc = build_program()
    in_maps = make_all_inputs(inp)
    res = run_bass_kernel_spmd(nc, in_maps, core_ids=list(range(8)))
    return assemble(res.results)
```
